# Optimizing a Trainium2 kernel written in Bass

```python
import jax, jax.numpy as jnp
from jax import lax
import numpy as np

D_MODEL = 1024
BATCH = 8
SEQ = 2048
DEPTH = 4

D_HGRN = D_MODEL
HGRN_EXPAND = 128
HGRN_HEADS = D_HGRN // HGRN_EXPAND
HEAD_K = HGRN_EXPAND
HEAD_V = D_HGRN // HGRN_HEADS
CHUNK = 64
D_CONV = D_MODEL
CONV_WIDTH = 31
FFN_HIDDEN = -(-8 * D_MODEL // (3 * 256)) * 256
ALPHA = (2 * DEPTH) ** 0.25
BETA = (8 * DEPTH) ** -0.25
LN_EPS = 1e-5
RMS_EPS = 1e-6
F_MIN = 1e-30
IN_SPLITS = (D_HGRN, D_HGRN, D_HGRN, D_HGRN, D_CONV, D_CONV, D_MODEL, D_MODEL)
D_IN = sum(IN_SPLITS)

kernel_name = "hybrid_hgrn2_conformer_deepnorm"


def layer_norm(x, g, b):
    xf = x.astype(jnp.float32)
    mu = jnp.mean(xf, axis=-1, keepdims=True)
    var = jnp.mean(jnp.square(xf - mu), axis=-1, keepdims=True)
    y = (xf - mu) * lax.rsqrt(var + LN_EPS) * g.astype(jnp.float32) + b.astype(jnp.float32)
    return y.astype(x.dtype)


def hgrn2_mixer(q_raw, f_raw, i_raw, g_raw, lb, g_norm_w):
    B, T, _ = q_raw.shape
    n_chunks = T // CHUNK
    q = jax.nn.silu(q_raw.astype(jnp.float32))
    z = f_raw.astype(jnp.float32)
    lb = lb.astype(jnp.float32)
    f = lb + (1.0 - lb) * jax.nn.sigmoid(z)
    log_f = jnp.log(jnp.maximum(f, F_MIN))
    k = (1.0 - lb) * jax.nn.sigmoid(-z)
    v = i_raw.astype(jnp.float32)

    def to_chunks(t, dh):
        return t.reshape(B, n_chunks, CHUNK, HGRN_HEADS, dh).transpose(1, 0, 3, 2, 4)

    qc, kc, gc = to_chunks(q, HEAD_K), to_chunks(k, HEAD_K), to_chunks(log_f, HEAD_K)
    vc = to_chunks(v, HEAD_V)
    causal = jnp.tril(jnp.ones((CHUNK, CHUNK), dtype=bool))[:, :, None]

    def chunk_step(S, inp):
        q_, k_, v_, g_ = inp
        G = jnp.cumsum(g_, axis=2)
        o_inter = jnp.einsum('bhtk,bhkv->bhtv', q_ * jnp.exp(G), S)
        diff = G[:, :, :, None, :] - G[:, :, None, :, :]
        decay = jnp.where(causal, jnp.exp(jnp.minimum(diff, 0.0)), 0.0)
        scores = jnp.einsum('bhtk,bhsk,bhtsk->bhts', q_, k_, decay)
        o_intra = jnp.einsum('bhts,bhsv->bhtv', scores, v_)
        G_last = G[:, :, -1:, :]
        S_new = jnp.exp(G_last[:, :, 0, :])[..., None] * S + jnp.einsum(
            'bhsk,bhsv->bhkv', k_ * jnp.exp(G_last - G), v_)
        return S_new, o_inter + o_intra

    S0 = jnp.zeros((B, HGRN_HEADS, HEAD_K, HEAD_V), jnp.float32)
    _, o = lax.scan(chunk_step, S0, (qc, kc, vc, gc))
    o = o.transpose(1, 0, 3, 2, 4).reshape(B, T, HGRN_HEADS, HEAD_V)
    o = o * lax.rsqrt(jnp.mean(jnp.square(o), axis=-1, keepdims=True) + RMS_EPS)
    o = o * g_norm_w.astype(jnp.float32)
    o = o.reshape(B, T, D_HGRN) * jax.nn.silu(g_raw.astype(jnp.float32))
    return o.astype(q_raw.dtype)


def conformer_conv_mixer(a, b, w_dw, b_dw, ln_g, ln_b):
    u = a * jax.nn.sigmoid(b)
    y = lax.conv_general_dilated(
        u, w_dw[:, None, :], window_strides=(1,), padding=[(CONV_WIDTH - 1, 0)],
        dimension_numbers=('NWC', 'WIO', 'NWC'), feature_group_count=D_CONV)
    y = layer_norm(y + b_dw, ln_g, ln_b)
    return jax.nn.silu(y)


def setup_inputs(seed: int = 0) -> dict:
    key = jax.random.key(seed)
    ks = jax.random.split(key, 24)
    f32 = jnp.float32

    def nrm(k, shape, scale):
        return jax.random.normal(k, shape, f32) * scale

    return {
        "x": nrm(ks[0], (BATCH, SEQ, D_MODEL), 1.0),
        "ln0_g": 1.0 + nrm(ks[1], (D_MODEL,), 0.02),
        "ln0_b": nrm(ks[2], (D_MODEL,), 0.02),
        "w_in": nrm(ks[3], (DEPTH, D_MODEL, D_IN), D_MODEL ** -0.5),
        "b_in": nrm(ks[4], (DEPTH, D_IN), 0.02),
        "lb_logits": nrm(ks[5], (DEPTH, D_HGRN), 0.1),
        "g_norm_w": 1.0 + nrm(ks[6], (DEPTH, HEAD_V), 0.02),
        "w_a": nrm(ks[7], (DEPTH, D_HGRN, D_MODEL), BETA * D_HGRN ** -0.5),
        "w_dw": nrm(ks[8], (DEPTH, CONV_WIDTH, D_CONV), CONV_WIDTH ** -0.5),
        "b_dw": nrm(ks[9], (DEPTH, D_CONV), 0.02),
        "conv_ln_g": 1.0 + nrm(ks[10], (DEPTH, D_CONV), 0.02),
        "conv_ln_b": nrm(ks[11], (DEPTH, D_CONV), 0.02),
        "w_b": nrm(ks[12], (DEPTH, D_CONV, D_MODEL), BETA * D_CONV ** -0.5),
        "b_b": nrm(ks[13], (DEPTH, D_MODEL), 0.02),
        "w_o": nrm(ks[14], (DEPTH, D_MODEL, D_MODEL), BETA * D_MODEL ** -0.5),
        "ln1_g": 1.0 + nrm(ks[15], (DEPTH, D_MODEL), 0.02),
        "ln1_b": nrm(ks[16], (DEPTH, D_MODEL), 0.02),
        "w_up": nrm(ks[17], (DEPTH, D_MODEL, 2 * FFN_HIDDEN), D_MODEL ** -0.5),
        "w_down": nrm(ks[18], (DEPTH, FFN_HIDDEN, D_MODEL), BETA * FFN_HIDDEN ** -0.5),
        "ln2_g": 1.0 + nrm(ks[19], (DEPTH, D_MODEL), 0.02),
        "ln2_b": nrm(ks[20], (DEPTH, D_MODEL), 0.02),
    }


def reference(x, ln0_g, ln0_b, w_in, b_in, lb_logits, g_norm_w, w_a, w_dw, b_dw,
              conv_ln_g, conv_ln_b, w_b, b_b, w_o, ln1_g, ln1_b, w_up, w_down,
              ln2_g, ln2_b):
    split_idx = list(np.cumsum(IN_SPLITS)[:-1])
    p = jax.nn.softmax(lb_logits.astype(jnp.float32), axis=0)
    lower_bounds = jnp.cumsum(p, axis=0) - p[0:1]

    x = layer_norm(x, ln0_g, ln0_b)
    for l in range(DEPTH):
        h = jnp.einsum('btd,de->bte', x, w_in[l]) + b_in[l]
        q_r, f_r, i_r, g_r, glu_a, glu_b, gate_h, gate_c = jnp.split(h, split_idx, axis=-1)
        y_h = hgrn2_mixer(q_r, f_r, i_r, g_r, lower_bounds[l], g_norm_w[l])
        y_h = jnp.einsum('bte,ed->btd', y_h, w_a[l])
        y_c = conformer_conv_mixer(glu_a, glu_b, w_dw[l], b_dw[l], conv_ln_g[l], conv_ln_b[l])
        y_c = jnp.einsum('bte,ed->btd', y_c, w_b[l]) + b_b[l]
        merged = jax.nn.sigmoid(gate_h) * y_h + jax.nn.sigmoid(gate_c) * y_c
        mix_out = jnp.einsum('btd,de->bte', merged, w_o[l])
        x = layer_norm(ALPHA * x + mix_out, ln1_g[l], ln1_b[l])
        up = jnp.einsum('btd,df->btf', x, w_up[l])
        u_gate, u_val = jnp.split(up, 2, axis=-1)
        ffn_out = jnp.einsum('btf,fd->btd', jax.nn.silu(u_gate) * u_val, w_down[l])
        x = layer_norm(ALPHA * x + ffn_out, ln2_g[l], ln2_b[l])
    return x
```

```python
import numpy as np
from contextlib import ExitStack
import concourse.bass as bass
import concourse.mybir as mybir
from concourse.bass_utils import run_bass_kernel_spmd

F32 = mybir.dt.float32
BF16 = mybir.dt.bfloat16
AF = mybir.ActivationFunctionType
ALU = mybir.AluOpType
AX = mybir.AxisListType

D = 1024
T = 2048
DEPTH = 4
NCH = 8
TB = 512
NTB = 4
FH = 2816
NFC = 22
CW = 31
ALPHA = (2 * DEPTH) ** 0.25
LN_EPS = 1e-5
RMS_EPS = 1e-6
F_MIN = 1e-30

C_BIN = 0
C_GNW = 64
C_WDW = 65
C_BDW = C_WDW + 8 * CW
C_CLG = C_BDW + 8
C_CLB = C_CLG + 8
C_BB = C_CLB + 8
C_L1G = C_BB + 8
C_L1B = C_L1G + 8
C_L2G = C_L1B + 8
C_L2B = C_L2G + 8
NCOL = C_L2B + 8
K_ID = 0
K_MASK = 128
K_RESET = 192
NCONST = 192
NWB = 2


class Tile:
    __slots__ = ("w", "r")

    def __init__(self):
        self.w = None
        self.r = {}


class Eng:
    def __init__(self, S, name):
        self.S = S
        self.name = name
        self.ops = []
        self.sem = None
        self.cnt = 0
        self.waited = {}

    def rotate(self):
        self.sem = self.S.new_sem(self.name)
        self.cnt = 0


class DmaStream:
    def __init__(self, S, name):
        self.sem = S.new_sem(name)
        self.cnt = 0


class Sched:
    def __init__(self, nc, stack):
        self.nc = nc
        self.stack = stack
        self.nsem = 0
        self.pe = Eng(self, "pe")
        self.act = Eng(self, "act")
        self.dve = Eng(self, "dve")
        self.pool = Eng(self, "pool")
        self.sp = Eng(self, "sp")
        self.engs = [self.pe, self.act, self.dve, self.pool, self.sp]
        for e in self.engs:
            e.rotate()

    def new_sem(self, name):
        self.nsem += 1
        return self.stack.enter_context(self.nc.semaphore(f"s{self.nsem}_{name}"))

    def rotate_all(self):
        for e in self.engs:
            e.rotate()

    def op(self, eng, fn, reads=(), writes=(), dma=None):
        need = {}
        for t in reads:
            if t.w is not None:
                s, v = t.w
                if need.get(s, 0) < v:
                    need[s] = v
        for t in writes:
            if t.w is not None:
                s, v = t.w
                if need.get(s, 0) < v:
                    need[s] = v
            for s, v in t.r.items():
                if need.get(s, 0) < v:
                    need[s] = v
        waits = []
        for s, v in need.items():
            if eng is self.pe and s is eng.sem:
                continue
            if eng.waited.get(s, 0) < v:
                eng.waited[s] = v
                waits.append((s, v))
        if dma is not None:
            dma.cnt += 16
            point = (dma.sem, dma.cnt)
            eng.ops.append((waits, fn, dma.sem, 16))
        else:
            eng.cnt += 1
            point = (eng.sem, eng.cnt)
            eng.ops.append((waits, fn, eng.sem, 1))
        s, v = point
        for t in reads:
            if t.r.get(s, 0) < v:
                t.r[s] = v
        for t in writes:
            t.w = point
            t.r = {}
        return point

    def barrier(self):
        comp = [self.pe, self.act, self.dve, self.pool]
        for e in comp:
            waits = []
            for f in comp:
                if f is e or f.cnt == 0:
                    continue
                if e.waited.get(f.sem, 0) < f.cnt:
                    e.waited[f.sem] = f.cnt
                    waits.append((f.sem, f.cnt))
            if waits:
                e.ops.append((waits, None, None, 0))

    def emit(self, final_waits=()):
        def replay(eng, hw):
            for waits, fn, sem, inc in eng.ops:
                for s, v in waits:
                    hw.wait_ge(s, v)
                if fn is None:
                    continue
                ins = fn(hw)
                if sem is not None:
                    ins.then_inc(sem, inc)

        with self.nc.Block() as block:
            @block.tensor
            def _(e):
                replay(self.pe, e)

            @block.scalar
            def _(e):
                replay(self.act, e)

            @block.vector
            def _(e):
                replay(self.dve, e)

            @block.gpsimd
            def _(e):
                replay(self.pool, e)

            @block.sync
            def _(e):
                replay(self.sp, e)
                for s, v in final_waits:
                    e.wait_ge(s, v)


class Buf3:
    def __init__(self, ap3, nj, ntok=T):
        self.t = ap3
        self.nj = nj
        self.T = [[Tile() for _ in range(ntok // TB)] for _ in range(nj)]

    def ap(self, j, tb):
        return self.t[:, j, tb * TB:(tb + 1) * TB]

    def col(self, tb):
        return [self.T[j][tb] for j in range(self.nj)]


class _Stop(Exception):
    pass


def build(depth=DEPTH, dbg=False, stop=None):
    def chk(n):
        if stop == n:
            raise _Stop()
    nc = bass.Bass("TRN2", target_bir_lowering=False)

    def din(name, shape):
        return nc.dram_tensor(name, list(shape), F32, kind="ExternalInput").ap()

    x_d = din("x", [T, D])
    consts_d = din("consts", [128, NCONST])
    ln0_d = din("ln0", [128, 16])
    lbl_d = din("lbl", [128, DEPTH * 8])
    cols_d = din("cols", [DEPTH, 128, NCOL])
    bibc_d = din("bibc", [DEPTH, 128, D])
    wih_d = din("wih", [DEPTH, 8, 128, 4096])
    wic_d = din("wic", [DEPTH, 4, 128, 4096])
    wma_d = din("wma", [DEPTH, 4, 128, 4096])
    wmb_d = din("wmb", [DEPTH, 4, 128, 4096])
    wo_d = din("wo", [DEPTH, 2, 128, 4096])
    wup_d = din("wup", [DEPTH, 11, 128, 4096])
    wdn_d = din("wdn", [DEPTH, 8, 128, NFC * 128])
    out_d = nc.dram_tensor("out", [T, D], F32, kind="ExternalOutput").ap()
    dbg_d = {}

    with ExitStack() as st:
        S = Sched(nc, st)

        def sb(name, shape, dt=F32):
            return st.enter_context(nc.sbuf_tensor(name, list(shape), dt))

        def pst(name, shape, dt=F32):
            return st.enter_context(nc.psum_tensor(name, list(shape), dt))

        DD = sb("DDEC", [128, 64])
        SC = sb("SCAL", [128, 64])
        X_t = sb("X", [128, NCH, T])
        XB_t = sb("XB", [128, NCH, T], BF16)
        BAB = sb("BAB", [128, 32768], BF16)
        X = Buf3(X_t, NCH)
        XB = Buf3(XB_t, NCH)
        BUFA = Buf3(BAB[:, 0:16384].rearrange("p (j t) -> p j t", t=T), NCH)
        BUFB = Buf3(BAB[:, 16384:32768].rearrange("p (j t) -> p j t", t=T), NCH)
        HID_t = BAB[:, 0:NFC * 1024].rearrange("p (j t) -> p j t", t=1024)

        def f32row(i):
            return BAB[:, 16384 + 4096 * i:16384 + 4096 * (i + 1)].bitcast(F32)
        TQ, TSG, TK, T4 = f32row(0), f32row(1), f32row(2), f32row(3)
        SCR = sb("SCR", [128, 10240], BF16)

        def scr(a, n, dt=BF16):
            v = SCR[:, a:a + n]
            return v.bitcast(F32) if dt == F32 else v
        QT = scr(0, 2048)
        KT = scr(2048, 2048)
        KTM = scr(4096, 2048).rearrange("p (a b) -> p a b", b=128)
        VTM = scr(6144, 2048).rearrange("p (a b) -> p a b", b=128)
        PM = scr(8192, 1024).rearrange("p (a b) -> p a b", b=64)
        OT = [TK[:, 0:512]] * 2
        RS = [TK[:, 512:1024]] * 2
        OSQ = [TK[:, 1024:1280].bitcast(BF16)] * 2
        UP = [scr(0, 2080)] * 2
        DG = [scr(2080, CW * 128).rearrange("p (a b) -> p a b", b=128)] * 2
        TMP = [scr(6048, 1024, F32), scr(7072, 1024, F32)]
        SQ = scr(6048, 2048).rearrange("p (a b) -> p a b", b=TB)
        T1 = [scr(8096, 1024, F32), scr(9120, 1024, F32)]
        T1P = scr(8096, 2048, F32).rearrange("p (a b) -> p a b", b=TB)
        TEP = TSG
        Zf = [sb(f"Zf{i}", [128, 128]) for i in range(3)]
        Zb = [sb(f"Zb{i}", [128, 128], BF16) for i in range(3)]
        UE4 = [TK[:, 1280:1792], sb("UE4B", [128, 512])]
        WB = [sb(f"WB{i}", [128, 4096], BF16) for i in range(NWB)]
        CONSTS = sb("CONSTS", [128, NCONST])
        IDB = sb("IDB", [128, 128], BF16)
        ONESB = sb("ONESB", [128, 128], BF16)
        ONESH = sb("ONESH", [128, 128], BF16)
        EPSL = sb("EPSL", [128, 1])
        EPSR = sb("EPSR", [128, 1])
        ONEC = sb("ONEC", [128, 1])
        COLS = [sb("COLS0", [128, NCOL])] * 2
        BIBC = [sb(f"BIBC{i}", [128, 128]) for i in range(2)]
        LN0 = sb("LN0", [128, 16])
        LBL = sb("LBL", [128, DEPTH, 8])
        LBE = sb("LBE", [128, DEPTH, 8])
        LB = sb("LB", [128, DEPTH, 8])
        OML = sb("OML", [128, DEPTH, 8])
        NOML = sb("NOML", [128, DEPTH, 8])
        THR = sb("THR", [128, DEPTH, 8])
        LBT = sb("LBT", [128, 8])

        def bab32(a, n):
            return BAB[:, a:a + 2 * n].bitcast(F32)
        XIN = [bab32(0, D), bab32(2048, D)]
        XN = [bab32(4096, D), bab32(6144, D)]
        JUNK = BAB[:, 8192:8192 + D]
        SS = [sb(f"SS{i}", [128, 8]) for i in range(2)]

        IDF = CONSTS[:, K_ID:K_ID + 128]
        MASK2 = CONSTS[:, K_MASK:K_MASK + 64]

        PS = [pst(f"PS{i}", [128, TB]) for i in range(3)]
        PSTR = pst("PSTR", [128, 1024], BF16)
        PSTRF = PSTR[:, 0:1024].bitcast(F32)
        PS4 = pst("PS4", [128, TB])
        PS5 = pst("PS5", [128, TB])
        PS6 = pst("PS6", [128, TB])
        PS7 = pst("PS7", [128, TB])
        TPS = [Tile() for _ in range(3)]
        TPSTR, TPS4, TPS5, TPS6, TPS7 = Tile(), Tile(), Tile(), Tile(), Tile()
        TU = [Tile() for _ in range(4)]
        ps_rr = [0]

        def psnext():
            i = ps_rr[0] % 3
            ps_rr[0] += 1
            return PS[i], TPS[i]

        tl = {}

        def TL(name):
            if name not in tl:
                tl[name] = Tile()
            return tl[name]

        def act(out, in_, func, reads, writes, bias=None, scale=None, accum_out=None):
            kw = {}
            if bias is not None:
                kw["bias"] = bias
            if scale is not None:
                kw["scale"] = scale
            if accum_out is not None:
                kw["accum_out"] = accum_out
            return S.op(S.act, lambda e: e.activation(out=out, in_=in_, func=func, **kw), reads, writes)

        def tt(out, in0, in1, op, reads, writes, eng=None):
            eng = eng or S.dve
            return S.op(eng, lambda e: e.tensor_tensor(out=out, in0=in0, in1=in1, op=op), reads, writes)

        def ts(out, in0, s1, s2, op0, op1, reads, writes, eng=None):
            eng = eng or S.dve
            if op1 is None:
                return S.op(eng, lambda e: e.tensor_scalar(out=out, in0=in0, scalar1=s1, scalar2=None, op0=op0), reads, writes)
            return S.op(eng, lambda e: e.tensor_scalar(out=out, in0=in0, scalar1=s1, scalar2=s2, op0=op0, op1=op1), reads, writes)

        def stt(out, in0, scalar, in1, op0, op1, reads, writes):
            return S.op(S.dve, lambda e: e.scalar_tensor_tensor(out=out, in0=in0, scalar=scalar, in1=in1, op0=op0, op1=op1), reads, writes)

        def cp(out, in_, reads, writes, eng=None):
            eng = eng or S.dve
            return S.op(eng, lambda e: e.tensor_copy(out=out, in_=in_), reads, writes)

        def mm(out, pairs, reads, writes, first=True, last=True):
            def fn(e):
                n = len(pairs)
                ins = None
                for i, (l, r) in enumerate(pairs):
                    ins = e.matmul(out, lhsT=l, rhs=r, start=(first and i == 0), stop=(last and i == n - 1))
                return ins
            return S.op(S.pe, fn, reads, writes)

        def tr(out, in_, ident, reads, writes):
            return S.op(S.pe, lambda e: e.transpose(out=out, in_=in_, identity=ident), reads, writes)

        ld_misc = DmaStream(S, "ldm")
        ld_x = DmaStream(S, "ldx")
        ld_w = DmaStream(S, "ldw")
        st_o = DmaStream(S, "sto")

        wlist = []
        for l in range(depth):
            for s_ in range(8):
                wlist.append((wih_d[l, s_], 4096))
            for s_ in range(4):
                wlist.append((wma_d[l, s_], 4096))
            for s_ in range(4):
                wlist.append((wic_d[l, s_], 4096))
            for s_ in range(4):
                wlist.append((wmb_d[l, s_], 4096))
            for s_ in range(2):
                wlist.append((wo_d[l, s_], 4096))
            for half in range(2):
                for s_ in range(11):
                    wlist.append((wup_d[l, s_], 4096))
                for s_ in range(8):
                    wlist.append((wdn_d[l, s_], NFC * 128))
        TWB = [Tile() for _ in range(NWB)]
        wstate = {"issued": 0, "used": 0}

        def w_issue():
            i = wstate["issued"]
            if i >= len(wlist):
                return
            src, n = wlist[i]
            b = i % NWB
            S.op(S.pool, lambda e: e.dma_start(out=WB[b][:, 0:n], in_=src), reads=[], writes=[TWB[b]], dma=ld_w)
            wstate["issued"] += 1

        def wnext():
            i = wstate["used"]
            while wstate["issued"] < min(i + NWB, len(wlist)):
                w_issue()
            wstate["used"] += 1
            b = i % NWB
            return WB[b], TWB[b]

        S.op(S.sp, lambda e: e.dma_start(out=CONSTS[:], in_=consts_d), writes=[TL("consts")], dma=ld_misc)
        S.op(S.sp, lambda e: e.dma_start(out=LN0[:], in_=ln0_d), writes=[TL("ln0")], dma=ld_misc)
        S.op(S.sp, lambda e: e.dma_start(out=LBL[:].rearrange("p a b -> p (a b)"), in_=lbl_d), writes=[TL("lbl")], dma=ld_misc)
        cp(IDB[:], IDF, [TL("consts")], [TL("idb")])
        S.op(S.dve, lambda e: e.memset(ONESB[:], 1.0 / D), writes=[TL("onesb")])
        S.op(S.dve, lambda e: e.memset(ONESH[:], 1.0 / 128), writes=[TL("onesh")])
        S.op(S.dve, lambda e: e.memset(EPSL[:], LN_EPS), writes=[TL("epsl")])
        S.op(S.dve, lambda e: e.memset(EPSR[:], RMS_EPS), writes=[TL("epsr")])
        S.op(S.dve, lambda e: e.memset(ONEC[:], 1.0), writes=[TL("onec")])

        Tlb = TL("lb")
        cp(LBT[:], LBL[:, 0, :], [TL("lbl")], [TL("lbt")])
        for l in range(1, DEPTH):
            tt(LBT[:], LBT[:], LBL[:, l, :], ALU.max, [TL("lbl"), TL("lbt")], [TL("lbt")])
        for l in range(DEPTH):
            tt(LBE[:, l, :], LBL[:, l, :], LBT[:], ALU.subtract, [TL("lbl"), TL("lbt")], [TL("lbe")])
        act(LBE[:].rearrange("p a b -> p (a b)"), LBE[:].rearrange("p a b -> p (a b)"), AF.Exp, [TL("lbe")], [TL("lbe")])
        cp(LBT[:], LBE[:, 0, :], [TL("lbe")], [TL("lbt")])
        for l in range(1, DEPTH):
            tt(LBT[:], LBT[:], LBE[:, l, :], ALU.add, [TL("lbe"), TL("lbt")], [TL("lbt")])
        S.op(S.dve, lambda e: e.reciprocal(out=LBT[:], in_=LBT[:]), [TL("lbt")], [TL("lbt")])
        for l in range(DEPTH):
            tt(LBE[:, l, :], LBE[:, l, :], LBT[:], ALU.mult, [TL("lbe"), TL("lbt")], [TL("lbe")])
        S.op(S.dve, lambda e: e.memset(LB[:, 0, :], 0.0), writes=[Tlb])
        for l in range(1, DEPTH):
            tt(LB[:, l, :], LB[:, l - 1, :], LBE[:, l, :], ALU.add, [TL("lbe"), Tlb], [Tlb])
        LBf = LB[:].rearrange("p a b -> p (a b)")
        ts(OML[:].rearrange("p a b -> p (a b)"), LBf, -1.0, 1.0, ALU.mult, ALU.add, [Tlb], [Tlb])
        ts(NOML[:].rearrange("p a b -> p (a b)"), LBf, 1.0, -1.0, ALU.mult, ALU.add, [Tlb], [Tlb])
        ts(THR[:].rearrange("p a b -> p (a b)"), LBf, -1.0, F_MIN, ALU.mult, ALU.add, [Tlb], [Tlb])

        TCOLS = [Tile()] * 2

        def load_params(l):
            b = l % 2
            S.op(S.sp, lambda e: e.dma_start(out=COLS[b][:], in_=cols_d[l]), writes=[TCOLS[b]], dma=ld_misc)

        load_params(0)

        TXIN = [Tile(), Tile()]
        TXN = [Tile(), Tile()]
        TSS = [Tile(), Tile()]
        for tt_i in range(16):
            b = tt_i % 2
            t0 = tt_i * 128
            tb = tt_i // 4
            S.op(S.sp, lambda e, b=b, t0=t0: e.dma_start(out=XIN[b][:], in_=x_d[t0:t0 + 128, :]), writes=[TXIN[b]], dma=ld_x)
            S.op(S.dve, lambda e, b=b: e.reduce_sum(out=SS[b][:, 0:1], in_=XIN[b][:], axis=AX.X), [TXIN[b]], [TSS[b]])
            act(JUNK[:], XIN[b][:], AF.Square, [TXIN[b]], [TL("junk"), TSS[b]], accum_out=SS[b][:, 1:2])
            ts(SS[b][:, 2:4], SS[b][:, 0:2], 1.0 / D, None, ALU.mult, None, [TSS[b]], [TSS[b]])
            tt(SS[b][:, 4:5], SS[b][:, 2:3], SS[b][:, 2:3], ALU.mult, [TSS[b]], [TSS[b]])
            tt(SS[b][:, 5:6], SS[b][:, 3:4], SS[b][:, 4:5], ALU.subtract, [TSS[b]], [TSS[b]])
            act(SS[b][:, 6:7], SS[b][:, 5:6], AF.Ln, [TSS[b], TL("epsl")], [TSS[b]], bias=EPSL[:])
            act(SS[b][:, 7:8], SS[b][:, 6:7], AF.Exp, [TSS[b]], [TSS[b]], scale=-0.5)
            ts(XN[b][:], XIN[b][:], SS[b][:, 2:3], SS[b][:, 7:8], ALU.subtract, ALU.mult, [TXIN[b], TSS[b]], [TXN[b]])
            for half in range(2):
                ps, Tps = psnext()
                for jj in range(4):
                    j = half * 4 + jj
                    tr(ps[:, jj * 128:(jj + 1) * 128], XN[b][:, j * 128:(j + 1) * 128], IDF, [TXN[b], TL("consts")], [Tps])
                for jj in range(4):
                    j = half * 4 + jj
                    act(X_t[:, j, t0:t0 + 128], ps[:, jj * 128:(jj + 1) * 128], AF.Identity, [Tps, TL("ln0")], [X.T[j][tb]],
                        scale=LN0[:, j:j + 1], bias=LN0[:, 8 + j:9 + j])
            cp(XB_t[:, :, t0:t0 + 128], X_t[:, :, t0:t0 + 128], X.col(tb), XB.col(tb))

        def ln_fm(src, src_f32, cols, gcol, bcol, func, dst_main, dst_bf=None, eps=None):
            banks = [(PS6, TPS6, PS7, TPS7), (PS4, TPS4, PS5, TPS5)]

            def stats(tb):
                PMn, TPMn, PVr, TPVr = banks[tb % 2]
                sl = slice(tb * TB, (tb + 1) * TB)
                sbuf = XB if src_f32 else src
                mm(PMn[:], [(ONESB[:], sbuf.ap(j, tb)) for j in range(NCH)], sbuf.col(tb) + [TL("onesb")], [TPMn])
                for hf in range(2):
                    act(SQ[:], sbuf.t[:, 4 * hf:4 * hf + 4, sl], AF.Square, sbuf.col(tb), [TL("tmp0"), TL("tmp1")])
                    mm(PVr[:], [(ONESB[:], SQ[:, j, :]) for j in range(4)], [TL("tmp0"), TL("tmp1"), TL("onesb")], [TPVr],
                       first=(hf == 0), last=(hf == 1))
                act(TMP[0][:], PMn[:], AF.Square, [TPMn], [TL("tmp0")])
                tt(TMP[1][:], PVr[:], TMP[0][:], ALU.subtract, [TPVr, TL("tmp0")], [TL("tmp1")])
                act(TMP[1][:], TMP[1][:], AF.Ln, [TL("tmp1"), TL("epsl")], [TL("tmp1")], bias=EPSL[:])
                act(PVr[:], TMP[1][:], AF.Exp, [TL("tmp1")], [TPVr], scale=-0.5)

            def apply(tb):
                PMn, TPMn, PVr, TPVr = banks[tb % 2]
                sl = slice(tb * TB, (tb + 1) * TB)
                if src_f32:
                    tt(src.t[:, :, sl], src.t[:, :, sl], PMn[:].unsqueeze(1).to_broadcast([128, NCH, TB]), ALU.subtract,
                       src.col(tb) + [TPMn], src.col(tb))
                    tt(src.t[:, :, sl], src.t[:, :, sl], PVr[:].unsqueeze(1).to_broadcast([128, NCH, TB]), ALU.mult,
                       src.col(tb) + [TPVr], src.col(tb))
                    for j in range(NCH):
                        act(dst_main.ap(j, tb), src.ap(j, tb), func, [src.T[j][tb], cols[1]], [dst_main.T[j][tb]],
                            scale=cols[0][:, gcol + j:gcol + j + 1], bias=cols[0][:, bcol + j:bcol + j + 1])
                        if dst_bf is not None:
                            act(dst_bf.ap(j, tb), dst_main.ap(j, tb), AF.Copy, [dst_main.T[j][tb]], [dst_bf.T[j][tb]])
                else:
                    for pr in range(NCH // 2):
                        tt(T1P, src.t[:, 2 * pr:2 * pr + 2, sl], PMn[:].unsqueeze(1).to_broadcast([128, 2, TB]), ALU.subtract,
                           [src.T[2 * pr][tb], src.T[2 * pr + 1][tb], TPMn], [TL("t10"), TL("t11")])
                        tt(T1P, T1P, PVr[:].unsqueeze(1).to_broadcast([128, 2, TB]), ALU.mult,
                           [TL("t10"), TL("t11"), TPVr], [TL("t10"), TL("t11")])
                        for jj in range(2):
                            j = 2 * pr + jj
                            act(dst_main.ap(j, tb), T1P[:, jj, :], func, [TL(f"t1{jj}"), cols[1]], [dst_main.T[j][tb]],
                                scale=cols[0][:, gcol + j:gcol + j + 1], bias=cols[0][:, bcol + j:bcol + j + 1])

            stats(0)
            for tb in range(NTB):
                if tb + 1 < NTB:
                    stats(tb + 1)
                apply(tb)

        TQt = [Tile() for _ in range(NTB)]
        TSGt = [Tile() for _ in range(NTB)]
        TVt = [Tile() for _ in range(NTB)]
        TPSO = [TPS6, TPS7]
        PSO = [PS6, PS7]

        for l in range(depth):
          try:
              if l > 0:
                  S.rotate_all()
              CO = COLS[l % 2]
              TCO = TCOLS[l % 2]
              colsp = (CO, TCO)

              def bcol(sec, c):
                  return CO[:, C_BIN + sec * 8 + c:C_BIN + sec * 8 + c + 1]

              S.barrier()
              if l > 0:
                  load_params(l)
              chk(0)
              OG = BUFA
              hw_ = {}

              def head_begin(h):
                  W, TW = wnext()
                  BI = BIBC[h % 2]
                  TBI = TL(f"bibc{h % 2}")
                  S.op(S.sp, lambda e, BI=BI, l=l, h=h: e.dma_start(out=BI[:], in_=bibc_d[l][:, h * 128:(h + 1) * 128]),
                       writes=[TBI], dma=ld_misc)
                  hw_[h] = (W[:].rearrange("p (k e) -> p k e", e=512), TW, BI, TBI)

              def p_qfg_groups(h):
                  W3, TW, BI, TBI = hw_[h]
                  TGG = OG.t[:, h, :]
                  groups = []
                  for tb in range(NTB):
                      sl = slice(tb * TB, (tb + 1) * TB)
                      for (c0, dst, dT, fn_, sec) in ((0, TQ, TQt, AF.Silu, 0), (128, TSG, TSGt, AF.Sigmoid, 1),
                                                      (384, TGG, OG.T[h], AF.Silu, 3)):
                          def g(tb=tb, sl=sl, c0=c0, dst=dst, dT=dT, fn_=fn_, sec=sec):
                              ps, Tps = psnext()
                              mm(ps[:], [(W3[:, k, c0:c0 + 128], XB.ap(k, tb)) for k in range(8)], [TW] + XB.col(tb), [Tps])
                              act(dst[:, sl], ps[:], fn_, [Tps, TCO], [dT[tb]], bias=bcol(sec, h))
                          groups.append(g)
                  return groups

              def p_v(h):
                  W3, TW, BI, TBI = hw_[h]
                  for tb in range(NTB):
                      ps, Tps = psnext()
                      for t4 in range(4):
                          t0 = tb * TB + t4 * 128
                          mm(ps[:, t4 * 128:(t4 + 1) * 128], [(XB_t[:, k, t0:t0 + 128], W3[:, k, 256:384]) for k in range(8)],
                             [TW] + XB.col(tb), [Tps])
                      tt(VTM[:, tb * 4:(tb + 1) * 4, :], ps[:].rearrange("p (a b) -> p a b", b=128),
                         BI[:].unsqueeze(1).to_broadcast([128, 4, 128]), ALU.add,
                         [Tps, TBI], [TVt[tb]])

              def gating(h):
                  lbc = LB[:, l, h:h + 1]
                  omlc = OML[:, l, h:h + 1]
                  nomlc = NOML[:, l, h:h + 1]
                  thrc = THR[:, l, h:h + 1]
                  ts(T4[:], TSG[:], omlc, thrc, ALU.mult, ALU.max, TSGt + [Tlb], [TL("t4")])
                  act(T4[:], T4[:], AF.Ln, [TL("t4"), Tlb], [TL("t4")], bias=lbc)
                  ts(TK[:], TSG[:], nomlc, omlc, ALU.mult, ALU.add, TSGt + [Tlb], [TL("tk"), TL("ot0"), TL("rs0"), TL("osq0"), TL("ue4_0")])
                  S.op(S.dve, lambda e: e.tensor_tensor_scan(out=TSG[:], data0=T4[:], data1=T4[:], initial=0.0,
                                                             op0=ALU.add, op1=ALU.add),
                       [TL("t4")] + TSGt + [TL("tk")], TSGt)
                  G3 = TSG[:].rearrange("p (c s) -> p c s", s=64)
                  A3 = T4[:].rearrange("p (c s) -> p c s", s=64)
                  tt(A3, G3, G3[:, :, 31:32].to_broadcast([128, 32, 64]), ALU.subtract, TSGt, [TL("t4")])
                  tt(DD[:, 0:31], G3[:, 1:32, 31:32].rearrange("p a b -> p (a b)"), G3[:, 0:31, 31:32].rearrange("p a b -> p (a b)"),
                     ALU.subtract, TSGt, [TL("dd")])
                  act(SC[:, 0:31], DD[:, 0:31], AF.Exp, [TL("dd")], [TL("sc")], scale=0.5)
                  act(TEP[:], T4[:], AF.Exp, [TL("t4")], TSGt, scale=0.5)
                  act(T4[:], T4[:], AF.Exp, [TL("t4")], [TL("t4")], scale=-0.5)
                  tt(QT[:], TQ[:], TEP[:], ALU.mult, TQt + TSGt, [TL("qt")])
                  tt(KT[:], TK[:], T4[:], ALU.mult, [TL("tk"), TL("t4")], [TL("kt")])

              def prelude(h):
                  for g in range(4):
                      for i in range(4):
                          bl = g * 4 + i
                          tr(PSTR[:, i * 128:(i + 1) * 128], KT[:, bl * 128:(bl + 1) * 128], IDB[:], [TL("kt"), TL("idb")], [TPSTR])
                      cp(KTM[:, g * 4:(g + 1) * 4, :].rearrange("p a b -> p (a b)"), PSTR[:, 0:512], [TPSTR], [TL("ktm")])
                  for g in range(2):
                      for i in range(8):
                          bl = g * 8 + i
                          for hh in range(2):
                              c = 2 * bl + hh
                              mm(PS4[hh * 64:(hh + 1) * 64, i * 64:(i + 1) * 64],
                                 [(KT[:, c * 64:(c + 1) * 64], QT[:, c * 64:(c + 1) * 64])], [TL("kt"), TL("qt")], [TPS4])
                      tt(PM[:, g * 8:(g + 1) * 8, :], PS4[:].rearrange("p (a b) -> p a b", b=64),
                         MASK2.unsqueeze(1).to_broadcast([128, 8, 64]), ALU.mult, [TPS4, TL("consts")], [TL("pm")])

              UBANK = [(PS5, TPS5), (PSTRF, TPSTR)]

              def u_batch(h, bt):
                  n = min(4, 31 - 4 * bt)
                  for i in range(n):
                      c = 4 * bt + i
                      bl = c // 2
                      p0 = 64 * (c % 2)
                      bank, Tbank = UBANK[i % 2]
                      mm(bank[:, (i // 2) * 128:(i // 2 + 1) * 128], [(KTM[p0:p0 + 64, bl, :], VTM[p0:p0 + 64, bl, :])],
                         [TL("ktm")] + TVt, [Tbank])
                  ue = UE4[bt % 2]
                  Tue = TL(f"ue4_{bt % 2}")
                  for i in range(n):
                      c = 4 * bt + i
                      bank, Tbank = UBANK[i % 2]
                      act(ue[:, i * 128:(i + 1) * 128], bank[:, (i // 2) * 128:(i // 2 + 1) * 128], AF.Identity,
                          [Tbank, TL("sc")], [Tue], scale=SC[:, c:c + 1])

              def r_step(h, c):
                  bl = c // 2
                  p0 = 64 * (c % 2)
                  ob = (c // 8) % 2
                  oc = (c % 8) * 64
                  pairs = []
                  rd = [TL("qt"), TL("pm")] + TVt
                  if c > 0:
                      pairs.append((Zb[c % 3][:], QT[:, c * 64:(c + 1) * 64]))
                      rd.append(TL(f"zb{c % 3}"))
                  pairs.append((VTM[p0:p0 + 64, bl, :], PM[p0:p0 + 64, bl, :]))
                  mm(PSO[ob][:, oc:oc + 64], pairs, rd, [TPSO[ob]])
                  if c < 31:
                      bt = c // 4
                      ue = UE4[bt % 2][:, (c % 4) * 128:(c % 4 + 1) * 128]
                      Tue = TL(f"ue4_{bt % 2}")
                      r3 = c % 3
                      n3 = (c + 1) % 3
                      if c == 0:
                          cp(Zf[n3][:], ue, [Tue], [TL(f"zf{n3}")])
                      else:
                          stt(Zf[n3][:], Zf[r3][:], SC[:, c:c + 1], ue, ALU.mult, ALU.add,
                              [TL(f"zf{r3}"), TL("sc"), Tue], [TL(f"zf{n3}")])
                      act(Zb[n3][:], Zf[n3][:], AF.Copy, [TL(f"zf{n3}")], [TL(f"zb{n3}")])
                  if c % 8 == 7:
                      tb = c // 8
                      act(OSQ[ob][:], PSO[ob][:], AF.Square, [TPSO[ob]], [TL("osq0")])
                      mm(PS4[:], [(ONESH[:], OSQ[ob][:])], [TL("onesh"), TL("osq0")], [TPS4])
                      act(RS[ob][:], PS4[:], AF.Ln, [TPS4, TL("epsr")], [TL("rs0")], bias=EPSR[:])
                      act(RS[ob][:], RS[ob][:], AF.Exp, [TL("rs0")], [TL("rs0")], scale=-0.5)
                      tt(OT[ob][:], PSO[ob][:], RS[ob][:], ALU.mult, [TPSO[ob], TL("rs0")], [TL("ot0")])
                      stt(OG.ap(h, tb), OT[ob][:], CO[:, C_GNW:C_GNW + 1], OG.ap(h, tb), ALU.mult, ALU.mult,
                          [TL("ot0"), TCO, OG.T[h][tb]], [OG.T[h][tb]])

              head_begin(0)
              for g in p_qfg_groups(0):
                  g()
              p_v(0)
              gating(0)
              for h in range(8):
                  prelude(h)
                  side = []
                  if h + 1 < 8:
                      head_begin(h + 1)
                      side = p_qfg_groups(h + 1)
                  si = 0
                  u_batch(h, 0)
                  if h == 0:
                      chk(301)
                  for c in range(32):
                      if c % 4 == 0 and 4 * (c // 4 + 1) < 31:
                          u_batch(h, c // 4 + 1)
                          if h == 0 and c == 0:
                              chk(302)
                      r_step(h, c)
                      if h == 0:
                          chk(310 + c)
                      want = (len(side) * (c + 1)) // 32
                      while si < want:
                          side[si]()
                          si += 1
                  while si < len(side):
                      side[si]()
                      si += 1
                  if h + 1 < 8:
                      p_v(h + 1)
                      gating(h + 1)

              S.barrier()
              chk(5)
              MRG = BUFB
              for s_ in range(4):
                  W, TW = wnext()
                  W3 = W[:].rearrange("p (k e) -> p k e", e=512)
                  for jj in range(2):
                      j = 2 * s_ + jj
                      co = jj * 256
                      for tb in range(NTB):
                          psA, TpsA = psnext()
                          mm(psA[:], [(W3[:, k, co:co + 128], OG.ap(k, tb)) for k in range(8)], [TW] + OG.col(tb), [TpsA])
                          psB, TpsB = psnext()
                          mm(psB[:], [(W3[:, k, co + 128:co + 256], XB.ap(k, tb)) for k in range(8)], [TW] + XB.col(tb), [TpsB])
                          b = tb % 2
                          act(TMP[b][:], psB[:], AF.Sigmoid, [TpsB, TCO], [TL(f"tmp{b}")], bias=bcol(6, j))
                          tt(MRG.ap(j, tb), psA[:], TMP[b][:], ALU.mult, [TpsA, TL(f"tmp{b}")], [MRG.T[j][tb]])

              S.barrier()
              chk(6)
              YC = BUFA
              S.op(S.dve, lambda e: e.memset(UP[0][:, 0:CW - 1], 0.0), writes=[TL("up0")])
              for s_ in range(4):
                  W, TW = wnext()
                  W3 = W[:].rearrange("p (k e) -> p k e", e=512)
                  for jj in range(2):
                      j = 2 * s_ + jj
                      co = jj * 256
                      ub = 0
                      for (ta, tb_, nm) in ((0, 16, "dga"), (16, CW, "dgb")):
                          nt = tb_ - ta
                          wcol = CO[:, C_WDW + j * CW + ta:C_WDW + j * CW + tb_]
                          tt(DG[ub][:, ta:tb_, :], IDF.unsqueeze(1).to_broadcast([128, nt, 128]),
                             wcol.unsqueeze(2).to_broadcast([128, nt, 128]), ALU.mult, [TL("consts"), TCO], [TL(nm)])
                      for tb in range(NTB):
                          psA, TpsA = psnext()
                          mm(psA[:], [(W3[:, k, co:co + 128], XB.ap(k, tb)) for k in range(8)], [TW] + XB.col(tb), [TpsA])
                          psB, TpsB = psnext()
                          mm(psB[:], [(W3[:, k, co + 128:co + 256], XB.ap(k, tb)) for k in range(8)], [TW] + XB.col(tb), [TpsB])
                          b = tb % 2
                          act(TMP[b][:], psB[:], AF.Sigmoid, [TpsB, TCO], [TL(f"tmp{b}")], bias=bcol(5, j))
                          stt(UP[ub][:, CW - 1 + tb * TB:CW - 1 + (tb + 1) * TB], psA[:], bcol(4, j), TMP[b][:], ALU.add, ALU.mult,
                              [TpsA, TCO, TL(f"tmp{b}")], [TL(f"up{ub}")])
                      for tb in range(NTB):
                          CB, TCB = ((PS4, TPS4), (PS5, TPS5))[tb % 2]
                          mm(CB[:], [(DG[ub][:, tap, :], UP[ub][:, tb * TB + tap:tb * TB + tap + TB]) for tap in range(CW)],
                             [TL("dga"), TL("dgb"), TL(f"up{ub}")], [TCB])
                          act(YC.ap(j, tb), CB[:], AF.Identity, [TCB, TCO], [YC.T[j][tb]], bias=CO[:, C_BDW + j:C_BDW + j + 1])
              ln_fm(YC, False, colsp, C_CLG, C_CLB, AF.Silu, YC)

              chk(7)
              for s_ in range(4):
                  W, TW = wnext()
                  W3 = W[:].rearrange("p (k e) -> p k e", e=512)
                  for jj in range(2):
                      j = 2 * s_ + jj
                      co = jj * 256
                      for tb in range(NTB):
                          psA, TpsA = psnext()
                          mm(psA[:], [(W3[:, k, co:co + 128], YC.ap(k, tb)) for k in range(8)], [TW] + YC.col(tb), [TpsA])
                          psB, TpsB = psnext()
                          mm(psB[:], [(W3[:, k, co + 128:co + 256], XB.ap(k, tb)) for k in range(8)], [TW] + XB.col(tb), [TpsB])
                          b = tb % 2
                          act(TMP[b][:], psB[:], AF.Sigmoid, [TpsB, TCO], [TL(f"tmp{b}")], bias=bcol(7, j))
                          stt(T1[b][:], psA[:], CO[:, C_BB + j:C_BB + j + 1], TMP[b][:], ALU.add, ALU.mult,
                              [TpsA, TCO, TL(f"tmp{b}")], [TL(f"t1{b}")])
                          tt(MRG.ap(j, tb), MRG.ap(j, tb), T1[b][:], ALU.add, [MRG.T[j][tb], TL(f"t1{b}")], [MRG.T[j][tb]])
              chk(8)
              for s_ in range(2):
                  W, TW = wnext()
                  W3 = W[:].rearrange("p (k e) -> p k e", e=512)
                  for jj in range(4):
                      j = 4 * s_ + jj
                      for tb in range(NTB):
                          ps, Tps = psnext()
                          mm(ps[:], [(W3[:, k, jj * 128:(jj + 1) * 128], MRG.ap(k, tb)) for k in range(8)], [TW] + MRG.col(tb), [Tps])
                          stt(X.ap(j, tb), X.ap(j, tb), ALPHA, ps[:], ALU.mult, ALU.add, [X.T[j][tb], Tps], [X.T[j][tb]])
                          act(XB.ap(j, tb), X.ap(j, tb), AF.Copy, [X.T[j][tb]], [XB.T[j][tb]])
              ln_fm(X, True, colsp, C_L1G, C_L1B, AF.Identity, X, XB)

              S.barrier()
              chk(9)
              THID = [[Tile() for _ in range(2)] for _ in range(NFC)]
              for half in range(2):
                  for s_ in range(11):
                      W, TW = wnext()
                      W3 = W[:].rearrange("p (k e) -> p k e", e=512)
                      for jj in range(2):
                          j = 2 * s_ + jj
                          co = jj * 256
                          for t2 in range(2):
                              tb = half * 2 + t2
                              psA, TpsA = psnext()
                              mm(psA[:], [(W3[:, k, co:co + 128], XB.ap(k, tb)) for k in range(8)], [TW] + XB.col(tb), [TpsA])
                              psB, TpsB = psnext()
                              mm(psB[:], [(W3[:, k, co + 128:co + 256], XB.ap(k, tb)) for k in range(8)], [TW] + XB.col(tb), [TpsB])
                              b = t2
                              act(TMP[b][:], psA[:], AF.Silu, [TpsA], [TL(f"tmp{b}")])
                              tt(HID_t[:, j, t2 * TB:(t2 + 1) * TB], psB[:], TMP[b][:], ALU.mult, [TpsB, TL(f"tmp{b}")], [THID[j][t2]])
                  for s_ in range(8):
                      W, TW = wnext()
                      W3 = W[:, 0:NFC * 128].rearrange("p (k e) -> p k e", e=128)
                      for t2 in range(2):
                          tb = half * 2 + t2
                          ps, Tps = psnext()
                          mm(ps[:], [(W3[:, k, :], HID_t[:, k, t2 * TB:(t2 + 1) * TB]) for k in range(NFC)],
                             [TW] + [THID[k][t2] for k in range(NFC)], [Tps])
                          stt(X.ap(s_, tb), X.ap(s_, tb), ALPHA, ps[:], ALU.mult, ALU.add, [X.T[s_][tb], Tps], [X.T[s_][tb]])
                          act(XB.ap(s_, tb), X.ap(s_, tb), AF.Copy, [X.T[s_][tb]], [XB.T[s_][tb]])
              ln_fm(X, True, colsp, C_L2G, C_L2B, AF.Identity, X, XB)


          except _Stop:
              break
        S.barrier()
        last = None
        for tt_i in range(16):
            b = tt_i % 2
            t0 = tt_i * 128
            tb = tt_i // 4
            for half in range(2):
                ps, Tps = psnext()
                for jj in range(4):
                    j = half * 4 + jj
                    tr(ps[:, jj * 128:(jj + 1) * 128], X_t[:, j, t0:t0 + 128], IDF, [X.T[j][tb], TL("consts")], [Tps])
                act(XN[b][:, half * 512:(half + 1) * 512], ps[:], AF.Copy, [Tps], [TXN[b]])
            last = S.op(S.sp, lambda e, b=b, t0=t0: e.dma_start(out=out_d[t0:t0 + 128, :], in_=XN[b][:]), reads=[TXN[b]], dma=st_o)
        S.emit(final_waits=[last])
    return nc


def _strip(W, colsets):
    K = W.shape[0]
    out = []
    for cols in colsets:
        Ws = W[:, cols]
        n = Ws.shape[1]
        out.append(Ws.reshape(K // 128, 128, n).transpose(1, 0, 2).reshape(128, (K // 128) * n))
    return np.ascontiguousarray(np.stack(out, 0))


def _fm(v):
    return np.ascontiguousarray(v.reshape(-1, 128).T)


def prep(inputs, depth=DEPTH):
    f = lambda a: np.asarray(a, dtype=np.float32)
    w_in, b_in = f(inputs["w_in"]), f(inputs["b_in"])
    w_a, w_b, w_o = f(inputs["w_a"]), f(inputs["w_b"]), f(inputs["w_o"])
    w_up, w_dn, w_dw = f(inputs["w_up"]), f(inputs["w_down"]), f(inputs["w_dw"])
    ar = np.arange(128)
    consts = np.zeros((128, NCONST), np.float32)
    consts[:, K_ID:K_ID + 128] = np.eye(128, dtype=np.float32)
    p = np.arange(128)[:, None] % 64
    t = np.arange(64)[None, :]
    consts[:, K_MASK:K_MASK + 64] = (p <= t).astype(np.float32)
    ln0 = np.concatenate([_fm(f(inputs["ln0_g"])), _fm(f(inputs["ln0_b"]))], axis=1)
    lbl = np.concatenate([_fm(f(inputs["lb_logits"])[l]) for l in range(DEPTH)], axis=1)
    cols = np.zeros((DEPTH, 128, NCOL), np.float32)
    bibc = np.zeros((DEPTH, 128, D), np.float32)
    wih, wic, wma, wmb, wo, wup, wdn = [], [], [], [], [], [], []
    for l in range(DEPTH):
        for sec in range(8):
            cols[l, :, C_BIN + sec * 8:C_BIN + sec * 8 + 8] = _fm(b_in[l, sec * D:(sec + 1) * D])
        cols[l, :, C_GNW] = f(inputs["g_norm_w"])[l]
        cols[l, :, C_WDW:C_WDW + 8 * CW] = w_dw[l].T.reshape(8, 128, CW).transpose(1, 0, 2).reshape(128, 8 * CW)
        for nm, c0 in (("b_dw", C_BDW), ("conv_ln_g", C_CLG), ("conv_ln_b", C_CLB), ("b_b", C_BB),
                       ("ln1_g", C_L1G), ("ln1_b", C_L1B), ("ln2_g", C_L2G), ("ln2_b", C_L2B)):
            cols[l, :, c0:c0 + 8] = _fm(f(inputs[nm])[l])
        bibc[l] = np.broadcast_to(b_in[l, 2 * D:3 * D][None, :], (128, D))
        if l >= depth:
            continue
        wih.append(_strip(w_in[l], [np.concatenate([sec * D + h * 128 + ar for sec in range(4)]) for h in range(8)]))
        wic.append(_strip(w_in[l], [np.concatenate([4 * D + j * 128 + ar, 5 * D + j * 128 + ar,
                                                    4 * D + (j + 1) * 128 + ar, 5 * D + (j + 1) * 128 + ar])
                                    for j in range(0, 8, 2)]))
        cat_a = np.concatenate([w_a[l], w_in[l][:, 6 * D:7 * D]], axis=1)
        cat_b = np.concatenate([w_b[l], w_in[l][:, 7 * D:8 * D]], axis=1)
        sets = [np.concatenate([j * 128 + ar, D + j * 128 + ar, (j + 1) * 128 + ar, D + (j + 1) * 128 + ar])
                for j in range(0, 8, 2)]
        wma.append(_strip(cat_a, sets))
        wmb.append(_strip(cat_b, sets))
        wo.append(_strip(w_o[l], [np.arange(s * 512, (s + 1) * 512) for s in range(2)]))
        wup.append(_strip(w_up[l], [np.concatenate([j * 128 + ar, FH + j * 128 + ar, (j + 1) * 128 + ar, FH + (j + 1) * 128 + ar])
                                    for j in range(0, NFC, 2)]))
        wdn.append(_strip(w_dn[l], [np.arange(s * 128, (s + 1) * 128) for s in range(8)]))

    def stk(lst, shape):
        a = np.zeros((DEPTH,) + shape, np.float32)
        for i, v in enumerate(lst):
            a[i] = v
        return a
    return {
        "consts": consts, "ln0": np.ascontiguousarray(ln0), "lbl": np.ascontiguousarray(lbl), "cols": cols, "bibc": bibc,
        "wih": stk(wih, (8, 128, 4096)), "wic": stk(wic, (4, 128, 4096)), "wma": stk(wma, (4, 128, 4096)),
        "wmb": stk(wmb, (4, 128, 4096)), "wo": stk(wo, (2, 128, 4096)), "wup": stk(wup, (11, 128, 4096)),
        "wdn": stk(wdn, (8, 128, NFC * 128)),
    }


_NC_CACHE = {}


def kernel(**inputs):
    x = np.asarray(inputs["x"], dtype=np.float32)
    shared = prep(inputs)
    if "nc" not in _NC_CACHE:
        _NC_CACHE["nc"] = build(DEPTH)
    nc = _NC_CACHE["nc"]
    in_maps = []
    for b in range(8):
        m = dict(shared)
        m["x"] = np.ascontiguousarray(x[b])
        in_maps.append(m)
    res = run_bass_kernel_spmd(nc, in_maps, core_ids=list(range(8)))
    return np.stack([np.asarray(r["out"], dtype=np.float32) for r in res.results], axis=0)
```

```python
import numpy as np
from contextlib import ExitStack
import concourse.bass as bass
import concourse.mybir as mybir
from concourse.bass_utils import run_bass_kernel_spmd

F32 = mybir.dt.float32
BF16 = mybir.dt.bfloat16
AF = mybir.ActivationFunctionType
ALU = mybir.AluOpType
AX = mybir.AxisListType

D = 1024
T = 2048
DEPTH = 4
NCH = 8
TB = 512
NTB = 4
FH = 2816
NFC = 22
CW = 31
ALPHA = (2 * DEPTH) ** 0.25
LN_EPS = 1e-5
RMS_EPS = 1e-6
F_MIN = 1e-30

C_BIN = 0
C_GNW = 64
C_WDW = 65
C_BDW = C_WDW + 8 * CW
C_CLG = C_BDW + 8
C_CLB = C_CLG + 8
C_BB = C_CLB + 8
C_L1G = C_BB + 8
C_L1B = C_L1G + 8
C_L2G = C_L1B + 8
C_L2B = C_L2G + 8
NCOL = C_L2B + 8
K_ID = 0
K_MASK = 128
K_RESET = 192
NCONST = 192
NWB = 2


class Tile:
    __slots__ = ("w", "r")

    def __init__(self):
        self.w = None
        self.r = {}


class Eng:
    def __init__(self, S, name):
        self.S = S
        self.name = name
        self.ops = []
        self.sem = None
        self.cnt = 0
        self.waited = {}

    def rotate(self):
        self.sem = self.S.new_sem(self.name)
        self.cnt = 0


class DmaStream:
    def __init__(self, S, name):
        self.sem = S.new_sem(name)
        self.cnt = 0


class Sched:
    def __init__(self, nc, stack):
        self.nc = nc
        self.stack = stack
        self.nsem = 0
        self.pe = Eng(self, "pe")
        self.act = Eng(self, "act")
        self.dve = Eng(self, "dve")
        self.pool = Eng(self, "pool")
        self.sp = Eng(self, "sp")
        self.engs = [self.pe, self.act, self.dve, self.pool, self.sp]
        for e in self.engs:
            e.rotate()

    def new_sem(self, name):
        self.nsem += 1
        return self.stack.enter_context(self.nc.semaphore(f"s{self.nsem}_{name}"))

    def rotate_all(self):
        for e in self.engs:
            e.rotate()

    def op(self, eng, fn, reads=(), writes=(), dma=None):
        need = {}
        for t in reads:
            if t.w is not None:
                s, v = t.w
                if need.get(s, 0) < v:
                    need[s] = v
        for t in writes:
            if t.w is not None:
                s, v = t.w
                if need.get(s, 0) < v:
                    need[s] = v
            for s, v in t.r.items():
                if need.get(s, 0) < v:
                    need[s] = v
        waits = []
        for s, v in need.items():
            if eng is self.pe and s is eng.sem:
                continue
            if eng.waited.get(s, 0) < v:
                eng.waited[s] = v
                waits.append((s, v))
        if dma is not None:
            dma.cnt += 16
            point = (dma.sem, dma.cnt)
            eng.ops.append((waits, fn, dma.sem, 16))
        else:
            eng.cnt += 1
            point = (eng.sem, eng.cnt)
            eng.ops.append((waits, fn, eng.sem, 1))
        s, v = point
        for t in reads:
            if t.r.get(s, 0) < v:
                t.r[s] = v
        for t in writes:
            t.w = point
            t.r = {}
        return point

    def barrier(self):
        comp = [self.pe, self.act, self.dve, self.pool]
        for e in comp:
            waits = []
            for f in comp:
                if f is e or f.cnt == 0:
                    continue
                if e.waited.get(f.sem, 0) < f.cnt:
                    e.waited[f.sem] = f.cnt
                    waits.append((f.sem, f.cnt))
            if waits:
                e.ops.append((waits, None, None, 0))

    def emit(self, final_waits=()):
        def replay(eng, hw):
            for waits, fn, sem, inc in eng.ops:
                for s, v in waits:
                    hw.wait_ge(s, v)
                if fn is None:
                    continue
                ins = fn(hw)
                if sem is not None:
                    ins.then_inc(sem, inc)

        with self.nc.Block() as block:
            @block.tensor
            def _(e):
                replay(self.pe, e)

            @block.scalar
            def _(e):
                replay(self.act, e)

            @block.vector
            def _(e):
                replay(self.dve, e)

            @block.gpsimd
            def _(e):
                replay(self.pool, e)

            @block.sync
            def _(e):
                replay(self.sp, e)
                for s, v in final_waits:
                    e.wait_ge(s, v)


class Buf3:
    def __init__(self, ap3, nj, ntok=T):
        self.t = ap3
        self.nj = nj
        self.T = [[Tile() for _ in range(ntok // TB)] for _ in range(nj)]

    def ap(self, j, tb):
        return self.t[:, j, tb * TB:(tb + 1) * TB]

    def col(self, tb):
        return [self.T[j][tb] for j in range(self.nj)]


class _Stop(Exception):
    pass


def build(depth=DEPTH, dbg=False, stop=None):
    def chk(n):
        if stop == n:
            raise _Stop()
    nc = bass.Bass("TRN2", target_bir_lowering=False)

    def din(name, shape):
        return nc.dram_tensor(name, list(shape), F32, kind="ExternalInput").ap()

    x_d = din("x", [T, D])
    consts_d = din("consts", [128, NCONST])
    ln0_d = din("ln0", [128, 16])
    lbl_d = din("lbl", [128, DEPTH * 8])
    cols_d = din("cols", [DEPTH, 128, NCOL])
    bibc_d = din("bibc", [DEPTH, 128, D])
    wih_d = din("wih", [DEPTH, 8, 128, 4096])
    wic_d = din("wic", [DEPTH, 4, 128, 4096])
    wma_d = din("wma", [DEPTH, 4, 128, 4096])
    wmb_d = din("wmb", [DEPTH, 4, 128, 4096])
    wo_d = din("wo", [DEPTH, 2, 128, 4096])
    wup_d = din("wup", [DEPTH, 11, 128, 4096])
    wdn_d = din("wdn", [DEPTH, 8, 128, NFC * 128])
    out_d = nc.dram_tensor("out", [T, D], F32, kind="ExternalOutput").ap()
    dbg_d = {}

    with ExitStack() as st:
        S = Sched(nc, st)

        def sb(name, shape, dt=F32):
            return st.enter_context(nc.sbuf_tensor(name, list(shape), dt))

        def pst(name, shape, dt=F32):
            return st.enter_context(nc.psum_tensor(name, list(shape), dt))

        DD = sb("DDEC", [128, 64])
        SC = sb("SCAL", [128, 64])
        X_t = sb("X", [128, NCH, T])
        XB_t = sb("XB", [128, NCH, T], BF16)
        BAB = sb("BAB", [128, 32768], BF16)
        X = Buf3(X_t, NCH)
        XB = Buf3(XB_t, NCH)
        BUFA = Buf3(BAB[:, 0:16384].rearrange("p (j t) -> p j t", t=T), NCH)
        BUFB = Buf3(BAB[:, 16384:32768].rearrange("p (j t) -> p j t", t=T), NCH)
        HID_t = BAB[:, 0:NFC * 1024].rearrange("p (j t) -> p j t", t=1024)

        def f32row(i):
            return BAB[:, 16384 + 4096 * i:16384 + 4096 * (i + 1)].bitcast(F32)
        TQ, TSG, TK, T4 = f32row(0), f32row(1), f32row(2), f32row(3)
        SCR = sb("SCR", [128, 10240], BF16)

        def scr(a, n, dt=BF16):
            v = SCR[:, a:a + n]
            return v.bitcast(F32) if dt == F32 else v
        QT = scr(0, 2048)
        KT = scr(2048, 2048)
        KTM = scr(4096, 2048).rearrange("p (a b) -> p a b", b=128)
        VTM = scr(6144, 2048).rearrange("p (a b) -> p a b", b=128)
        PM = scr(8192, 1024).rearrange("p (a b) -> p a b", b=64)
        OT = [TK[:, 0:512]] * 2
        RS = [TK[:, 512:1024]] * 2
        OSQ = [TK[:, 1024:1280].bitcast(BF16)] * 2
        UP = [scr(0, 2080)] * 2
        DG = [scr(2080, CW * 128).rearrange("p (a b) -> p a b", b=128)] * 2
        TMP = [scr(6048, 1024, F32), scr(7072, 1024, F32)]
        SQ = scr(6048, 2048).rearrange("p (a b) -> p a b", b=TB)
        T1 = [scr(8096, 1024, F32), scr(9120, 1024, F32)]
        T1P = scr(8096, 2048, F32).rearrange("p (a b) -> p a b", b=TB)
        TEP = TSG
        Zf = [sb(f"Zf{i}", [128, 128]) for i in range(3)]
        Zb = [sb(f"Zb{i}", [128, 128], BF16) for i in range(3)]
        UE4 = [TK[:, 1280:1792], sb("UE4B", [128, 512])]
        WB = [sb(f"WB{i}", [128, 4096], BF16) for i in range(NWB)]
        CONSTS = sb("CONSTS", [128, NCONST])
        IDB = sb("IDB", [128, 128], BF16)
        ONESB = sb("ONESB", [128, 128], BF16)
        ONESH = sb("ONESH", [128, 128], BF16)
        EPSL = sb("EPSL", [128, 1])
        EPSR = sb("EPSR", [128, 1])
        ONEC = sb("ONEC", [128, 1])
        COLS = [sb("COLS0", [128, NCOL])] * 2
        BIBC = [sb(f"BIBC{i}", [128, 128]) for i in range(2)]
        LN0 = sb("LN0", [128, 16])
        LBL = sb("LBL", [128, DEPTH, 8])
        LBE = sb("LBE", [128, DEPTH, 8])
        LB = sb("LB", [128, DEPTH, 8])
        OML = sb("OML", [128, DEPTH, 8])
        NOML = sb("NOML", [128, DEPTH, 8])
        THR = sb("THR", [128, DEPTH, 8])
        LBT = sb("LBT", [128, 8])

        def bab32(a, n):
            return BAB[:, a:a + 2 * n].bitcast(F32)
        XIN = [bab32(0, D), bab32(2048, D)]
        XN = [bab32(4096, D), bab32(6144, D)]
        JUNK = BAB[:, 8192:8192 + D]
        SS = [sb(f"SS{i}", [128, 8]) for i in range(2)]

        IDF = CONSTS[:, K_ID:K_ID + 128]
        MASK2 = CONSTS[:, K_MASK:K_MASK + 64]

        PS = [pst(f"PS{i}", [128, TB]) for i in range(3)]
        PSTR = pst("PSTR", [128, 1024], BF16)
        PSTRF = PSTR[:, 0:1024].bitcast(F32)
        PS4 = pst("PS4", [128, TB])
        PS5 = pst("PS5", [128, TB])
        PS6 = pst("PS6", [128, TB])
        PS7 = pst("PS7", [128, TB])
        TPS = [Tile() for _ in range(3)]
        TPSTR, TPS4, TPS5, TPS6, TPS7 = Tile(), Tile(), Tile(), Tile(), Tile()
        TU = [Tile() for _ in range(4)]
        ps_rr = [0]

        def psnext():
            i = ps_rr[0] % 3
            ps_rr[0] += 1
            return PS[i], TPS[i]

        tl = {}

        def TL(name):
            if name not in tl:
                tl[name] = Tile()
            return tl[name]

        def act(out, in_, func, reads, writes, bias=None, scale=None, accum_out=None):
            kw = {}
            if bias is not None:
                kw["bias"] = bias
            if scale is not None:
                kw["scale"] = scale
            if accum_out is not None:
                kw["accum_out"] = accum_out
            return S.op(S.act, lambda e: e.activation(out=out, in_=in_, func=func, **kw), reads, writes)

        def tt(out, in0, in1, op, reads, writes, eng=None):
            eng = eng or S.dve
            return S.op(eng, lambda e: e.tensor_tensor(out=out, in0=in0, in1=in1, op=op), reads, writes)

        def ts(out, in0, s1, s2, op0, op1, reads, writes, eng=None):
            eng = eng or S.dve
            if op1 is None:
                return S.op(eng, lambda e: e.tensor_scalar(out=out, in0=in0, scalar1=s1, scalar2=None, op0=op0), reads, writes)
            return S.op(eng, lambda e: e.tensor_scalar(out=out, in0=in0, scalar1=s1, scalar2=s2, op0=op0, op1=op1), reads, writes)

        def stt(out, in0, scalar, in1, op0, op1, reads, writes):
            return S.op(S.dve, lambda e: e.scalar_tensor_tensor(out=out, in0=in0, scalar=scalar, in1=in1, op0=op0, op1=op1), reads, writes)

        def cp(out, in_, reads, writes, eng=None):
            eng = eng or S.dve
            return S.op(eng, lambda e: e.tensor_copy(out=out, in_=in_), reads, writes)

        def mm(out, pairs, reads, writes, first=True, last=True):
            def fn(e):
                n = len(pairs)
                ins = None
                for i, (l, r) in enumerate(pairs):
                    ins = e.matmul(out, lhsT=l, rhs=r, start=(first and i == 0), stop=(last and i == n - 1))
                return ins
            return S.op(S.pe, fn, reads, writes)

        def tr(out, in_, ident, reads, writes):
            return S.op(S.pe, lambda e: e.transpose(out=out, in_=in_, identity=ident), reads, writes)

        ld_misc = DmaStream(S, "ldm")
        ld_x = DmaStream(S, "ldx")
        ld_w = DmaStream(S, "ldw")
        st_o = DmaStream(S, "sto")

        wlist = []
        for l in range(depth):
            for s_ in range(8):
                wlist.append((wih_d[l, s_], 4096))
            for s_ in range(4):
                wlist.append((wma_d[l, s_], 4096))
            for s_ in range(4):
                wlist.append((wic_d[l, s_], 4096))
            for s_ in range(4):
                wlist.append((wmb_d[l, s_], 4096))
            for s_ in range(2):
                wlist.append((wo_d[l, s_], 4096))
            for half in range(2):
                for s_ in range(11):
                    wlist.append((wup_d[l, s_], 4096))
                for s_ in range(8):
                    wlist.append((wdn_d[l, s_], NFC * 128))
        TWB = [Tile() for _ in range(NWB)]
        wstate = {"issued": 0, "used": 0}

        def w_issue():
            i = wstate["issued"]
            if i >= len(wlist):
                return
            src, n = wlist[i]
            b = i % NWB
            S.op(S.pool, lambda e: e.dma_start(out=WB[b][:, 0:n], in_=src), reads=[], writes=[TWB[b]], dma=ld_w)
            wstate["issued"] += 1

        def wnext():
            i = wstate["used"]
            while wstate["issued"] < min(i + NWB, len(wlist)):
                w_issue()
            wstate["used"] += 1
            b = i % NWB
            return WB[b], TWB[b]

        S.op(S.sp, lambda e: e.dma_start(out=CONSTS[:], in_=consts_d), writes=[TL("consts")], dma=ld_misc)
        S.op(S.sp, lambda e: e.dma_start(out=LN0[:], in_=ln0_d), writes=[TL("ln0")], dma=ld_misc)
        S.op(S.sp, lambda e: e.dma_start(out=LBL[:].rearrange("p a b -> p (a b)"), in_=lbl_d), writes=[TL("lbl")], dma=ld_misc)
        cp(IDB[:], IDF, [TL("consts")], [TL("idb")])
        S.op(S.dve, lambda e: e.memset(ONESB[:], 1.0 / D), writes=[TL("onesb")])
        S.op(S.dve, lambda e: e.memset(ONESH[:], 1.0 / 128), writes=[TL("onesh")])
        S.op(S.dve, lambda e: e.memset(EPSL[:], LN_EPS), writes=[TL("epsl")])
        S.op(S.dve, lambda e: e.memset(EPSR[:], RMS_EPS), writes=[TL("epsr")])
        S.op(S.dve, lambda e: e.memset(ONEC[:], 1.0), writes=[TL("onec")])

        Tlb = TL("lb")
        cp(LBT[:], LBL[:, 0, :], [TL("lbl")], [TL("lbt")])
        for l in range(1, DEPTH):
            tt(LBT[:], LBT[:], LBL[:, l, :], ALU.max, [TL("lbl"), TL("lbt")], [TL("lbt")])
        for l in range(DEPTH):
            tt(LBE[:, l, :], LBL[:, l, :], LBT[:], ALU.subtract, [TL("lbl"), TL("lbt")], [TL("lbe")])
        act(LBE[:].rearrange("p a b -> p (a b)"), LBE[:].rearrange("p a b -> p (a b)"), AF.Exp, [TL("lbe")], [TL("lbe")])
        cp(LBT[:], LBE[:, 0, :], [TL("lbe")], [TL("lbt")])
        for l in range(1, DEPTH):
            tt(LBT[:], LBT[:], LBE[:, l, :], ALU.add, [TL("lbe"), TL("lbt")], [TL("lbt")])
        S.op(S.dve, lambda e: e.reciprocal(out=LBT[:], in_=LBT[:]), [TL("lbt")], [TL("lbt")])
        for l in range(DEPTH):
            tt(LBE[:, l, :], LBE[:, l, :], LBT[:], ALU.mult, [TL("lbe"), TL("lbt")], [TL("lbe")])
        S.op(S.dve, lambda e: e.memset(LB[:, 0, :], 0.0), writes=[Tlb])
        for l in range(1, DEPTH):
            tt(LB[:, l, :], LB[:, l - 1, :], LBE[:, l, :], ALU.add, [TL("lbe"), Tlb], [Tlb])
        LBf = LB[:].rearrange("p a b -> p (a b)")
        ts(OML[:].rearrange("p a b -> p (a b)"), LBf, -1.0, 1.0, ALU.mult, ALU.add, [Tlb], [Tlb])
        ts(NOML[:].rearrange("p a b -> p (a b)"), LBf, 1.0, -1.0, ALU.mult, ALU.add, [Tlb], [Tlb])
        ts(THR[:].rearrange("p a b -> p (a b)"), LBf, -1.0, F_MIN, ALU.mult, ALU.add, [Tlb], [Tlb])

        TCOLS = [Tile()] * 2

        def load_params(l):
            b = l % 2
            S.op(S.sp, lambda e: e.dma_start(out=COLS[b][:], in_=cols_d[l]), writes=[TCOLS[b]], dma=ld_misc)

        load_params(0)

        TXIN = [Tile(), Tile()]
        TXN = [Tile(), Tile()]
        TSS = [Tile(), Tile()]
        for tt_i in range(16):
            b = tt_i % 2
            t0 = tt_i * 128
            tb = tt_i // 4
            S.op(S.sp, lambda e, b=b, t0=t0: e.dma_start(out=XIN[b][:], in_=x_d[t0:t0 + 128, :]), writes=[TXIN[b]], dma=ld_x)
            S.op(S.dve, lambda e, b=b: e.reduce_sum(out=SS[b][:, 0:1], in_=XIN[b][:], axis=AX.X), [TXIN[b]], [TSS[b]])
            act(JUNK[:], XIN[b][:], AF.Square, [TXIN[b]], [TL("junk"), TSS[b]], accum_out=SS[b][:, 1:2])
            ts(SS[b][:, 2:4], SS[b][:, 0:2], 1.0 / D, None, ALU.mult, None, [TSS[b]], [TSS[b]])
            tt(SS[b][:, 4:5], SS[b][:, 2:3], SS[b][:, 2:3], ALU.mult, [TSS[b]], [TSS[b]])
            tt(SS[b][:, 5:6], SS[b][:, 3:4], SS[b][:, 4:5], ALU.subtract, [TSS[b]], [TSS[b]])
            act(SS[b][:, 6:7], SS[b][:, 5:6], AF.Ln, [TSS[b], TL("epsl")], [TSS[b]], bias=EPSL[:])
            act(SS[b][:, 7:8], SS[b][:, 6:7], AF.Exp, [TSS[b]], [TSS[b]], scale=-0.5)
            ts(XN[b][:], XIN[b][:], SS[b][:, 2:3], SS[b][:, 7:8], ALU.subtract, ALU.mult, [TXIN[b], TSS[b]], [TXN[b]])
            for half in range(2):
                ps, Tps = psnext()
                for jj in range(4):
                    j = half * 4 + jj
                    tr(ps[:, jj * 128:(jj + 1) * 128], XN[b][:, j * 128:(j + 1) * 128], IDF, [TXN[b], TL("consts")], [Tps])
                for jj in range(4):
                    j = half * 4 + jj
                    act(X_t[:, j, t0:t0 + 128], ps[:, jj * 128:(jj + 1) * 128], AF.Identity, [Tps, TL("ln0")], [X.T[j][tb]],
                        scale=LN0[:, j:j + 1], bias=LN0[:, 8 + j:9 + j])
            cp(XB_t[:, :, t0:t0 + 128], X_t[:, :, t0:t0 + 128], X.col(tb), XB.col(tb))

        def ln_fm(src, src_f32, cols, gcol, bcol, func, dst_main, dst_bf=None, eps=None):
            banks = [(PS6, TPS6, PS7, TPS7), (PS4, TPS4, PS5, TPS5)]

            def stats(tb):
                PMn, TPMn, PVr, TPVr = banks[tb % 2]
                sl = slice(tb * TB, (tb + 1) * TB)
                sbuf = XB if src_f32 else src
                mm(PMn[:], [(ONESB[:], sbuf.ap(j, tb)) for j in range(NCH)], sbuf.col(tb) + [TL("onesb")], [TPMn])
                for hf in range(2):
                    act(SQ[:], sbuf.t[:, 4 * hf:4 * hf + 4, sl], AF.Square, sbuf.col(tb), [TL("tmp0"), TL("tmp1")])
                    mm(PVr[:], [(ONESB[:], SQ[:, j, :]) for j in range(4)], [TL("tmp0"), TL("tmp1"), TL("onesb")], [TPVr],
                       first=(hf == 0), last=(hf == 1))
                act(TMP[0][:], PMn[:], AF.Square, [TPMn], [TL("tmp0")])
                tt(TMP[1][:], PVr[:], TMP[0][:], ALU.subtract, [TPVr, TL("tmp0")], [TL("tmp1")])
                act(TMP[1][:], TMP[1][:], AF.Ln, [TL("tmp1"), TL("epsl")], [TL("tmp1")], bias=EPSL[:])
                act(PVr[:], TMP[1][:], AF.Exp, [TL("tmp1")], [TPVr], scale=-0.5)

            def apply(tb):
                PMn, TPMn, PVr, TPVr = banks[tb % 2]
                sl = slice(tb * TB, (tb + 1) * TB)
                if src_f32:
                    tt(src.t[:, :, sl], src.t[:, :, sl], PMn[:].unsqueeze(1).to_broadcast([128, NCH, TB]), ALU.subtract,
                       src.col(tb) + [TPMn], src.col(tb))
                    tt(src.t[:, :, sl], src.t[:, :, sl], PVr[:].unsqueeze(1).to_broadcast([128, NCH, TB]), ALU.mult,
                       src.col(tb) + [TPVr], src.col(tb))
                    for j in range(NCH):
                        act(dst_main.ap(j, tb), src.ap(j, tb), func, [src.T[j][tb], cols[1]], [dst_main.T[j][tb]],
                            scale=cols[0][:, gcol + j:gcol + j + 1], bias=cols[0][:, bcol + j:bcol + j + 1])
                        if dst_bf is not None:
                            act(dst_bf.ap(j, tb), dst_main.ap(j, tb), AF.Copy, [dst_main.T[j][tb]], [dst_bf.T[j][tb]])
                else:
                    for pr in range(NCH // 2):
                        tt(T1P, src.t[:, 2 * pr:2 * pr + 2, sl], PMn[:].unsqueeze(1).to_broadcast([128, 2, TB]), ALU.subtract,
                           [src.T[2 * pr][tb], src.T[2 * pr + 1][tb], TPMn], [TL("t10"), TL("t11")])
                        tt(T1P, T1P, PVr[:].unsqueeze(1).to_broadcast([128, 2, TB]), ALU.mult,
                           [TL("t10"), TL("t11"), TPVr], [TL("t10"), TL("t11")])
                        for jj in range(2):
                            j = 2 * pr + jj
                            act(dst_main.ap(j, tb), T1P[:, jj, :], func, [TL(f"t1{jj}"), cols[1]], [dst_main.T[j][tb]],
                                scale=cols[0][:, gcol + j:gcol + j + 1], bias=cols[0][:, bcol + j:bcol + j + 1])

            stats(0)
            for tb in range(NTB):
                if tb + 1 < NTB:
                    stats(tb + 1)
                apply(tb)

        TQt = [Tile() for _ in range(NTB)]
        TSGt = [Tile() for _ in range(NTB)]
        TVt = [Tile() for _ in range(NTB)]
        TPSO = [TPS6, TPS7]
        PSO = [PS6, PS7]

        for l in range(depth):
          try:
              if l > 0:
                  S.rotate_all()
              CO = COLS[l % 2]
              TCO = TCOLS[l % 2]
              colsp = (CO, TCO)

              def bcol(sec, c):
                  return CO[:, C_BIN + sec * 8 + c:C_BIN + sec * 8 + c + 1]

              S.barrier()
              if l > 0:
                  load_params(l)
              chk(0)
              OG = BUFA
              hw_ = {}

              def head_begin(h):
                  W, TW = wnext()
                  BI = BIBC[h % 2]
                  TBI = TL(f"bibc{h % 2}")
                  S.op(S.sp, lambda e, BI=BI, l=l, h=h: e.dma_start(out=BI[:], in_=bibc_d[l][:, h * 128:(h + 1) * 128]),
                       writes=[TBI], dma=ld_misc)
                  hw_[h] = (W[:].rearrange("p (k e) -> p k e", e=512), TW, BI, TBI)

              def p_qfg_groups(h):
                  W3, TW, BI, TBI = hw_[h]
                  TGG = OG.t[:, h, :]
                  groups = []
                  for tb in range(NTB):
                      sl = slice(tb * TB, (tb + 1) * TB)
                      for (c0, dst, dT, fn_, sec) in ((0, TQ, TQt, AF.Silu, 0), (128, TSG, TSGt, AF.Sigmoid, 1),
                                                      (384, TGG, OG.T[h], AF.Silu, 3)):
                          def g(tb=tb, sl=sl, c0=c0, dst=dst, dT=dT, fn_=fn_, sec=sec):
                              ps, Tps = psnext()
                              mm(ps[:], [(W3[:, k, c0:c0 + 128], XB.ap(k, tb)) for k in range(8)], [TW] + XB.col(tb), [Tps])
                              act(dst[:, sl], ps[:], fn_, [Tps, TCO], [dT[tb]], bias=bcol(sec, h))
                          groups.append(g)
                  return groups

              def p_v(h):
                  W3, TW, BI, TBI = hw_[h]
                  for tb in range(NTB):
                      ps, Tps = psnext()
                      for t4 in range(4):
                          t0 = tb * TB + t4 * 128
                          mm(ps[:, t4 * 128:(t4 + 1) * 128], [(XB_t[:, k, t0:t0 + 128], W3[:, k, 256:384]) for k in range(8)],
                             [TW] + XB.col(tb), [Tps])
                      tt(VTM[:, tb * 4:(tb + 1) * 4, :], ps[:].rearrange("p (a b) -> p a b", b=128),
                         BI[:].unsqueeze(1).to_broadcast([128, 4, 128]), ALU.add,
                         [Tps, TBI], [TVt[tb]])

              def gating(h):
                  lbc = LB[:, l, h:h + 1]
                  omlc = OML[:, l, h:h + 1]
                  nomlc = NOML[:, l, h:h + 1]
                  thrc = THR[:, l, h:h + 1]
                  ts(T4[:], TSG[:], omlc, thrc, ALU.mult, ALU.max, TSGt + [Tlb], [TL("t4")])
                  act(T4[:], T4[:], AF.Ln, [TL("t4"), Tlb], [TL("t4")], bias=lbc)
                  ts(TK[:], TSG[:], nomlc, omlc, ALU.mult, ALU.add, TSGt + [Tlb], [TL("tk"), TL("ot0"), TL("rs0"), TL("osq0"), TL("ue4_0")])
                  S.op(S.dve, lambda e: e.tensor_tensor_scan(out=TSG[:], data0=T4[:], data1=T4[:], initial=0.0,
                                                             op0=ALU.add, op1=ALU.add),
                       [TL("t4")] + TSGt + [TL("tk")], TSGt)
                  G3 = TSG[:].rearrange("p (c s) -> p c s", s=64)
                  A3 = T4[:].rearrange("p (c s) -> p c s", s=64)
                  tt(A3, G3, G3[:, :, 31:32].to_broadcast([128, 32, 64]), ALU.subtract, TSGt, [TL("t4")])
                  tt(DD[:, 0:31], G3[:, 1:32, 31:32].rearrange("p a b -> p (a b)"), G3[:, 0:31, 31:32].rearrange("p a b -> p (a b)"),
                     ALU.subtract, TSGt, [TL("dd")])
                  act(SC[:, 0:31], DD[:, 0:31], AF.Exp, [TL("dd")], [TL("sc")], scale=0.5)
                  act(TEP[:], T4[:], AF.Exp, [TL("t4")], TSGt, scale=0.5)
                  act(T4[:], T4[:], AF.Exp, [TL("t4")], [TL("t4")], scale=-0.5)
                  tt(QT[:], TQ[:], TEP[:], ALU.mult, TQt + TSGt, [TL("qt")])
                  tt(KT[:], TK[:], T4[:], ALU.mult, [TL("tk"), TL("t4")], [TL("kt")])

              def prelude(h):
                  for g in range(4):
                      for i in range(4):
                          bl = g * 4 + i
                          tr(PSTR[:, i * 128:(i + 1) * 128], KT[:, bl * 128:(bl + 1) * 128], IDB[:], [TL("kt"), TL("idb")], [TPSTR])
                      cp(KTM[:, g * 4:(g + 1) * 4, :].rearrange("p a b -> p (a b)"), PSTR[:, 0:512], [TPSTR], [TL("ktm")])
                  for g in range(2):
                      for i in range(8):
                          bl = g * 8 + i
                          for hh in range(2):
                              c = 2 * bl + hh
                              mm(PS4[hh * 64:(hh + 1) * 64, i * 64:(i + 1) * 64],
                                 [(KT[:, c * 64:(c + 1) * 64], QT[:, c * 64:(c + 1) * 64])], [TL("kt"), TL("qt")], [TPS4])
                      tt(PM[:, g * 8:(g + 1) * 8, :], PS4[:].rearrange("p (a b) -> p a b", b=64),
                         MASK2.unsqueeze(1).to_broadcast([128, 8, 64]), ALU.mult, [TPS4, TL("consts")], [TL("pm")])

              UBANK = [(PS5, TPS5), (PSTRF, TPSTR)]

              def u_batch(h, bt):
                  n = min(4, 31 - 4 * bt)
                  for i in range(n):
                      c = 4 * bt + i
                      bl = c // 2
                      p0 = 64 * (c % 2)
                      bank, Tbank = UBANK[i % 2]
                      mm(bank[:, (i // 2) * 128:(i // 2 + 1) * 128], [(KTM[p0:p0 + 64, bl, :], VTM[p0:p0 + 64, bl, :])],
                         [TL("ktm")] + TVt, [Tbank])
                  ue = UE4[bt % 2]
                  Tue = TL(f"ue4_{bt % 2}")
                  for i in range(n):
                      c = 4 * bt + i
                      bank, Tbank = UBANK[i % 2]
                      act(ue[:, i * 128:(i + 1) * 128], bank[:, (i // 2) * 128:(i // 2 + 1) * 128], AF.Identity,
                          [Tbank, TL("sc")], [Tue], scale=SC[:, c:c + 1])

              def r_step(h, c):
                  bl = c // 2
                  p0 = 64 * (c % 2)
                  ob = (c // 8) % 2
                  oc = (c % 8) * 64
                  pairs = []
                  rd = [TL("qt"), TL("pm")] + TVt
                  if c > 0:
                      pairs.append((Zb[c % 3][:], QT[:, c * 64:(c + 1) * 64]))
                      rd.append(TL(f"zb{c % 3}"))
                  pairs.append((VTM[p0:p0 + 64, bl, :], PM[p0:p0 + 64, bl, :]))
                  mm(PSO[ob][:, oc:oc + 64], pairs, rd, [TPSO[ob]])
                  if c < 31:
                      bt = c // 4
                      ue = UE4[bt % 2][:, (c % 4) * 128:(c % 4 + 1) * 128]
                      Tue = TL(f"ue4_{bt % 2}")
                      r3 = c % 3
                      n3 = (c + 1) % 3
                      if c == 0:
                          cp(Zf[n3][:], ue, [Tue], [TL(f"zf{n3}")])
                      else:
                          stt(Zf[n3][:], Zf[r3][:], SC[:, c:c + 1], ue, ALU.mult, ALU.add,
                              [TL(f"zf{r3}"), TL("sc"), Tue], [TL(f"zf{n3}")])
                      cp(Zb[n3][:], Zf[n3][:], [TL(f"zf{n3}")], [TL(f"zb{n3}")])
                  if c % 8 == 7:
                      tb = c // 8
                      act(OSQ[ob][:], PSO[ob][:], AF.Square, [TPSO[ob]], [TL("osq0")])
                      mm(PS4[:], [(ONESH[:], OSQ[ob][:])], [TL("onesh"), TL("osq0")], [TPS4])
                      act(RS[ob][:], PS4[:], AF.Ln, [TPS4, TL("epsr")], [TL("rs0")], bias=EPSR[:])
                      act(RS[ob][:], RS[ob][:], AF.Exp, [TL("rs0")], [TL("rs0")], scale=-0.5)
                      tt(OT[ob][:], PSO[ob][:], RS[ob][:], ALU.mult, [TPSO[ob], TL("rs0")], [TL("ot0")])
                      stt(OG.ap(h, tb), OT[ob][:], CO[:, C_GNW:C_GNW + 1], OG.ap(h, tb), ALU.mult, ALU.mult,
                          [TL("ot0"), TCO, OG.T[h][tb]], [OG.T[h][tb]])

              head_begin(0)
              for g in p_qfg_groups(0):
                  g()
              p_v(0)
              gating(0)
              for h in range(8):
                  prelude(h)
                  side = []
                  if h + 1 < 8:
                      head_begin(h + 1)
                      side = p_qfg_groups(h + 1)
                  si = 0
                  u_batch(h, 0)
                  if h == 0:
                      chk(301)
                  for c in range(32):
                      if c % 4 == 0 and 4 * (c // 4 + 1) < 31:
                          u_batch(h, c // 4 + 1)
                          if h == 0 and c == 0:
                              chk(302)
                      r_step(h, c)
                      if h == 0:
                          chk(310 + c)
                      want = (len(side) * (c + 1)) // 32
                      while si < want:
                          side[si]()
                          si += 1
                  while si < len(side):
                      side[si]()
                      si += 1
                  if h + 1 < 8:
                      p_v(h + 1)
                      gating(h + 1)

              S.barrier()
              chk(5)
              MRG = BUFB
              for s_ in range(4):
                  W, TW = wnext()
                  W3 = W[:].rearrange("p (k e) -> p k e", e=512)
                  for jj in range(2):
                      j = 2 * s_ + jj
                      co = jj * 256
                      for tb in range(NTB):
                          psA, TpsA = psnext()
                          mm(psA[:], [(W3[:, k, co:co + 128], OG.ap(k, tb)) for k in range(8)], [TW] + OG.col(tb), [TpsA])
                          psB, TpsB = psnext()
                          mm(psB[:], [(W3[:, k, co + 128:co + 256], XB.ap(k, tb)) for k in range(8)], [TW] + XB.col(tb), [TpsB])
                          b = tb % 2
                          act(TMP[b][:], psB[:], AF.Sigmoid, [TpsB, TCO], [TL(f"tmp{b}")], bias=bcol(6, j))
                          tt(MRG.ap(j, tb), psA[:], TMP[b][:], ALU.mult, [TpsA, TL(f"tmp{b}")], [MRG.T[j][tb]])

              S.barrier()
              chk(6)
              YC = BUFA
              S.op(S.dve, lambda e: e.memset(UP[0][:, 0:CW - 1], 0.0), writes=[TL("up0")])
              for s_ in range(4):
                  W, TW = wnext()
                  W3 = W[:].rearrange("p (k e) -> p k e", e=512)
                  for jj in range(2):
                      j = 2 * s_ + jj
                      co = jj * 256
                      ub = 0
                      for (ta, tb_, nm) in ((0, 16, "dga"), (16, CW, "dgb")):
                          nt = tb_ - ta
                          wcol = CO[:, C_WDW + j * CW + ta:C_WDW + j * CW + tb_]
                          tt(DG[ub][:, ta:tb_, :], IDF.unsqueeze(1).to_broadcast([128, nt, 128]),
                             wcol.unsqueeze(2).to_broadcast([128, nt, 128]), ALU.mult, [TL("consts"), TCO], [TL(nm)])
                      for tb in range(NTB):
                          psA, TpsA = psnext()
                          mm(psA[:], [(W3[:, k, co:co + 128], XB.ap(k, tb)) for k in range(8)], [TW] + XB.col(tb), [TpsA])
                          psB, TpsB = psnext()
                          mm(psB[:], [(W3[:, k, co + 128:co + 256], XB.ap(k, tb)) for k in range(8)], [TW] + XB.col(tb), [TpsB])
                          b = tb % 2
                          act(TMP[b][:], psB[:], AF.Sigmoid, [TpsB, TCO], [TL(f"tmp{b}")], bias=bcol(5, j))
                          stt(UP[ub][:, CW - 1 + tb * TB:CW - 1 + (tb + 1) * TB], psA[:], bcol(4, j), TMP[b][:], ALU.add, ALU.mult,
                              [TpsA, TCO, TL(f"tmp{b}")], [TL(f"up{ub}")])
                      for tb in range(NTB):
                          CB, TCB = ((PS4, TPS4), (PS5, TPS5))[tb % 2]
                          mm(CB[:], [(DG[ub][:, tap, :], UP[ub][:, tb * TB + tap:tb * TB + tap + TB]) for tap in range(CW)],
                             [TL("dga"), TL("dgb"), TL(f"up{ub}")], [TCB])
                          act(YC.ap(j, tb), CB[:], AF.Identity, [TCB, TCO], [YC.T[j][tb]], bias=CO[:, C_BDW + j:C_BDW + j + 1])
              ln_fm(YC, False, colsp, C_CLG, C_CLB, AF.Silu, YC)

              chk(7)
              for s_ in range(4):
                  W, TW = wnext()
                  W3 = W[:].rearrange("p (k e) -> p k e", e=512)
                  for jj in range(2):
                      j = 2 * s_ + jj
                      co = jj * 256
                      for tb in range(NTB):
                          psA, TpsA = psnext()
                          mm(psA[:], [(W3[:, k, co:co + 128], YC.ap(k, tb)) for k in range(8)], [TW] + YC.col(tb), [TpsA])
                          psB, TpsB = psnext()
                          mm(psB[:], [(W3[:, k, co + 128:co + 256], XB.ap(k, tb)) for k in range(8)], [TW] + XB.col(tb), [TpsB])
                          b = tb % 2
                          act(TMP[b][:], psB[:], AF.Sigmoid, [TpsB, TCO], [TL(f"tmp{b}")], bias=bcol(7, j))
                          stt(T1[b][:], psA[:], CO[:, C_BB + j:C_BB + j + 1], TMP[b][:], ALU.add, ALU.mult,
                              [TpsA, TCO, TL(f"tmp{b}")], [TL(f"t1{b}")])
                          tt(MRG.ap(j, tb), MRG.ap(j, tb), T1[b][:], ALU.add, [MRG.T[j][tb], TL(f"t1{b}")], [MRG.T[j][tb]])
              chk(8)
              for s_ in range(2):
                  W, TW = wnext()
                  W3 = W[:].rearrange("p (k e) -> p k e", e=512)
                  for jj in range(4):
                      j = 4 * s_ + jj
                      for tb in range(NTB):
                          ps, Tps = psnext()
                          mm(ps[:], [(W3[:, k, jj * 128:(jj + 1) * 128], MRG.ap(k, tb)) for k in range(8)], [TW] + MRG.col(tb), [Tps])
                          stt(X.ap(j, tb), X.ap(j, tb), ALPHA, ps[:], ALU.mult, ALU.add, [X.T[j][tb], Tps], [X.T[j][tb]])
                          act(XB.ap(j, tb), X.ap(j, tb), AF.Copy, [X.T[j][tb]], [XB.T[j][tb]])
              ln_fm(X, True, colsp, C_L1G, C_L1B, AF.Identity, X, XB)

              S.barrier()
              chk(9)
              THID = [[Tile() for _ in range(2)] for _ in range(NFC)]
              for half in range(2):
                  for s_ in range(11):
                      W, TW = wnext()
                      W3 = W[:].rearrange("p (k e) -> p k e", e=512)
                      for jj in range(2):
                          j = 2 * s_ + jj
                          co = jj * 256
                          for t2 in range(2):
                              tb = half * 2 + t2
                              psA, TpsA = psnext()
                              mm(psA[:], [(W3[:, k, co:co + 128], XB.ap(k, tb)) for k in range(8)], [TW] + XB.col(tb), [TpsA])
                              psB, TpsB = psnext()
                              mm(psB[:], [(W3[:, k, co + 128:co + 256], XB.ap(k, tb)) for k in range(8)], [TW] + XB.col(tb), [TpsB])
                              b = t2
                              act(TMP[b][:], psA[:], AF.Silu, [TpsA], [TL(f"tmp{b}")])
                              tt(HID_t[:, j, t2 * TB:(t2 + 1) * TB], psB[:], TMP[b][:], ALU.mult, [TpsB, TL(f"tmp{b}")], [THID[j][t2]])
                  for s_ in range(8):
                      W, TW = wnext()
                      W3 = W[:, 0:NFC * 128].rearrange("p (k e) -> p k e", e=128)
                      for t2 in range(2):
                          tb = half * 2 + t2
                          ps, Tps = psnext()
                          mm(ps[:], [(W3[:, k, :], HID_t[:, k, t2 * TB:(t2 + 1) * TB]) for k in range(NFC)],
                             [TW] + [THID[k][t2] for k in range(NFC)], [Tps])
                          stt(X.ap(s_, tb), X.ap(s_, tb), ALPHA, ps[:], ALU.mult, ALU.add, [X.T[s_][tb], Tps], [X.T[s_][tb]])
                          act(XB.ap(s_, tb), X.ap(s_, tb), AF.Copy, [X.T[s_][tb]], [XB.T[s_][tb]])
              ln_fm(X, True, colsp, C_L2G, C_L2B, AF.Identity, X, XB)


          except _Stop:
              break
        S.barrier()
        last = None
        for tt_i in range(16):
            b = tt_i % 2
            t0 = tt_i * 128
            tb = tt_i // 4
            for half in range(2):
                ps, Tps = psnext()
                for jj in range(4):
                    j = half * 4 + jj
                    tr(ps[:, jj * 128:(jj + 1) * 128], X_t[:, j, t0:t0 + 128], IDF, [X.T[j][tb], TL("consts")], [Tps])
                act(XN[b][:, half * 512:(half + 1) * 512], ps[:], AF.Copy, [Tps], [TXN[b]])
            last = S.op(S.sp, lambda e, b=b, t0=t0: e.dma_start(out=out_d[t0:t0 + 128, :], in_=XN[b][:]), reads=[TXN[b]], dma=st_o)
        S.emit(final_waits=[last])
    return nc


def _strip(W, colsets):
    K = W.shape[0]
    out = []
    for cols in colsets:
        Ws = W[:, cols]
        n = Ws.shape[1]
        out.append(Ws.reshape(K // 128, 128, n).transpose(1, 0, 2).reshape(128, (K // 128) * n))
    return np.ascontiguousarray(np.stack(out, 0))


def _fm(v):
    return np.ascontiguousarray(v.reshape(-1, 128).T)


def prep(inputs, depth=DEPTH):
    f = lambda a: np.asarray(a, dtype=np.float32)
    w_in, b_in = f(inputs["w_in"]), f(inputs["b_in"])
    w_a, w_b, w_o = f(inputs["w_a"]), f(inputs["w_b"]), f(inputs["w_o"])
    w_up, w_dn, w_dw = f(inputs["w_up"]), f(inputs["w_down"]), f(inputs["w_dw"])
    ar = np.arange(128)
    consts = np.zeros((128, NCONST), np.float32)
    consts[:, K_ID:K_ID + 128] = np.eye(128, dtype=np.float32)
    p = np.arange(128)[:, None] % 64
    t = np.arange(64)[None, :]
    consts[:, K_MASK:K_MASK + 64] = (p <= t).astype(np.float32)
    ln0 = np.concatenate([_fm(f(inputs["ln0_g"])), _fm(f(inputs["ln0_b"]))], axis=1)
    lbl = np.concatenate([_fm(f(inputs["lb_logits"])[l]) for l in range(DEPTH)], axis=1)
    cols = np.zeros((DEPTH, 128, NCOL), np.float32)
    bibc = np.zeros((DEPTH, 128, D), np.float32)
    wih, wic, wma, wmb, wo, wup, wdn = [], [], [], [], [], [], []
    for l in range(DEPTH):
        for sec in range(8):
            cols[l, :, C_BIN + sec * 8:C_BIN + sec * 8 + 8] = _fm(b_in[l, sec * D:(sec + 1) * D])
        cols[l, :, C_GNW] = f(inputs["g_norm_w"])[l]
        cols[l, :, C_WDW:C_WDW + 8 * CW] = w_dw[l].T.reshape(8, 128, CW).transpose(1, 0, 2).reshape(128, 8 * CW)
        for nm, c0 in (("b_dw", C_BDW), ("conv_ln_g", C_CLG), ("conv_ln_b", C_CLB), ("b_b", C_BB),
                       ("ln1_g", C_L1G), ("ln1_b", C_L1B), ("ln2_g", C_L2G), ("ln2_b", C_L2B)):
            cols[l, :, c0:c0 + 8] = _fm(f(inputs[nm])[l])
        bibc[l] = np.broadcast_to(b_in[l, 2 * D:3 * D][None, :], (128, D))
        if l >= depth:
            continue
        wih.append(_strip(w_in[l], [np.concatenate([sec * D + h * 128 + ar for sec in range(4)]) for h in range(8)]))
        wic.append(_strip(w_in[l], [np.concatenate([4 * D + j * 128 + ar, 5 * D + j * 128 + ar,
                                                    4 * D + (j + 1) * 128 + ar, 5 * D + (j + 1) * 128 + ar])
                                    for j in range(0, 8, 2)]))
        cat_a = np.concatenate([w_a[l], w_in[l][:, 6 * D:7 * D]], axis=1)
        cat_b = np.concatenate([w_b[l], w_in[l][:, 7 * D:8 * D]], axis=1)
        sets = [np.concatenate([j * 128 + ar, D + j * 128 + ar, (j + 1) * 128 + ar, D + (j + 1) * 128 + ar])
                for j in range(0, 8, 2)]
        wma.append(_strip(cat_a, sets))
        wmb.append(_strip(cat_b, sets))
        wo.append(_strip(w_o[l], [np.arange(s * 512, (s + 1) * 512) for s in range(2)]))
        wup.append(_strip(w_up[l], [np.concatenate([j * 128 + ar, FH + j * 128 + ar, (j + 1) * 128 + ar, FH + (j + 1) * 128 + ar])
                                    for j in range(0, NFC, 2)]))
        wdn.append(_strip(w_dn[l], [np.arange(s * 128, (s + 1) * 128) for s in range(8)]))

    def stk(lst, shape):
        a = np.zeros((DEPTH,) + shape, np.float32)
        for i, v in enumerate(lst):
            a[i] = v
        return a
    return {
        "consts": consts, "ln0": np.ascontiguousarray(ln0), "lbl": np.ascontiguousarray(lbl), "cols": cols, "bibc": bibc,
        "wih": stk(wih, (8, 128, 4096)), "wic": stk(wic, (4, 128, 4096)), "wma": stk(wma, (4, 128, 4096)),
        "wmb": stk(wmb, (4, 128, 4096)), "wo": stk(wo, (2, 128, 4096)), "wup": stk(wup, (11, 128, 4096)),
        "wdn": stk(wdn, (8, 128, NFC * 128)),
    }


_NC_CACHE = {}


def kernel(**inputs):
    x = np.asarray(inputs["x"], dtype=np.float32)
    shared = prep(inputs)
    if "nc" not in _NC_CACHE:
        _NC_CACHE["nc"] = build(DEPTH)
    nc = _NC_CACHE["nc"]
    in_maps = []
    for b in range(8):
        m = dict(shared)
        m["x"] = np.ascontiguousarray(x[b])
        in_maps.append(m)
    res = run_bass_kernel_spmd(nc, in_maps, core_ids=list(range(8)))
    return np.stack([np.asarray(r["out"], dtype=np.float32) for r in res.results], axis=0)
```

```python
import numpy as np
from contextlib import ExitStack
import concourse.bass as bass
import concourse.mybir as mybir
from concourse.bass_utils import run_bass_kernel_spmd

F32 = mybir.dt.float32
BF16 = mybir.dt.bfloat16
AF = mybir.ActivationFunctionType
ALU = mybir.AluOpType
AX = mybir.AxisListType

D = 1024
T = 2048
DEPTH = 4
NCH = 8
TB = 512
NTB = 4
FH = 2816
NFC = 22
CW = 31
ALPHA = (2 * DEPTH) ** 0.25
LN_EPS = 1e-5
RMS_EPS = 1e-6
F_MIN = 1e-30

C_BIN = 0
C_GNW = 64
C_WDW = 65
C_BDW = C_WDW + 8 * CW
C_CLG = C_BDW + 8
C_CLB = C_CLG + 8
C_BB = C_CLB + 8
C_L1G = C_BB + 8
C_L1B = C_L1G + 8
C_L2G = C_L1B + 8
C_L2B = C_L2G + 8
NCOL = C_L2B + 8
K_ID = 0
K_MASK = 128
K_RESET = 192
NCONST = 192
NWB = 2


class Tile:
    __slots__ = ("w", "r")

    def __init__(self):
        self.w = None
        self.r = {}


class Eng:
    def __init__(self, S, name):
        self.S = S
        self.name = name
        self.ops = []
        self.sem = None
        self.cnt = 0
        self.waited = {}

    def rotate(self):
        self.sem = self.S.new_sem(self.name)
        self.cnt = 0


class DmaStream:
    def __init__(self, S, name):
        self.sem = S.new_sem(name)
        self.cnt = 0


class Sched:
    def __init__(self, nc, stack):
        self.nc = nc
        self.stack = stack
        self.nsem = 0
        self.pe = Eng(self, "pe")
        self.act = Eng(self, "act")
        self.dve = Eng(self, "dve")
        self.pool = Eng(self, "pool")
        self.sp = Eng(self, "sp")
        self.engs = [self.pe, self.act, self.dve, self.pool, self.sp]
        for e in self.engs:
            e.rotate()

    def new_sem(self, name):
        self.nsem += 1
        return self.stack.enter_context(self.nc.semaphore(f"s{self.nsem}_{name}"))

    def rotate_all(self):
        for e in self.engs:
            e.rotate()

    def op(self, eng, fn, reads=(), writes=(), dma=None):
        need = {}
        for t in reads:
            if t.w is not None:
                s, v = t.w
                if need.get(s, 0) < v:
                    need[s] = v
        for t in writes:
            if t.w is not None:
                s, v = t.w
                if need.get(s, 0) < v:
                    need[s] = v
            for s, v in t.r.items():
                if need.get(s, 0) < v:
                    need[s] = v
        waits = []
        for s, v in need.items():
            if eng is self.pe and s is eng.sem:
                continue
            if eng.waited.get(s, 0) < v:
                eng.waited[s] = v
                waits.append((s, v))
        if dma is not None:
            dma.cnt += 16
            point = (dma.sem, dma.cnt)
            eng.ops.append((waits, fn, dma.sem, 16))
        else:
            eng.cnt += 1
            point = (eng.sem, eng.cnt)
            eng.ops.append((waits, fn, eng.sem, 1))
        s, v = point
        for t in reads:
            if t.r.get(s, 0) < v:
                t.r[s] = v
        for t in writes:
            t.w = point
            t.r = {}
        return point

    def barrier(self):
        comp = [self.pe, self.act, self.dve, self.pool]
        for e in comp:
            waits = []
            for f in comp:
                if f is e or f.cnt == 0:
                    continue
                if e.waited.get(f.sem, 0) < f.cnt:
                    e.waited[f.sem] = f.cnt
                    waits.append((f.sem, f.cnt))
            if waits:
                e.ops.append((waits, None, None, 0))

    def emit(self, final_waits=()):
        def replay(eng, hw):
            for waits, fn, sem, inc in eng.ops:
                for s, v in waits:
                    hw.wait_ge(s, v)
                if fn is None:
                    continue
                ins = fn(hw)
                if sem is not None:
                    ins.then_inc(sem, inc)

        with self.nc.Block() as block:
            @block.tensor
            def _(e):
                replay(self.pe, e)

            @block.scalar
            def _(e):
                replay(self.act, e)

            @block.vector
            def _(e):
                replay(self.dve, e)

            @block.gpsimd
            def _(e):
                replay(self.pool, e)

            @block.sync
            def _(e):
                replay(self.sp, e)
                for s, v in final_waits:
                    e.wait_ge(s, v)


class Buf3:
    def __init__(self, ap3, nj, ntok=T):
        self.t = ap3
        self.nj = nj
        self.T = [[Tile() for _ in range(ntok // TB)] for _ in range(nj)]

    def ap(self, j, tb):
        return self.t[:, j, tb * TB:(tb + 1) * TB]

    def col(self, tb):
        return [self.T[j][tb] for j in range(self.nj)]


class _Stop(Exception):
    pass


def build(depth=DEPTH, dbg=False, stop=None):
    def chk(n):
        if stop == n:
            raise _Stop()
    nc = bass.Bass("TRN2", target_bir_lowering=False)

    def din(name, shape):
        return nc.dram_tensor(name, list(shape), F32, kind="ExternalInput").ap()

    x_d = din("x", [T, D])
    consts_d = din("consts", [128, NCONST])
    ln0_d = din("ln0", [128, 16])
    lbl_d = din("lbl", [128, DEPTH * 8])
    cols_d = din("cols", [DEPTH, 128, NCOL])
    bibc_d = din("bibc", [DEPTH, 128, D])
    wih_d = din("wih", [DEPTH, 8, 128, 4096])
    wic_d = din("wic", [DEPTH, 4, 128, 4096])
    wma_d = din("wma", [DEPTH, 4, 128, 4096])
    wmb_d = din("wmb", [DEPTH, 4, 128, 4096])
    wo_d = din("wo", [DEPTH, 2, 128, 4096])
    wup_d = din("wup", [DEPTH, 11, 128, 4096])
    wdn_d = din("wdn", [DEPTH, 8, 128, NFC * 128])
    out_d = nc.dram_tensor("out", [T, D], F32, kind="ExternalOutput").ap()
    dbg_d = {}

    with ExitStack() as st:
        S = Sched(nc, st)

        def sb(name, shape, dt=F32):
            return st.enter_context(nc.sbuf_tensor(name, list(shape), dt))

        def pst(name, shape, dt=F32):
            return st.enter_context(nc.psum_tensor(name, list(shape), dt))

        DD = sb("DDEC", [128, 64])
        SC = sb("SCAL", [128, 64])
        X_t = sb("X", [128, NCH, T])
        XB_t = sb("XB", [128, NCH, T], BF16)
        BAB = sb("BAB", [128, 32768], BF16)
        X = Buf3(X_t, NCH)
        XB = Buf3(XB_t, NCH)
        BUFA = Buf3(BAB[:, 0:16384].rearrange("p (j t) -> p j t", t=T), NCH)
        BUFB = Buf3(BAB[:, 16384:32768].rearrange("p (j t) -> p j t", t=T), NCH)
        HID_t = BAB[:, 0:NFC * 1024].rearrange("p (j t) -> p j t", t=1024)

        def f32row(i):
            return BAB[:, 16384 + 4096 * i:16384 + 4096 * (i + 1)].bitcast(F32)
        TQ, TSG, TK, T4 = f32row(0), f32row(1), f32row(2), f32row(3)
        SCR = sb("SCR", [128, 10240], BF16)

        def scr(a, n, dt=BF16):
            v = SCR[:, a:a + n]
            return v.bitcast(F32) if dt == F32 else v
        QT = scr(0, 2048)
        KT = scr(2048, 2048)
        KTM = scr(4096, 2048).rearrange("p (a b) -> p a b", b=128)
        VTM = scr(6144, 2048).rearrange("p (a b) -> p a b", b=128)
        PM = scr(8192, 1024).rearrange("p (a b) -> p a b", b=64)
        OT = [TK[:, 0:512]] * 2
        RS = [TK[:, 512:1024]] * 2
        OSQ = [TK[:, 1024:1280].bitcast(BF16)] * 2
        UP = [scr(0, 2080)] * 2
        DG = [scr(2080, CW * 128).rearrange("p (a b) -> p a b", b=128)] * 2
        TMP = [scr(6048, 1024, F32), scr(7072, 1024, F32)]
        SQ = scr(6048, 2048).rearrange("p (a b) -> p a b", b=TB)
        T1 = [scr(8096, 1024, F32), scr(9120, 1024, F32)]
        T1P = scr(8096, 2048, F32).rearrange("p (a b) -> p a b", b=TB)
        TEP = TSG
        Zf = [sb(f"Zf{i}", [128, 128]) for i in range(3)]
        Zb = [sb(f"Zb{i}", [128, 128], BF16) for i in range(3)]
        UE4 = [TK[:, 1280:1792], sb("UE4B", [128, 512])]
        WB = [sb(f"WB{i}", [128, 4096], BF16) for i in range(NWB)]
        CONSTS = sb("CONSTS", [128, NCONST])
        IDB = sb("IDB", [128, 128], BF16)
        ONESB = sb("ONESB", [128, 128], BF16)
        ONESH = sb("ONESH", [128, 128], BF16)
        EPSL = sb("EPSL", [128, 1])
        EPSR = sb("EPSR", [128, 1])
        ONEC = sb("ONEC", [128, 1])
        COLS = [sb("COLS0", [128, NCOL])] * 2
        BIBC = [sb(f"BIBC{i}", [128, 128]) for i in range(2)]
        LN0 = sb("LN0", [128, 16])
        LBL = sb("LBL", [128, DEPTH, 8])
        LBE = sb("LBE", [128, DEPTH, 8])
        LB = sb("LB", [128, DEPTH, 8])
        OML = sb("OML", [128, DEPTH, 8])
        NOML = sb("NOML", [128, DEPTH, 8])
        THR = sb("THR", [128, DEPTH, 8])
        LBT = sb("LBT", [128, 8])

        def bab32(a, n):
            return BAB[:, a:a + 2 * n].bitcast(F32)
        XIN = [bab32(0, D), bab32(2048, D)]
        XN = [bab32(4096, D), bab32(6144, D)]
        JUNK = BAB[:, 8192:8192 + D]
        SS = [sb(f"SS{i}", [128, 8]) for i in range(2)]

        IDF = CONSTS[:, K_ID:K_ID + 128]
        MASK2 = CONSTS[:, K_MASK:K_MASK + 64]

        PS = [pst(f"PS{i}", [128, TB]) for i in range(3)]
        PSTR = pst("PSTR", [128, 1024], BF16)
        PSTRF = PSTR[:, 0:1024].bitcast(F32)
        PS4 = pst("PS4", [128, TB])
        PS5 = pst("PS5", [128, TB])
        PS6 = pst("PS6", [128, TB])
        PS7 = pst("PS7", [128, TB])
        TPS = [Tile() for _ in range(3)]
        TPSTR, TPS4, TPS5, TPS6, TPS7 = Tile(), Tile(), Tile(), Tile(), Tile()
        TU = [Tile() for _ in range(4)]
        ps_rr = [0]

        def psnext():
            i = ps_rr[0] % 3
            ps_rr[0] += 1
            return PS[i], TPS[i]

        tl = {}

        def TL(name):
            if name not in tl:
                tl[name] = Tile()
            return tl[name]

        def act(out, in_, func, reads, writes, bias=None, scale=None, accum_out=None):
            kw = {}
            if bias is not None:
                kw["bias"] = bias
            if scale is not None:
                kw["scale"] = scale
            if accum_out is not None:
                kw["accum_out"] = accum_out
            return S.op(S.act, lambda e: e.activation(out=out, in_=in_, func=func, **kw), reads, writes)

        def tt(out, in0, in1, op, reads, writes, eng=None):
            eng = eng or S.dve
            return S.op(eng, lambda e: e.tensor_tensor(out=out, in0=in0, in1=in1, op=op), reads, writes)

        def ts(out, in0, s1, s2, op0, op1, reads, writes, eng=None):
            eng = eng or S.dve
            if op1 is None:
                return S.op(eng, lambda e: e.tensor_scalar(out=out, in0=in0, scalar1=s1, scalar2=None, op0=op0), reads, writes)
            return S.op(eng, lambda e: e.tensor_scalar(out=out, in0=in0, scalar1=s1, scalar2=s2, op0=op0, op1=op1), reads, writes)

        def stt(out, in0, scalar, in1, op0, op1, reads, writes):
            return S.op(S.dve, lambda e: e.scalar_tensor_tensor(out=out, in0=in0, scalar=scalar, in1=in1, op0=op0, op1=op1), reads, writes)

        def cp(out, in_, reads, writes, eng=None):
            eng = eng or S.dve
            return S.op(eng, lambda e: e.tensor_copy(out=out, in_=in_), reads, writes)

        def mm(out, pairs, reads, writes, first=True, last=True):
            def fn(e):
                n = len(pairs)
                ins = None
                for i, (l, r) in enumerate(pairs):
                    ins = e.matmul(out, lhsT=l, rhs=r, start=(first and i == 0), stop=(last and i == n - 1))
                return ins
            return S.op(S.pe, fn, reads, writes)

        def tr(out, in_, ident, reads, writes):
            return S.op(S.pe, lambda e: e.transpose(out=out, in_=in_, identity=ident), reads, writes)

        ld_misc = DmaStream(S, "ldm")
        ld_x = DmaStream(S, "ldx")
        ld_w = DmaStream(S, "ldw")
        st_o = DmaStream(S, "sto")

        wlist = []
        for l in range(depth):
            for s_ in range(8):
                wlist.append((wih_d[l, s_], 4096))
            for s_ in range(4):
                wlist.append((wma_d[l, s_], 4096))
            for s_ in range(4):
                wlist.append((wic_d[l, s_], 4096))
            for s_ in range(4):
                wlist.append((wmb_d[l, s_], 4096))
            for s_ in range(2):
                wlist.append((wo_d[l, s_], 4096))
            for half in range(2):
                for s_ in range(11):
                    wlist.append((wup_d[l, s_], 4096))
                for s_ in range(8):
                    wlist.append((wdn_d[l, s_], NFC * 128))
        TWB = [Tile() for _ in range(NWB)]
        wstate = {"issued": 0, "used": 0}

        def w_issue():
            i = wstate["issued"]
            if i >= len(wlist):
                return
            src, n = wlist[i]
            b = i % NWB
            S.op(S.pool, lambda e: e.dma_start(out=WB[b][:, 0:n], in_=src), reads=[], writes=[TWB[b]], dma=ld_w)
            wstate["issued"] += 1

        def wnext():
            i = wstate["used"]
            while wstate["issued"] < min(i + NWB, len(wlist)):
                w_issue()
            wstate["used"] += 1
            b = i % NWB
            return WB[b], TWB[b]

        S.op(S.sp, lambda e: e.dma_start(out=CONSTS[:], in_=consts_d), writes=[TL("consts")], dma=ld_misc)
        S.op(S.sp, lambda e: e.dma_start(out=LN0[:], in_=ln0_d), writes=[TL("ln0")], dma=ld_misc)
        S.op(S.sp, lambda e: e.dma_start(out=LBL[:].rearrange("p a b -> p (a b)"), in_=lbl_d), writes=[TL("lbl")], dma=ld_misc)
        cp(IDB[:], IDF, [TL("consts")], [TL("idb")])
        S.op(S.dve, lambda e: e.memset(ONESB[:], 1.0 / D), writes=[TL("onesb")])
        S.op(S.dve, lambda e: e.memset(ONESH[:], 1.0 / 128), writes=[TL("onesh")])
        S.op(S.dve, lambda e: e.memset(EPSL[:], LN_EPS), writes=[TL("epsl")])
        S.op(S.dve, lambda e: e.memset(EPSR[:], RMS_EPS), writes=[TL("epsr")])
        S.op(S.dve, lambda e: e.memset(ONEC[:], 1.0), writes=[TL("onec")])

        Tlb = TL("lb")
        cp(LBT[:], LBL[:, 0, :], [TL("lbl")], [TL("lbt")])
        for l in range(1, DEPTH):
            tt(LBT[:], LBT[:], LBL[:, l, :], ALU.max, [TL("lbl"), TL("lbt")], [TL("lbt")])
        for l in range(DEPTH):
            tt(LBE[:, l, :], LBL[:, l, :], LBT[:], ALU.subtract, [TL("lbl"), TL("lbt")], [TL("lbe")])
        act(LBE[:].rearrange("p a b -> p (a b)"), LBE[:].rearrange("p a b -> p (a b)"), AF.Exp, [TL("lbe")], [TL("lbe")])
        cp(LBT[:], LBE[:, 0, :], [TL("lbe")], [TL("lbt")])
        for l in range(1, DEPTH):
            tt(LBT[:], LBT[:], LBE[:, l, :], ALU.add, [TL("lbe"), TL("lbt")], [TL("lbt")])
        S.op(S.dve, lambda e: e.reciprocal(out=LBT[:], in_=LBT[:]), [TL("lbt")], [TL("lbt")])
        for l in range(DEPTH):
            tt(LBE[:, l, :], LBE[:, l, :], LBT[:], ALU.mult, [TL("lbe"), TL("lbt")], [TL("lbe")])
        S.op(S.dve, lambda e: e.memset(LB[:, 0, :], 0.0), writes=[Tlb])
        for l in range(1, DEPTH):
            tt(LB[:, l, :], LB[:, l - 1, :], LBE[:, l, :], ALU.add, [TL("lbe"), Tlb], [Tlb])
        LBf = LB[:].rearrange("p a b -> p (a b)")
        ts(OML[:].rearrange("p a b -> p (a b)"), LBf, -1.0, 1.0, ALU.mult, ALU.add, [Tlb], [Tlb])
        ts(NOML[:].rearrange("p a b -> p (a b)"), LBf, 1.0, -1.0, ALU.mult, ALU.add, [Tlb], [Tlb])
        ts(THR[:].rearrange("p a b -> p (a b)"), LBf, -1.0, F_MIN, ALU.mult, ALU.add, [Tlb], [Tlb])

        TCOLS = [Tile()] * 2

        def load_params(l):
            b = l % 2
            S.op(S.sp, lambda e: e.dma_start(out=COLS[b][:], in_=cols_d[l]), writes=[TCOLS[b]], dma=ld_misc)

        load_params(0)

        TXIN = [Tile(), Tile()]
        TXN = [Tile(), Tile()]
        TSS = [Tile(), Tile()]
        for tt_i in range(16):
            b = tt_i % 2
            t0 = tt_i * 128
            tb = tt_i // 4
            S.op(S.sp, lambda e, b=b, t0=t0: e.dma_start(out=XIN[b][:], in_=x_d[t0:t0 + 128, :]), writes=[TXIN[b]], dma=ld_x)
            S.op(S.dve, lambda e, b=b: e.reduce_sum(out=SS[b][:, 0:1], in_=XIN[b][:], axis=AX.X), [TXIN[b]], [TSS[b]])
            act(JUNK[:], XIN[b][:], AF.Square, [TXIN[b]], [TL("junk"), TSS[b]], accum_out=SS[b][:, 1:2])
            ts(SS[b][:, 2:4], SS[b][:, 0:2], 1.0 / D, None, ALU.mult, None, [TSS[b]], [TSS[b]])
            tt(SS[b][:, 4:5], SS[b][:, 2:3], SS[b][:, 2:3], ALU.mult, [TSS[b]], [TSS[b]])
            tt(SS[b][:, 5:6], SS[b][:, 3:4], SS[b][:, 4:5], ALU.subtract, [TSS[b]], [TSS[b]])
            act(SS[b][:, 6:7], SS[b][:, 5:6], AF.Ln, [TSS[b], TL("epsl")], [TSS[b]], bias=EPSL[:])
            act(SS[b][:, 7:8], SS[b][:, 6:7], AF.Exp, [TSS[b]], [TSS[b]], scale=-0.5)
            ts(XN[b][:], XIN[b][:], SS[b][:, 2:3], SS[b][:, 7:8], ALU.subtract, ALU.mult, [TXIN[b], TSS[b]], [TXN[b]])
            for half in range(2):
                ps, Tps = psnext()
                for jj in range(4):
                    j = half * 4 + jj
                    tr(ps[:, jj * 128:(jj + 1) * 128], XN[b][:, j * 128:(j + 1) * 128], IDF, [TXN[b], TL("consts")], [Tps])
                for jj in range(4):
                    j = half * 4 + jj
                    act(X_t[:, j, t0:t0 + 128], ps[:, jj * 128:(jj + 1) * 128], AF.Identity, [Tps, TL("ln0")], [X.T[j][tb]],
                        scale=LN0[:, j:j + 1], bias=LN0[:, 8 + j:9 + j])
            cp(XB_t[:, :, t0:t0 + 128], X_t[:, :, t0:t0 + 128], X.col(tb), XB.col(tb))

        def ln_fm(src, src_f32, cols, gcol, bcol, func, dst_main, dst_bf=None, eps=None):
            banks = [(PS6, TPS6, PS7, TPS7), (PS4, TPS4, PS5, TPS5)]

            def stats_a(tb):
                PMn, TPMn, PVr, TPVr = banks[tb % 2]
                sl = slice(tb * TB, (tb + 1) * TB)
                sbuf = XB if src_f32 else src
                mm(PMn[:], [(ONESB[:], sbuf.ap(j, tb)) for j in range(NCH)], sbuf.col(tb) + [TL("onesb")], [TPMn])
                for hf in range(2):
                    act(SQ[:], sbuf.t[:, 4 * hf:4 * hf + 4, sl], AF.Square, sbuf.col(tb), [TL("tmp0"), TL("tmp1")])
                    mm(PVr[:], [(ONESB[:], SQ[:, j, :]) for j in range(4)], [TL("tmp0"), TL("tmp1"), TL("onesb")], [TPVr],
                       first=(hf == 0), last=(hf == 1))
                act(TMP[0][:], PMn[:], AF.Square, [TPMn], [TL("tmp0")])

            def stats_b(tb):
                PMn, TPMn, PVr, TPVr = banks[tb % 2]
                tt(TMP[1][:], PVr[:], TMP[0][:], ALU.subtract, [TPVr, TL("tmp0")], [TL("tmp1")])
                act(TMP[1][:], TMP[1][:], AF.Ln, [TL("tmp1"), TL("epsl")], [TL("tmp1")], bias=EPSL[:])
                act(PVr[:], TMP[1][:], AF.Exp, [TL("tmp1")], [TPVr], scale=-0.5)

            def apply_dve(tb):
                PMn, TPMn, PVr, TPVr = banks[tb % 2]
                sl = slice(tb * TB, (tb + 1) * TB)
                tt(src.t[:, :, sl], src.t[:, :, sl], PMn[:].unsqueeze(1).to_broadcast([128, NCH, TB]), ALU.subtract,
                   src.col(tb) + [TPMn], src.col(tb))
                tt(src.t[:, :, sl], src.t[:, :, sl], PVr[:].unsqueeze(1).to_broadcast([128, NCH, TB]), ALU.mult,
                   src.col(tb) + [TPVr], src.col(tb))

            def apply_act(tb):
                for j in range(NCH):
                    act(dst_main.ap(j, tb), src.ap(j, tb), func, [src.T[j][tb], cols[1]], [dst_main.T[j][tb]],
                        scale=cols[0][:, gcol + j:gcol + j + 1], bias=cols[0][:, bcol + j:bcol + j + 1])
                if dst_bf is not None:
                    sl = slice(tb * TB, (tb + 1) * TB)
                    cp(dst_bf.t[:, :, sl], dst_main.t[:, :, sl], dst_main.col(tb), dst_bf.col(tb))

            def apply_pairs(tb):
                PMn, TPMn, PVr, TPVr = banks[tb % 2]
                sl = slice(tb * TB, (tb + 1) * TB)
                for pr in range(NCH // 2):
                    tt(T1P, src.t[:, 2 * pr:2 * pr + 2, sl], PMn[:].unsqueeze(1).to_broadcast([128, 2, TB]), ALU.subtract,
                       [src.T[2 * pr][tb], src.T[2 * pr + 1][tb], TPMn], [TL("t10"), TL("t11")])
                    tt(T1P, T1P, PVr[:].unsqueeze(1).to_broadcast([128, 2, TB]), ALU.mult,
                       [TL("t10"), TL("t11"), TPVr], [TL("t10"), TL("t11")])
                    for jj in range(2):
                        j = 2 * pr + jj
                        act(dst_main.ap(j, tb), T1P[:, jj, :], func, [TL(f"t1{jj}"), cols[1]], [dst_main.T[j][tb]],
                            scale=cols[0][:, gcol + j:gcol + j + 1], bias=cols[0][:, bcol + j:bcol + j + 1])

            stats_a(0)
            stats_b(0)
            for tb in range(NTB):
                if tb + 1 < NTB:
                    stats_a(tb + 1)
                if src_f32:
                    apply_dve(tb)
                    if tb + 1 < NTB:
                        stats_b(tb + 1)
                    apply_act(tb)
                else:
                    apply_pairs(tb)
                    if tb + 1 < NTB:
                        stats_b(tb + 1)

        TQt = [Tile() for _ in range(NTB)]
        TSGt = [Tile() for _ in range(NTB)]
        TVt = [Tile() for _ in range(NTB)]
        TPSO = [TPS6, TPS7]
        PSO = [PS6, PS7]

        for l in range(depth):
          try:
              if l > 0:
                  S.rotate_all()
              CO = COLS[l % 2]
              TCO = TCOLS[l % 2]
              colsp = (CO, TCO)

              def bcol(sec, c):
                  return CO[:, C_BIN + sec * 8 + c:C_BIN + sec * 8 + c + 1]

              S.barrier()
              if l > 0:
                  load_params(l)
              chk(0)
              OG = BUFA
              hw_ = {}

              def head_begin(h):
                  W, TW = wnext()
                  BI = BIBC[h % 2]
                  TBI = TL(f"bibc{h % 2}")
                  S.op(S.sp, lambda e, BI=BI, l=l, h=h: e.dma_start(out=BI[:], in_=bibc_d[l][:, h * 128:(h + 1) * 128]),
                       writes=[TBI], dma=ld_misc)
                  hw_[h] = (W[:].rearrange("p (k e) -> p k e", e=512), TW, BI, TBI)

              def p_qfg_groups(h):
                  W3, TW, BI, TBI = hw_[h]
                  TGG = OG.t[:, h, :]
                  groups = []
                  for (c0, dst, dT, fn_, sec) in ((0, TQ, TQt, AF.Silu, 0), (384, TGG, OG.T[h], AF.Silu, 3),
                                                  (128, TSG, TSGt, AF.Sigmoid, 1)):
                      for tb in range(NTB):
                          sl = slice(tb * TB, (tb + 1) * TB)

                          def g(tb=tb, sl=sl, c0=c0, dst=dst, dT=dT, fn_=fn_, sec=sec):
                              ps, Tps = psnext()
                              mm(ps[:], [(W3[:, k, c0:c0 + 128], XB.ap(k, tb)) for k in range(8)], [TW] + XB.col(tb), [Tps])
                              act(dst[:, sl], ps[:], fn_, [Tps, TCO], [dT[tb]], bias=bcol(sec, h))
                          groups.append(g)
                  return groups

              def p_v(h):
                  W3, TW, BI, TBI = hw_[h]
                  for tb in range(NTB):
                      ps, Tps = psnext()
                      for t4 in range(4):
                          t0 = tb * TB + t4 * 128
                          mm(ps[:, t4 * 128:(t4 + 1) * 128], [(XB_t[:, k, t0:t0 + 128], W3[:, k, 256:384]) for k in range(8)],
                             [TW] + XB.col(tb), [Tps])
                      tt(VTM[:, tb * 4:(tb + 1) * 4, :], ps[:].rearrange("p (a b) -> p a b", b=128),
                         BI[:].unsqueeze(1).to_broadcast([128, 4, 128]), ALU.add,
                         [Tps, TBI], [TVt[tb]])

              def gating(h):
                  lbc = LB[:, l, h:h + 1]
                  omlc = OML[:, l, h:h + 1]
                  nomlc = NOML[:, l, h:h + 1]
                  thrc = THR[:, l, h:h + 1]
                  ts(T4[:], TSG[:], omlc, thrc, ALU.mult, ALU.max, TSGt + [Tlb], [TL("t4")])
                  act(T4[:], T4[:], AF.Ln, [TL("t4"), Tlb], [TL("t4")], bias=lbc)
                  ts(TK[:], TSG[:], nomlc, omlc, ALU.mult, ALU.add, TSGt + [Tlb], [TL("tk"), TL("ot0"), TL("rs0"), TL("osq0"), TL("ue4_0")])
                  S.op(S.dve, lambda e: e.tensor_tensor_scan(out=TSG[:], data0=T4[:], data1=T4[:], initial=0.0,
                                                             op0=ALU.add, op1=ALU.add),
                       [TL("t4")] + TSGt + [TL("tk")], TSGt)
                  G3 = TSG[:].rearrange("p (c s) -> p c s", s=64)
                  A3 = T4[:].rearrange("p (c s) -> p c s", s=64)
                  tt(A3, G3, G3[:, :, 31:32].to_broadcast([128, 32, 64]), ALU.subtract, TSGt, [TL("t4")])
                  tt(DD[:, 0:31], G3[:, 1:32, 31:32].rearrange("p a b -> p (a b)"), G3[:, 0:31, 31:32].rearrange("p a b -> p (a b)"),
                     ALU.subtract, TSGt, [TL("dd")])
                  act(SC[:, 0:31], DD[:, 0:31], AF.Exp, [TL("dd")], [TL("sc")], scale=0.5)
                  act(TEP[:], T4[:], AF.Exp, [TL("t4")], TSGt, scale=0.5)
                  act(T4[:], T4[:], AF.Exp, [TL("t4")], [TL("t4")], scale=-0.5)
                  tt(QT[:], TQ[:], TEP[:], ALU.mult, TQt + TSGt, [TL("qt")])
                  tt(KT[:], TK[:], T4[:], ALU.mult, [TL("tk"), TL("t4")], [TL("kt")])

              def prelude(h):
                  for g in range(4):
                      for i in range(4):
                          bl = g * 4 + i
                          tr(PSTR[:, i * 128:(i + 1) * 128], KT[:, bl * 128:(bl + 1) * 128], IDB[:], [TL("kt"), TL("idb")], [TPSTR])
                      cp(KTM[:, g * 4:(g + 1) * 4, :].rearrange("p a b -> p (a b)"), PSTR[:, 0:512], [TPSTR], [TL("ktm")])
                  for g in range(2):
                      for i in range(8):
                          bl = g * 8 + i
                          for hh in range(2):
                              c = 2 * bl + hh
                              mm(PS4[hh * 64:(hh + 1) * 64, i * 64:(i + 1) * 64],
                                 [(KT[:, c * 64:(c + 1) * 64], QT[:, c * 64:(c + 1) * 64])], [TL("kt"), TL("qt")], [TPS4])
                      tt(PM[:, g * 8:(g + 1) * 8, :], PS4[:].rearrange("p (a b) -> p a b", b=64),
                         MASK2.unsqueeze(1).to_broadcast([128, 8, 64]), ALU.mult, [TPS4, TL("consts")], [TL("pm")])

              UBANK = [(PS5, TPS5), (PSTRF, TPSTR)]

              def u_batch(h, bt):
                  n = min(4, 31 - 4 * bt)
                  for i in range(n):
                      c = 4 * bt + i
                      bl = c // 2
                      p0 = 64 * (c % 2)
                      bank, Tbank = UBANK[i % 2]
                      mm(bank[:, (i // 2) * 128:(i // 2 + 1) * 128], [(KTM[p0:p0 + 64, bl, :], VTM[p0:p0 + 64, bl, :])],
                         [TL("ktm")] + TVt, [Tbank])
                  ne = (n + 1) // 2
                  no = n // 2
                  ue = UE4[bt % 2]
                  Tue = TL(f"ue4_{bt % 2}")
                  act(ue[:, 0:ne * 128], PS5[:, 0:ne * 128], AF.Copy, [TPS5], [Tue])
                  if no:
                      act(ue[:, 256:256 + no * 128], PSTRF[:, 0:no * 128], AF.Copy, [TPSTR], [Tue])

              def r_step(h, c):
                  bl = c // 2
                  p0 = 64 * (c % 2)
                  ob = (c // 8) % 2
                  oc = (c % 8) * 64
                  pairs = []
                  rd = [TL("qt"), TL("pm")] + TVt
                  if c > 0:
                      pairs.append((Zb[c % 3][:], QT[:, c * 64:(c + 1) * 64]))
                      rd.append(TL(f"zb{c % 3}"))
                  pairs.append((VTM[p0:p0 + 64, bl, :], PM[p0:p0 + 64, bl, :]))
                  mm(PSO[ob][:, oc:oc + 64], pairs, rd, [TPSO[ob]])
                  if c < 31:
                      bt = c // 4
                      pos = ((c % 4) % 2) * 2 + (c % 4) // 2
                      ue = UE4[bt % 2][:, pos * 128:(pos + 1) * 128]
                      Tue = TL(f"ue4_{bt % 2}")
                      r3 = c % 3
                      n3 = (c + 1) % 3
                      scb = SC[:, c:c + 1].to_broadcast([128, 128])
                      if c == 0:
                          tt(Zf[n3][:], ue, scb, ALU.mult, [Tue, TL("sc")], [TL(f"zf{n3}")])
                      else:
                          tt(Zf[n3][:], Zf[r3][:], ue, ALU.add, [TL(f"zf{r3}"), Tue], [TL(f"zf{n3}")])
                          tt(Zf[n3][:], Zf[n3][:], scb, ALU.mult, [TL(f"zf{n3}"), TL("sc")], [TL(f"zf{n3}")])
                      cp(Zb[n3][:], Zf[n3][:], [TL(f"zf{n3}")], [TL(f"zb{n3}")])
                  if c % 8 == 7:
                      tb = c // 8
                      act(OSQ[ob][:], PSO[ob][:], AF.Square, [TPSO[ob]], [TL("osq0")])
                      mm(PS4[:], [(ONESH[:], OSQ[ob][:])], [TL("onesh"), TL("osq0")], [TPS4])
                      act(RS[ob][:], PS4[:], AF.Ln, [TPS4, TL("epsr")], [TL("rs0")], bias=EPSR[:])
                      act(RS[ob][:], RS[ob][:], AF.Exp, [TL("rs0")], [TL("rs0")], scale=-0.5)
                      tt(OT[ob][:], PSO[ob][:], RS[ob][:], ALU.mult, [TPSO[ob], TL("rs0")], [TL("ot0")])
                      stt(OG.ap(h, tb), OT[ob][:], CO[:, C_GNW:C_GNW + 1], OG.ap(h, tb), ALU.mult, ALU.mult,
                          [TL("ot0"), TCO, OG.T[h][tb]], [OG.T[h][tb]])

              head_begin(0)
              for g in p_qfg_groups(0):
                  g()
              p_v(0)
              gating(0)
              for h in range(8):
                  prelude(h)
                  side = []
                  if h + 1 < 8:
                      head_begin(h + 1)
                      side = p_qfg_groups(h + 1)
                  si = 0
                  u_batch(h, 0)
                  if h == 0:
                      chk(301)
                  for c in range(32):
                      if c % 4 == 0 and 4 * (c // 4 + 1) < 31:
                          u_batch(h, c // 4 + 1)
                          if h == 0 and c == 0:
                              chk(302)
                      r_step(h, c)
                      if h == 0:
                          chk(310 + c)
                      want = (len(side) * (c + 1)) // 32
                      while si < want:
                          side[si]()
                          si += 1
                  while si < len(side):
                      side[si]()
                      si += 1
                  if h + 1 < 8:
                      p_v(h + 1)
                      gating(h + 1)

              S.barrier()
              chk(5)
              MRG = BUFB
              for s_ in range(4):
                  W, TW = wnext()
                  W3 = W[:].rearrange("p (k e) -> p k e", e=512)
                  for jj in range(2):
                      j = 2 * s_ + jj
                      co = jj * 256
                      for tb in range(NTB):
                          psA, TpsA = psnext()
                          mm(psA[:], [(W3[:, k, co:co + 128], OG.ap(k, tb)) for k in range(8)], [TW] + OG.col(tb), [TpsA])
                          psB, TpsB = psnext()
                          mm(psB[:], [(W3[:, k, co + 128:co + 256], XB.ap(k, tb)) for k in range(8)], [TW] + XB.col(tb), [TpsB])
                          b = tb % 2
                          act(TMP[b][:], psB[:], AF.Sigmoid, [TpsB, TCO], [TL(f"tmp{b}")], bias=bcol(6, j))
                          tt(MRG.ap(j, tb), psA[:], TMP[b][:], ALU.mult, [TpsA, TL(f"tmp{b}")], [MRG.T[j][tb]])

              S.barrier()
              chk(6)
              YC = BUFA
              S.op(S.dve, lambda e: e.memset(UP[0][:, 0:CW - 1], 0.0), writes=[TL("up0")])
              for s_ in range(4):
                  W, TW = wnext()
                  W3 = W[:].rearrange("p (k e) -> p k e", e=512)
                  for jj in range(2):
                      j = 2 * s_ + jj
                      co = jj * 256
                      ub = 0
                      for (ta, tb_, nm) in ((0, 16, "dga"), (16, CW, "dgb")):
                          nt = tb_ - ta
                          wcol = CO[:, C_WDW + j * CW + ta:C_WDW + j * CW + tb_]
                          tt(DG[ub][:, ta:tb_, :], IDF.unsqueeze(1).to_broadcast([128, nt, 128]),
                             wcol.unsqueeze(2).to_broadcast([128, nt, 128]), ALU.mult, [TL("consts"), TCO], [TL(nm)])
                      for tb in range(NTB):
                          psA, TpsA = psnext()
                          mm(psA[:], [(W3[:, k, co:co + 128], XB.ap(k, tb)) for k in range(8)], [TW] + XB.col(tb), [TpsA])
                          psB, TpsB = psnext()
                          mm(psB[:], [(W3[:, k, co + 128:co + 256], XB.ap(k, tb)) for k in range(8)], [TW] + XB.col(tb), [TpsB])
                          b = tb % 2
                          act(TMP[b][:], psB[:], AF.Sigmoid, [TpsB, TCO], [TL(f"tmp{b}")], bias=bcol(5, j))
                          stt(UP[ub][:, CW - 1 + tb * TB:CW - 1 + (tb + 1) * TB], psA[:], bcol(4, j), TMP[b][:], ALU.add, ALU.mult,
                              [TpsA, TCO, TL(f"tmp{b}")], [TL(f"up{ub}")])
                      for tb in range(NTB):
                          CB, TCB = ((PS4, TPS4), (PS5, TPS5))[tb % 2]
                          mm(CB[:], [(DG[ub][:, tap, :], UP[ub][:, tb * TB + tap:tb * TB + tap + TB]) for tap in range(CW)],
                             [TL("dga"), TL("dgb"), TL(f"up{ub}")], [TCB])
                          act(YC.ap(j, tb), CB[:], AF.Identity, [TCB, TCO], [YC.T[j][tb]], bias=CO[:, C_BDW + j:C_BDW + j + 1])
              ln_fm(YC, False, colsp, C_CLG, C_CLB, AF.Silu, YC)

              chk(7)
              for s_ in range(4):
                  W, TW = wnext()
                  W3 = W[:].rearrange("p (k e) -> p k e", e=512)
                  for jj in range(2):
                      j = 2 * s_ + jj
                      co = jj * 256
                      for tb in range(NTB):
                          psA, TpsA = psnext()
                          mm(psA[:], [(W3[:, k, co:co + 128], YC.ap(k, tb)) for k in range(8)], [TW] + YC.col(tb), [TpsA])
                          psB, TpsB = psnext()
                          mm(psB[:], [(W3[:, k, co + 128:co + 256], XB.ap(k, tb)) for k in range(8)], [TW] + XB.col(tb), [TpsB])
                          b = tb % 2
                          act(TMP[b][:], psB[:], AF.Sigmoid, [TpsB, TCO], [TL(f"tmp{b}")], bias=bcol(7, j))
                          stt(T1[b][:], psA[:], CO[:, C_BB + j:C_BB + j + 1], TMP[b][:], ALU.add, ALU.mult,
                              [TpsA, TCO, TL(f"tmp{b}")], [TL(f"t1{b}")])
                          tt(MRG.ap(j, tb), MRG.ap(j, tb), T1[b][:], ALU.add, [MRG.T[j][tb], TL(f"t1{b}")], [MRG.T[j][tb]])
              chk(8)
              for s_ in range(2):
                  W, TW = wnext()
                  W3 = W[:].rearrange("p (k e) -> p k e", e=512)
                  for jj in range(4):
                      j = 4 * s_ + jj
                      for tb in range(NTB):
                          ps, Tps = psnext()
                          mm(ps[:], [(W3[:, k, jj * 128:(jj + 1) * 128], MRG.ap(k, tb)) for k in range(8)], [TW] + MRG.col(tb), [Tps])
                          stt(X.ap(j, tb), X.ap(j, tb), ALPHA, ps[:], ALU.mult, ALU.add, [X.T[j][tb], Tps], [X.T[j][tb]])
                          act(XB.ap(j, tb), X.ap(j, tb), AF.Copy, [X.T[j][tb]], [XB.T[j][tb]])
              ln_fm(X, True, colsp, C_L1G, C_L1B, AF.Identity, X, XB)

              S.barrier()
              chk(9)
              THID = [[Tile() for _ in range(2)] for _ in range(NFC)]
              for half in range(2):
                  for s_ in range(11):
                      W, TW = wnext()
                      W3 = W[:].rearrange("p (k e) -> p k e", e=512)
                      for jj in range(2):
                          j = 2 * s_ + jj
                          co = jj * 256
                          for t2 in range(2):
                              tb = half * 2 + t2
                              psA, TpsA = psnext()
                              mm(psA[:], [(W3[:, k, co:co + 128], XB.ap(k, tb)) for k in range(8)], [TW] + XB.col(tb), [TpsA])
                              psB, TpsB = psnext()
                              mm(psB[:], [(W3[:, k, co + 128:co + 256], XB.ap(k, tb)) for k in range(8)], [TW] + XB.col(tb), [TpsB])
                              b = t2
                              act(TMP[b][:], psA[:], AF.Silu, [TpsA], [TL(f"tmp{b}")])
                              tt(HID_t[:, j, t2 * TB:(t2 + 1) * TB], psB[:], TMP[b][:], ALU.mult, [TpsB, TL(f"tmp{b}")], [THID[j][t2]])
                  for s_ in range(8):
                      W, TW = wnext()
                      W3 = W[:, 0:NFC * 128].rearrange("p (k e) -> p k e", e=128)
                      for t2 in range(2):
                          tb = half * 2 + t2
                          ps, Tps = psnext()
                          mm(ps[:], [(W3[:, k, :], HID_t[:, k, t2 * TB:(t2 + 1) * TB]) for k in range(NFC)],
                             [TW] + [THID[k][t2] for k in range(NFC)], [Tps])
                          stt(X.ap(s_, tb), X.ap(s_, tb), ALPHA, ps[:], ALU.mult, ALU.add, [X.T[s_][tb], Tps], [X.T[s_][tb]])
                          act(XB.ap(s_, tb), X.ap(s_, tb), AF.Copy, [X.T[s_][tb]], [XB.T[s_][tb]])
              ln_fm(X, True, colsp, C_L2G, C_L2B, AF.Identity, X, XB)


          except _Stop:
              break
        S.barrier()
        last = None
        for tt_i in range(16):
            b = tt_i % 2
            t0 = tt_i * 128
            tb = tt_i // 4
            for half in range(2):
                ps, Tps = psnext()
                for jj in range(4):
                    j = half * 4 + jj
                    tr(ps[:, jj * 128:(jj + 1) * 128], X_t[:, j, t0:t0 + 128], IDF, [X.T[j][tb], TL("consts")], [Tps])
                act(XN[b][:, half * 512:(half + 1) * 512], ps[:], AF.Copy, [Tps], [TXN[b]])
            last = S.op(S.sp, lambda e, b=b, t0=t0: e.dma_start(out=out_d[t0:t0 + 128, :], in_=XN[b][:]), reads=[TXN[b]], dma=st_o)
        S.emit(final_waits=[last])
    return nc


def _strip(W, colsets):
    K = W.shape[0]
    out = []
    for cols in colsets:
        Ws = W[:, cols]
        n = Ws.shape[1]
        out.append(Ws.reshape(K // 128, 128, n).transpose(1, 0, 2).reshape(128, (K // 128) * n))
    return np.ascontiguousarray(np.stack(out, 0))


def _fm(v):
    return np.ascontiguousarray(v.reshape(-1, 128).T)


def prep(inputs, depth=DEPTH):
    f = lambda a: np.asarray(a, dtype=np.float32)
    w_in, b_in = f(inputs["w_in"]), f(inputs["b_in"])
    w_a, w_b, w_o = f(inputs["w_a"]), f(inputs["w_b"]), f(inputs["w_o"])
    w_up, w_dn, w_dw = f(inputs["w_up"]), f(inputs["w_down"]), f(inputs["w_dw"])
    ar = np.arange(128)
    consts = np.zeros((128, NCONST), np.float32)
    consts[:, K_ID:K_ID + 128] = np.eye(128, dtype=np.float32)
    p = np.arange(128)[:, None] % 64
    t = np.arange(64)[None, :]
    consts[:, K_MASK:K_MASK + 64] = (p <= t).astype(np.float32)
    ln0 = np.concatenate([_fm(f(inputs["ln0_g"])), _fm(f(inputs["ln0_b"]))], axis=1)
    lbl = np.concatenate([_fm(f(inputs["lb_logits"])[l]) for l in range(DEPTH)], axis=1)
    cols = np.zeros((DEPTH, 128, NCOL), np.float32)
    bibc = np.zeros((DEPTH, 128, D), np.float32)
    wih, wic, wma, wmb, wo, wup, wdn = [], [], [], [], [], [], []
    for l in range(DEPTH):
        for sec in range(8):
            cols[l, :, C_BIN + sec * 8:C_BIN + sec * 8 + 8] = _fm(b_in[l, sec * D:(sec + 1) * D])
        cols[l, :, C_GNW] = f(inputs["g_norm_w"])[l]
        cols[l, :, C_WDW:C_WDW + 8 * CW] = w_dw[l].T.reshape(8, 128, CW).transpose(1, 0, 2).reshape(128, 8 * CW)
        for nm, c0 in (("b_dw", C_BDW), ("conv_ln_g", C_CLG), ("conv_ln_b", C_CLB), ("b_b", C_BB),
                       ("ln1_g", C_L1G), ("ln1_b", C_L1B), ("ln2_g", C_L2G), ("ln2_b", C_L2B)):
            cols[l, :, c0:c0 + 8] = _fm(f(inputs[nm])[l])
        bibc[l] = np.broadcast_to(b_in[l, 2 * D:3 * D][None, :], (128, D))
        if l >= depth:
            continue
        wih.append(_strip(w_in[l], [np.concatenate([sec * D + h * 128 + ar for sec in range(4)]) for h in range(8)]))
        wic.append(_strip(w_in[l], [np.concatenate([4 * D + j * 128 + ar, 5 * D + j * 128 + ar,
                                                    4 * D + (j + 1) * 128 + ar, 5 * D + (j + 1) * 128 + ar])
                                    for j in range(0, 8, 2)]))
        cat_a = np.concatenate([w_a[l], w_in[l][:, 6 * D:7 * D]], axis=1)
        cat_b = np.concatenate([w_b[l], w_in[l][:, 7 * D:8 * D]], axis=1)
        sets = [np.concatenate([j * 128 + ar, D + j * 128 + ar, (j + 1) * 128 + ar, D + (j + 1) * 128 + ar])
                for j in range(0, 8, 2)]
        wma.append(_strip(cat_a, sets))
        wmb.append(_strip(cat_b, sets))
        wo.append(_strip(w_o[l], [np.arange(s * 512, (s + 1) * 512) for s in range(2)]))
        wup.append(_strip(w_up[l], [np.concatenate([j * 128 + ar, FH + j * 128 + ar, (j + 1) * 128 + ar, FH + (j + 1) * 128 + ar])
                                    for j in range(0, NFC, 2)]))
        wdn.append(_strip(w_dn[l], [np.arange(s * 128, (s + 1) * 128) for s in range(8)]))

    def stk(lst, shape):
        a = np.zeros((DEPTH,) + shape, np.float32)
        for i, v in enumerate(lst):
            a[i] = v
        return a
    return {
        "consts": consts, "ln0": np.ascontiguousarray(ln0), "lbl": np.ascontiguousarray(lbl), "cols": cols, "bibc": bibc,
        "wih": stk(wih, (8, 128, 4096)), "wic": stk(wic, (4, 128, 4096)), "wma": stk(wma, (4, 128, 4096)),
        "wmb": stk(wmb, (4, 128, 4096)), "wo": stk(wo, (2, 128, 4096)), "wup": stk(wup, (11, 128, 4096)),
        "wdn": stk(wdn, (8, 128, NFC * 128)),
    }


_NC_CACHE = {}


def kernel(**inputs):
    x = np.asarray(inputs["x"], dtype=np.float32)
    shared = prep(inputs)
    if "nc" not in _NC_CACHE:
        _NC_CACHE["nc"] = build(DEPTH)
    nc = _NC_CACHE["nc"]
    in_maps = []
    for b in range(8):
        m = dict(shared)
        m["x"] = np.ascontiguousarray(x[b])
        in_maps.append(m)
    res = run_bass_kernel_spmd(nc, in_maps, core_ids=list(range(8)))
    return np.stack([np.asarray(r["out"], dtype=np.float32) for r in res.results], axis=0)
```

```python
import numpy as np
from contextlib import ExitStack
import concourse.bass as bass
import concourse.mybir as mybir
from concourse.bass_utils import run_bass_kernel_spmd

F32 = mybir.dt.float32
BF16 = mybir.dt.bfloat16
AF = mybir.ActivationFunctionType
ALU = mybir.AluOpType
AX = mybir.AxisListType

D = 1024
T = 2048
DEPTH = 4
NCH = 8
TB = 512
NTB = 4
FH = 2816
NFC = 22
CW = 31
ALPHA = (2 * DEPTH) ** 0.25
LN_EPS = 1e-5
RMS_EPS = 1e-6
F_MIN = 1e-30

C_BIN = 0
C_GNW = 64
C_WDW = 65
C_BDW = C_WDW + 8 * CW
C_CLG = C_BDW + 8
C_CLB = C_CLG + 8
C_BB = C_CLB + 8
C_L1G = C_BB + 8
C_L1B = C_L1G + 8
C_L2G = C_L1B + 8
C_L2B = C_L2G + 8
NCOL = C_L2B + 8
K_ID = 0
K_MASK = 128
K_RESET = 192
NCONST = 192
NWB = 2


class Tile:
    __slots__ = ("w", "r")

    def __init__(self):
        self.w = None
        self.r = {}


class Eng:
    def __init__(self, S, name):
        self.S = S
        self.name = name
        self.ops = []
        self.sem = None
        self.cnt = 0
        self.waited = {}

    def rotate(self):
        self.sem = self.S.new_sem(self.name)
        self.cnt = 0


class DmaStream:
    def __init__(self, S, name):
        self.sem = S.new_sem(name)
        self.cnt = 0


class Sched:
    def __init__(self, nc, stack):
        self.nc = nc
        self.stack = stack
        self.nsem = 0
        self.pe = Eng(self, "pe")
        self.act = Eng(self, "act")
        self.dve = Eng(self, "dve")
        self.pool = Eng(self, "pool")
        self.sp = Eng(self, "sp")
        self.engs = [self.pe, self.act, self.dve, self.pool, self.sp]
        for e in self.engs:
            e.rotate()

    def new_sem(self, name):
        self.nsem += 1
        return self.stack.enter_context(self.nc.semaphore(f"s{self.nsem}_{name}"))

    def rotate_all(self):
        for e in self.engs:
            e.rotate()

    def op(self, eng, fn, reads=(), writes=(), dma=None):
        need = {}
        for t in reads:
            if t.w is not None:
                s, v = t.w
                if need.get(s, 0) < v:
                    need[s] = v
        for t in writes:
            if t.w is not None:
                s, v = t.w
                if need.get(s, 0) < v:
                    need[s] = v
            for s, v in t.r.items():
                if need.get(s, 0) < v:
                    need[s] = v
        waits = []
        for s, v in need.items():
            if eng is self.pe and s is eng.sem:
                continue
            if eng.waited.get(s, 0) < v:
                eng.waited[s] = v
                waits.append((s, v))
        if dma is not None:
            dma.cnt += 16
            point = (dma.sem, dma.cnt)
            eng.ops.append((waits, fn, dma.sem, 16))
        else:
            eng.cnt += 1
            point = (eng.sem, eng.cnt)
            eng.ops.append((waits, fn, eng.sem, 1))
        s, v = point
        for t in reads:
            if t.r.get(s, 0) < v:
                t.r[s] = v
        for t in writes:
            t.w = point
            t.r = {}
        return point

    def barrier(self):
        comp = [self.pe, self.act, self.dve, self.pool]
        for e in comp:
            waits = []
            for f in comp:
                if f is e or f.cnt == 0:
                    continue
                if e.waited.get(f.sem, 0) < f.cnt:
                    e.waited[f.sem] = f.cnt
                    waits.append((f.sem, f.cnt))
            if waits:
                e.ops.append((waits, None, None, 0))

    def emit(self, final_waits=()):
        def replay(eng, hw):
            for waits, fn, sem, inc in eng.ops:
                for s, v in waits:
                    hw.wait_ge(s, v)
                if fn is None:
                    continue
                ins = fn(hw)
                if sem is not None:
                    ins.then_inc(sem, inc)

        with self.nc.Block() as block:
            @block.tensor
            def _(e):
                replay(self.pe, e)

            @block.scalar
            def _(e):
                replay(self.act, e)

            @block.vector
            def _(e):
                replay(self.dve, e)

            @block.gpsimd
            def _(e):
                replay(self.pool, e)

            @block.sync
            def _(e):
                replay(self.sp, e)
                for s, v in final_waits:
                    e.wait_ge(s, v)


class Buf3:
    def __init__(self, ap3, nj, ntok=T):
        self.t = ap3
        self.nj = nj
        self.T = [[Tile() for _ in range(ntok // TB)] for _ in range(nj)]

    def ap(self, j, tb):
        return self.t[:, j, tb * TB:(tb + 1) * TB]

    def col(self, tb):
        return [self.T[j][tb] for j in range(self.nj)]


class _Stop(Exception):
    pass


def build(depth=DEPTH, dbg=False, stop=None):
    def chk(n):
        if stop == n:
            raise _Stop()
    nc = bass.Bass("TRN2", target_bir_lowering=False)

    def din(name, shape):
        return nc.dram_tensor(name, list(shape), F32, kind="ExternalInput").ap()

    x_d = din("x", [T, D])
    consts_d = din("consts", [128, NCONST])
    ln0_d = din("ln0", [128, 16])
    lbl_d = din("lbl", [128, DEPTH * 8])
    cols_d = din("cols", [DEPTH, 128, NCOL])
    bibc_d = din("bibc", [DEPTH, 128, D])
    wih_d = din("wih", [DEPTH, 8, 128, 4096])
    wic_d = din("wic", [DEPTH, 4, 128, 4096])
    wma_d = din("wma", [DEPTH, 4, 128, 4096])
    wmb_d = din("wmb", [DEPTH, 4, 128, 4096])
    wo_d = din("wo", [DEPTH, 2, 128, 4096])
    wup_d = din("wup", [DEPTH, 11, 128, 4096])
    wdn_d = din("wdn", [DEPTH, 8, 128, NFC * 128])
    out_d = nc.dram_tensor("out", [T, D], F32, kind="ExternalOutput").ap()
    dbg_d = {}

    with ExitStack() as st:
        S = Sched(nc, st)

        def sb(name, shape, dt=F32):
            return st.enter_context(nc.sbuf_tensor(name, list(shape), dt))

        def pst(name, shape, dt=F32):
            return st.enter_context(nc.psum_tensor(name, list(shape), dt))

        DD = sb("DDEC", [128, 64])
        SC = sb("SCAL", [128, 64])
        X_t = sb("X", [128, NCH, T])
        XB_t = sb("XB", [128, NCH, T], BF16)
        BAB = sb("BAB", [128, 32768], BF16)
        X = Buf3(X_t, NCH)
        XB = Buf3(XB_t, NCH)
        BUFA = Buf3(BAB[:, 0:16384].rearrange("p (j t) -> p j t", t=T), NCH)
        BUFB = Buf3(BAB[:, 16384:32768].rearrange("p (j t) -> p j t", t=T), NCH)
        HID_t = BAB[:, 0:NFC * 1024].rearrange("p (j t) -> p j t", t=1024)

        def f32row(i):
            return BAB[:, 16384 + 4096 * i:16384 + 4096 * (i + 1)].bitcast(F32)
        TQ, TSG, TK, T4 = f32row(0), f32row(1), f32row(2), f32row(3)
        SCR = sb("SCR", [128, 10240], BF16)

        def scr(a, n, dt=BF16):
            v = SCR[:, a:a + n]
            return v.bitcast(F32) if dt == F32 else v
        QT = scr(0, 2048)
        KT = scr(2048, 2048)
        KTM = scr(4096, 2048).rearrange("p (a b) -> p a b", b=128)
        VTM = scr(6144, 2048).rearrange("p (a b) -> p a b", b=128)
        PM = scr(8192, 1024).rearrange("p (a b) -> p a b", b=64)
        OT = [TK[:, 0:512]] * 2
        RS = [TK[:, 512:1024]] * 2
        OSQ = [TK[:, 1024:1280].bitcast(BF16)] * 2
        UP = [scr(0, 2080)] * 2
        DG = [scr(2080, CW * 128).rearrange("p (a b) -> p a b", b=128)] * 2
        TMP = [scr(6048, 1024, F32), scr(7072, 1024, F32)]
        SQ = scr(6048, 2048).rearrange("p (a b) -> p a b", b=TB)
        T1 = [scr(8096, 1024, F32), scr(9120, 1024, F32)]
        T1P = scr(8096, 2048, F32).rearrange("p (a b) -> p a b", b=TB)
        TEP = TSG
        Zf = [sb(f"Zf{i}", [128, 128]) for i in range(3)]
        Zb = [sb(f"Zb{i}", [128, 128], BF16) for i in range(3)]
        UE4 = [TK[:, 1280:1792], sb("UE4B", [128, 512])]
        WB = [sb(f"WB{i}", [128, 4096], BF16) for i in range(NWB)]
        CONSTS = sb("CONSTS", [128, NCONST])
        IDB = sb("IDB", [128, 128], BF16)
        ONESB = sb("ONESB", [128, 128], BF16)
        ONESH = sb("ONESH", [128, 128], BF16)
        EPSL = sb("EPSL", [128, 1])
        EPSR = sb("EPSR", [128, 1])
        ONEC = sb("ONEC", [128, 1])
        COLS = [sb("COLS0", [128, NCOL])] * 2
        BIBC = [sb(f"BIBC{i}", [128, 128]) for i in range(2)]
        LN0 = sb("LN0", [128, 16])
        LBL = sb("LBL", [128, DEPTH, 8])
        LBE = sb("LBE", [128, DEPTH, 8])
        LB = sb("LB", [128, DEPTH, 8])
        OML = sb("OML", [128, DEPTH, 8])
        NOML = sb("NOML", [128, DEPTH, 8])
        THR = sb("THR", [128, DEPTH, 8])
        LBT = sb("LBT", [128, 8])

        def bab32(a, n):
            return BAB[:, a:a + 2 * n].bitcast(F32)
        XIN = [bab32(0, D), bab32(2048, D)]
        XN = [bab32(4096, D), bab32(6144, D)]
        JUNK = BAB[:, 8192:8192 + D]
        SS = [sb(f"SS{i}", [128, 8]) for i in range(2)]

        IDF = CONSTS[:, K_ID:K_ID + 128]
        MASK2 = CONSTS[:, K_MASK:K_MASK + 64]

        PS = [pst(f"PS{i}", [128, TB]) for i in range(3)]
        PSTR = pst("PSTR", [128, 1024], BF16)
        PSTRF = PSTR[:, 0:1024].bitcast(F32)
        PS4 = pst("PS4", [128, TB])
        PS5 = pst("PS5", [128, TB])
        PS6 = pst("PS6", [128, TB])
        PS7 = pst("PS7", [128, TB])
        TPS = [Tile() for _ in range(3)]
        TPSTR, TPS4, TPS5, TPS6, TPS7 = Tile(), Tile(), Tile(), Tile(), Tile()
        TU = [Tile() for _ in range(4)]
        ps_rr = [0]

        def psnext():
            i = ps_rr[0] % 3
            ps_rr[0] += 1
            return PS[i], TPS[i]

        tl = {}

        def TL(name):
            if name not in tl:
                tl[name] = Tile()
            return tl[name]

        def act(out, in_, func, reads, writes, bias=None, scale=None, accum_out=None):
            kw = {}
            if bias is not None:
                kw["bias"] = bias
            if scale is not None:
                kw["scale"] = scale
            if accum_out is not None:
                kw["accum_out"] = accum_out
            return S.op(S.act, lambda e: e.activation(out=out, in_=in_, func=func, **kw), reads, writes)

        def tt(out, in0, in1, op, reads, writes, eng=None):
            eng = eng or S.dve
            return S.op(eng, lambda e: e.tensor_tensor(out=out, in0=in0, in1=in1, op=op), reads, writes)

        def ts(out, in0, s1, s2, op0, op1, reads, writes, eng=None):
            eng = eng or S.dve
            if op1 is None:
                return S.op(eng, lambda e: e.tensor_scalar(out=out, in0=in0, scalar1=s1, scalar2=None, op0=op0), reads, writes)
            return S.op(eng, lambda e: e.tensor_scalar(out=out, in0=in0, scalar1=s1, scalar2=s2, op0=op0, op1=op1), reads, writes)

        def stt(out, in0, scalar, in1, op0, op1, reads, writes):
            return S.op(S.dve, lambda e: e.scalar_tensor_tensor(out=out, in0=in0, scalar=scalar, in1=in1, op0=op0, op1=op1), reads, writes)

        def cp(out, in_, reads, writes, eng=None):
            eng = eng or S.dve
            return S.op(eng, lambda e: e.tensor_copy(out=out, in_=in_), reads, writes)

        def mm(out, pairs, reads, writes, first=True, last=True):
            def fn(e):
                n = len(pairs)
                ins = None
                for i, (l, r) in enumerate(pairs):
                    ins = e.matmul(out, lhsT=l, rhs=r, start=(first and i == 0), stop=(last and i == n - 1))
                return ins
            return S.op(S.pe, fn, reads, writes)

        def tr(out, in_, ident, reads, writes):
            return S.op(S.pe, lambda e: e.transpose(out=out, in_=in_, identity=ident), reads, writes)

        ld_misc = DmaStream(S, "ldm")
        ld_x = DmaStream(S, "ldx")
        ld_w = DmaStream(S, "ldw")
        st_o = DmaStream(S, "sto")

        wlist = []
        for l in range(depth):
            for s_ in range(8):
                wlist.append((wih_d[l, s_], 4096))
            for s_ in range(4):
                wlist.append((wma_d[l, s_], 4096))
            for s_ in range(4):
                wlist.append((wic_d[l, s_], 4096))
            for s_ in range(4):
                wlist.append((wmb_d[l, s_], 4096))
            for s_ in range(2):
                wlist.append((wo_d[l, s_], 4096))
            for half in range(2):
                for s_ in range(11):
                    wlist.append((wup_d[l, s_], 4096))
                for s_ in range(8):
                    wlist.append((wdn_d[l, s_], NFC * 128))
        TWB = [Tile() for _ in range(NWB)]
        wstate = {"issued": 0, "used": 0}

        def w_issue():
            i = wstate["issued"]
            if i >= len(wlist):
                return
            src, n = wlist[i]
            b = i % NWB
            S.op(S.pool, lambda e: e.dma_start(out=WB[b][:, 0:n], in_=src), reads=[], writes=[TWB[b]], dma=ld_w)
            wstate["issued"] += 1

        def wnext():
            i = wstate["used"]
            while wstate["issued"] < min(i + NWB, len(wlist)):
                w_issue()
            wstate["used"] += 1
            b = i % NWB
            return WB[b], TWB[b]

        S.op(S.sp, lambda e: e.dma_start(out=CONSTS[:], in_=consts_d), writes=[TL("consts")], dma=ld_misc)
        S.op(S.sp, lambda e: e.dma_start(out=LN0[:], in_=ln0_d), writes=[TL("ln0")], dma=ld_misc)
        S.op(S.sp, lambda e: e.dma_start(out=LBL[:].rearrange("p a b -> p (a b)"), in_=lbl_d), writes=[TL("lbl")], dma=ld_misc)
        cp(IDB[:], IDF, [TL("consts")], [TL("idb")])
        S.op(S.dve, lambda e: e.memset(ONESB[:], 1.0 / D), writes=[TL("onesb")])
        S.op(S.dve, lambda e: e.memset(ONESH[:], 1.0 / 128), writes=[TL("onesh")])
        S.op(S.dve, lambda e: e.memset(EPSL[:], LN_EPS), writes=[TL("epsl")])
        S.op(S.dve, lambda e: e.memset(EPSR[:], RMS_EPS), writes=[TL("epsr")])
        S.op(S.dve, lambda e: e.memset(ONEC[:], 1.0), writes=[TL("onec")])

        Tlb = TL("lb")
        cp(LBT[:], LBL[:, 0, :], [TL("lbl")], [TL("lbt")])
        for l in range(1, DEPTH):
            tt(LBT[:], LBT[:], LBL[:, l, :], ALU.max, [TL("lbl"), TL("lbt")], [TL("lbt")])
        for l in range(DEPTH):
            tt(LBE[:, l, :], LBL[:, l, :], LBT[:], ALU.subtract, [TL("lbl"), TL("lbt")], [TL("lbe")])
        act(LBE[:].rearrange("p a b -> p (a b)"), LBE[:].rearrange("p a b -> p (a b)"), AF.Exp, [TL("lbe")], [TL("lbe")])
        cp(LBT[:], LBE[:, 0, :], [TL("lbe")], [TL("lbt")])
        for l in range(1, DEPTH):
            tt(LBT[:], LBT[:], LBE[:, l, :], ALU.add, [TL("lbe"), TL("lbt")], [TL("lbt")])
        S.op(S.dve, lambda e: e.reciprocal(out=LBT[:], in_=LBT[:]), [TL("lbt")], [TL("lbt")])
        for l in range(DEPTH):
            tt(LBE[:, l, :], LBE[:, l, :], LBT[:], ALU.mult, [TL("lbe"), TL("lbt")], [TL("lbe")])
        S.op(S.dve, lambda e: e.memset(LB[:, 0, :], 0.0), writes=[Tlb])
        for l in range(1, DEPTH):
            tt(LB[:, l, :], LB[:, l - 1, :], LBE[:, l, :], ALU.add, [TL("lbe"), Tlb], [Tlb])
        LBf = LB[:].rearrange("p a b -> p (a b)")
        ts(OML[:].rearrange("p a b -> p (a b)"), LBf, -1.0, 1.0, ALU.mult, ALU.add, [Tlb], [Tlb])
        ts(NOML[:].rearrange("p a b -> p (a b)"), LBf, 1.0, -1.0, ALU.mult, ALU.add, [Tlb], [Tlb])
        ts(THR[:].rearrange("p a b -> p (a b)"), LBf, -1.0, F_MIN, ALU.mult, ALU.add, [Tlb], [Tlb])

        TCOLS = [Tile()] * 2

        def load_params(l):
            b = l % 2
            S.op(S.sp, lambda e: e.dma_start(out=COLS[b][:], in_=cols_d[l]), writes=[TCOLS[b]], dma=ld_misc)

        load_params(0)

        TXIN = [Tile(), Tile()]
        TXN = [Tile(), Tile()]
        TSS = [Tile(), Tile()]
        for tt_i in range(16):
            b = tt_i % 2
            t0 = tt_i * 128
            tb = tt_i // 4
            S.op(S.sp, lambda e, b=b, t0=t0: e.dma_start(out=XIN[b][:], in_=x_d[t0:t0 + 128, :]), writes=[TXIN[b]], dma=ld_x)
            S.op(S.dve, lambda e, b=b: e.reduce_sum(out=SS[b][:, 0:1], in_=XIN[b][:], axis=AX.X), [TXIN[b]], [TSS[b]])
            act(JUNK[:], XIN[b][:], AF.Square, [TXIN[b]], [TL("junk"), TSS[b]], accum_out=SS[b][:, 1:2])
            ts(SS[b][:, 2:4], SS[b][:, 0:2], 1.0 / D, None, ALU.mult, None, [TSS[b]], [TSS[b]])
            tt(SS[b][:, 4:5], SS[b][:, 2:3], SS[b][:, 2:3], ALU.mult, [TSS[b]], [TSS[b]])
            tt(SS[b][:, 5:6], SS[b][:, 3:4], SS[b][:, 4:5], ALU.subtract, [TSS[b]], [TSS[b]])
            act(SS[b][:, 6:7], SS[b][:, 5:6], AF.Ln, [TSS[b], TL("epsl")], [TSS[b]], bias=EPSL[:])
            act(SS[b][:, 7:8], SS[b][:, 6:7], AF.Exp, [TSS[b]], [TSS[b]], scale=-0.5)
            ts(XN[b][:], XIN[b][:], SS[b][:, 2:3], SS[b][:, 7:8], ALU.subtract, ALU.mult, [TXIN[b], TSS[b]], [TXN[b]])
            for half in range(2):
                ps, Tps = psnext()
                for jj in range(4):
                    j = half * 4 + jj
                    tr(ps[:, jj * 128:(jj + 1) * 128], XN[b][:, j * 128:(j + 1) * 128], IDF, [TXN[b], TL("consts")], [Tps])
                for jj in range(4):
                    j = half * 4 + jj
                    act(X_t[:, j, t0:t0 + 128], ps[:, jj * 128:(jj + 1) * 128], AF.Identity, [Tps, TL("ln0")], [X.T[j][tb]],
                        scale=LN0[:, j:j + 1], bias=LN0[:, 8 + j:9 + j])
            cp(XB_t[:, :, t0:t0 + 128], X_t[:, :, t0:t0 + 128], X.col(tb), XB.col(tb))

        def ln_fm(src, src_f32, cols, gcol, bcol, func, dst_main, dst_bf=None, eps=None):
            banks = [(PS6, TPS6, PS7, TPS7), (PS4, TPS4, PS5, TPS5)]

            def stats_a(tb):
                PMn, TPMn, PVr, TPVr = banks[tb % 2]
                sl = slice(tb * TB, (tb + 1) * TB)
                sbuf = XB if src_f32 else src
                mm(PMn[:], [(ONESB[:], sbuf.ap(j, tb)) for j in range(NCH)], sbuf.col(tb) + [TL("onesb")], [TPMn])
                for hf in range(2):
                    act(SQ[:], sbuf.t[:, 4 * hf:4 * hf + 4, sl], AF.Square, sbuf.col(tb), [TL("tmp0"), TL("tmp1")])
                    mm(PVr[:], [(ONESB[:], SQ[:, j, :]) for j in range(4)], [TL("tmp0"), TL("tmp1"), TL("onesb")], [TPVr],
                       first=(hf == 0), last=(hf == 1))
                act(TMP[0][:], PMn[:], AF.Square, [TPMn], [TL("tmp0")])

            def stats_b(tb):
                PMn, TPMn, PVr, TPVr = banks[tb % 2]
                tt(TMP[1][:], PVr[:], TMP[0][:], ALU.subtract, [TPVr, TL("tmp0")], [TL("tmp1")])
                act(TMP[1][:], TMP[1][:], AF.Ln, [TL("tmp1"), TL("epsl")], [TL("tmp1")], bias=EPSL[:])
                act(PVr[:], TMP[1][:], AF.Exp, [TL("tmp1")], [TPVr], scale=-0.5)

            def apply_dve(tb):
                PMn, TPMn, PVr, TPVr = banks[tb % 2]
                sl = slice(tb * TB, (tb + 1) * TB)
                tt(src.t[:, :, sl], src.t[:, :, sl], PMn[:].unsqueeze(1).to_broadcast([128, NCH, TB]), ALU.subtract,
                   src.col(tb) + [TPMn], src.col(tb))
                tt(src.t[:, :, sl], src.t[:, :, sl], PVr[:].unsqueeze(1).to_broadcast([128, NCH, TB]), ALU.mult,
                   src.col(tb) + [TPVr], src.col(tb))

            def apply_act(tb):
                for j in range(NCH):
                    act(dst_main.ap(j, tb), src.ap(j, tb), func, [src.T[j][tb], cols[1]], [dst_main.T[j][tb]],
                        scale=cols[0][:, gcol + j:gcol + j + 1], bias=cols[0][:, bcol + j:bcol + j + 1])
                if dst_bf is not None:
                    sl = slice(tb * TB, (tb + 1) * TB)
                    cp(dst_bf.t[:, :, sl], dst_main.t[:, :, sl], dst_main.col(tb), dst_bf.col(tb))

            def apply_pairs(tb):
                PMn, TPMn, PVr, TPVr = banks[tb % 2]
                sl = slice(tb * TB, (tb + 1) * TB)
                for pr in range(NCH // 2):
                    tt(T1P, src.t[:, 2 * pr:2 * pr + 2, sl], PMn[:].unsqueeze(1).to_broadcast([128, 2, TB]), ALU.subtract,
                       [src.T[2 * pr][tb], src.T[2 * pr + 1][tb], TPMn], [TL("t10"), TL("t11")])
                    tt(T1P, T1P, PVr[:].unsqueeze(1).to_broadcast([128, 2, TB]), ALU.mult,
                       [TL("t10"), TL("t11"), TPVr], [TL("t10"), TL("t11")])
                    for jj in range(2):
                        j = 2 * pr + jj
                        act(dst_main.ap(j, tb), T1P[:, jj, :], func, [TL(f"t1{jj}"), cols[1]], [dst_main.T[j][tb]],
                            scale=cols[0][:, gcol + j:gcol + j + 1], bias=cols[0][:, bcol + j:bcol + j + 1])

            stats_a(0)
            stats_b(0)
            for tb in range(NTB):
                if tb + 1 < NTB:
                    stats_a(tb + 1)
                if src_f32:
                    apply_dve(tb)
                    if tb + 1 < NTB:
                        stats_b(tb + 1)
                    apply_act(tb)
                else:
                    apply_pairs(tb)
                    if tb + 1 < NTB:
                        stats_b(tb + 1)

        TQt = [Tile() for _ in range(NTB)]
        TSGt = [Tile() for _ in range(NTB)]
        TVt = [Tile() for _ in range(NTB)]
        TPSO = [TPS6, TPS7]
        PSO = [PS6, PS7]

        for l in range(depth):
          try:
              if l > 0:
                  S.rotate_all()
              CO = COLS[l % 2]
              TCO = TCOLS[l % 2]
              colsp = (CO, TCO)

              def bcol(sec, c):
                  return CO[:, C_BIN + sec * 8 + c:C_BIN + sec * 8 + c + 1]

              S.barrier()
              if l > 0:
                  load_params(l)
              chk(0)
              OG = BUFA
              hw_ = {}

              def head_begin(h):
                  W, TW = wnext()
                  BI = BIBC[h % 2]
                  TBI = TL(f"bibc{h % 2}")
                  S.op(S.sp, lambda e, BI=BI, l=l, h=h: e.dma_start(out=BI[:], in_=bibc_d[l][:, h * 128:(h + 1) * 128]),
                       writes=[TBI], dma=ld_misc)
                  hw_[h] = (W[:].rearrange("p (k e) -> p k e", e=512), TW, BI, TBI)

              def p_qfg_groups(h):
                  W3, TW, BI, TBI = hw_[h]
                  TGG = OG.t[:, h, :]
                  groups = []
                  for (c0, dst, dT, fn_, sec) in ((0, TQ, TQt, AF.Silu, 0), (384, TGG, OG.T[h], AF.Silu, 3),
                                                  (128, TSG, TSGt, AF.Sigmoid, 1)):
                      for tb in range(NTB):
                          sl = slice(tb * TB, (tb + 1) * TB)

                          def g(tb=tb, sl=sl, c0=c0, dst=dst, dT=dT, fn_=fn_, sec=sec):
                              ps, Tps = psnext()
                              mm(ps[:], [(W3[:, k, c0:c0 + 128], XB.ap(k, tb)) for k in range(8)], [TW] + XB.col(tb), [Tps])
                              act(dst[:, sl], ps[:], AF.Identity, [Tps, TCO], [dT[tb]], bias=bcol(sec, h))
                          groups.append(g)
                  return groups

              def p_v(h):
                  W3, TW, BI, TBI = hw_[h]
                  for tb in range(NTB):
                      ps, Tps = psnext()
                      for t4 in range(4):
                          t0 = tb * TB + t4 * 128
                          mm(ps[:, t4 * 128:(t4 + 1) * 128], [(XB_t[:, k, t0:t0 + 128], W3[:, k, 256:384]) for k in range(8)],
                             [TW] + XB.col(tb), [Tps])
                      tt(VTM[:, tb * 4:(tb + 1) * 4, :], ps[:].rearrange("p (a b) -> p a b", b=128),
                         BI[:].unsqueeze(1).to_broadcast([128, 4, 128]), ALU.add,
                         [Tps, TBI], [TVt[tb]])

              def gating(h):
                  lbc = LB[:, l, h:h + 1]
                  omlc = OML[:, l, h:h + 1]
                  nomlc = NOML[:, l, h:h + 1]
                  thrc = THR[:, l, h:h + 1]
                  act(TQ[:], TQ[:], AF.Silu, TQt, TQt)
                  act(OG.t[:, h, :], OG.t[:, h, :], AF.Silu, OG.T[h], OG.T[h])
                  act(TSG[:], TSG[:], AF.Sigmoid, TSGt, TSGt)
                  ts(T4[:], TSG[:], omlc, thrc, ALU.mult, ALU.max, TSGt + [Tlb], [TL("t4")])
                  act(T4[:], T4[:], AF.Ln, [TL("t4"), Tlb], [TL("t4")], bias=lbc)
                  ts(TK[:], TSG[:], nomlc, omlc, ALU.mult, ALU.add, TSGt + [Tlb], [TL("tk"), TL("ot0"), TL("rs0"), TL("osq0"), TL("ue4_0")])
                  S.op(S.dve, lambda e: e.tensor_tensor_scan(out=TSG[:], data0=T4[:], data1=T4[:], initial=0.0,
                                                             op0=ALU.add, op1=ALU.add),
                       [TL("t4")] + TSGt + [TL("tk")], TSGt)
                  G3 = TSG[:].rearrange("p (c s) -> p c s", s=64)
                  A3 = T4[:].rearrange("p (c s) -> p c s", s=64)
                  tt(A3, G3, G3[:, :, 31:32].to_broadcast([128, 32, 64]), ALU.subtract, TSGt, [TL("t4")])
                  tt(DD[:, 0:31], G3[:, 1:32, 31:32].rearrange("p a b -> p (a b)"), G3[:, 0:31, 31:32].rearrange("p a b -> p (a b)"),
                     ALU.subtract, TSGt, [TL("dd")])
                  act(SC[:, 0:31], DD[:, 0:31], AF.Exp, [TL("dd")], [TL("sc")], scale=0.5)
                  act(TEP[:], T4[:], AF.Exp, [TL("t4")], TSGt, scale=0.5)
                  act(T4[:], T4[:], AF.Exp, [TL("t4")], [TL("t4")], scale=-0.5)
                  tt(QT[:], TQ[:], TEP[:], ALU.mult, TQt + TSGt, [TL("qt")])
                  tt(KT[:], TK[:], T4[:], ALU.mult, [TL("tk"), TL("t4")], [TL("kt")])

              def prelude(h):
                  for g in range(4):
                      for i in range(4):
                          bl = g * 4 + i
                          tr(PSTR[:, i * 128:(i + 1) * 128], KT[:, bl * 128:(bl + 1) * 128], IDB[:], [TL("kt"), TL("idb")], [TPSTR])
                      cp(KTM[:, g * 4:(g + 1) * 4, :].rearrange("p a b -> p (a b)"), PSTR[:, 0:512], [TPSTR], [TL("ktm")])
                  for g in range(2):
                      for i in range(8):
                          bl = g * 8 + i
                          for hh in range(2):
                              c = 2 * bl + hh
                              mm(PS4[hh * 64:(hh + 1) * 64, i * 64:(i + 1) * 64],
                                 [(KT[:, c * 64:(c + 1) * 64], QT[:, c * 64:(c + 1) * 64])], [TL("kt"), TL("qt")], [TPS4])
                      tt(PM[:, g * 8:(g + 1) * 8, :], PS4[:].rearrange("p (a b) -> p a b", b=64),
                         MASK2.unsqueeze(1).to_broadcast([128, 8, 64]), ALU.mult, [TPS4, TL("consts")], [TL("pm")])

              UBANK = [(PS5, TPS5), (PSTRF, TPSTR)]

              def u_batch(h, bt):
                  n = min(4, 31 - 4 * bt)
                  for i in range(n):
                      c = 4 * bt + i
                      bl = c // 2
                      p0 = 64 * (c % 2)
                      bank, Tbank = UBANK[i % 2]
                      mm(bank[:, (i // 2) * 128:(i // 2 + 1) * 128], [(KTM[p0:p0 + 64, bl, :], VTM[p0:p0 + 64, bl, :])],
                         [TL("ktm")] + TVt, [Tbank])
                  ne = (n + 1) // 2
                  no = n // 2
                  ue = UE4[bt % 2]
                  Tue = TL(f"ue4_{bt % 2}")
                  act(ue[:, 0:ne * 128], PS5[:, 0:ne * 128], AF.Copy, [TPS5], [Tue])
                  if no:
                      act(ue[:, 256:256 + no * 128], PSTRF[:, 0:no * 128], AF.Copy, [TPSTR], [Tue])

              def r_step(h, c):
                  bl = c // 2
                  p0 = 64 * (c % 2)
                  ob = (c // 8) % 2
                  oc = (c % 8) * 64
                  pairs = []
                  rd = [TL("qt"), TL("pm")] + TVt
                  if c > 0:
                      pairs.append((Zb[c % 3][:], QT[:, c * 64:(c + 1) * 64]))
                      rd.append(TL(f"zb{c % 3}"))
                  pairs.append((VTM[p0:p0 + 64, bl, :], PM[p0:p0 + 64, bl, :]))
                  mm(PSO[ob][:, oc:oc + 64], pairs, rd, [TPSO[ob]])
                  if c < 31:
                      bt = c // 4
                      pos = ((c % 4) % 2) * 2 + (c % 4) // 2
                      ue = UE4[bt % 2][:, pos * 128:(pos + 1) * 128]
                      Tue = TL(f"ue4_{bt % 2}")
                      r3 = c % 3
                      n3 = (c + 1) % 3
                      scb = SC[:, c:c + 1].to_broadcast([128, 128])
                      if c == 0:
                          tt(Zf[n3][:], ue, scb, ALU.mult, [Tue, TL("sc")], [TL(f"zf{n3}")])
                      else:
                          tt(Zf[n3][:], Zf[r3][:], ue, ALU.add, [TL(f"zf{r3}"), Tue], [TL(f"zf{n3}")])
                          tt(Zf[n3][:], Zf[n3][:], scb, ALU.mult, [TL(f"zf{n3}"), TL("sc")], [TL(f"zf{n3}")])
                      cp(Zb[n3][:], Zf[n3][:], [TL(f"zf{n3}")], [TL(f"zb{n3}")])
                  if c % 8 == 7:
                      tb = c // 8
                      act(OSQ[ob][:], PSO[ob][:], AF.Square, [TPSO[ob]], [TL("osq0")])
                      mm(PS4[:], [(ONESH[:], OSQ[ob][:])], [TL("onesh"), TL("osq0")], [TPS4])
                      act(RS[ob][:], PS4[:], AF.Ln, [TPS4, TL("epsr")], [TL("rs0")], bias=EPSR[:])
                      act(RS[ob][:], RS[ob][:], AF.Exp, [TL("rs0")], [TL("rs0")], scale=-0.5)
                      tt(OT[ob][:], PSO[ob][:], RS[ob][:], ALU.mult, [TPSO[ob], TL("rs0")], [TL("ot0")])
                      stt(OG.ap(h, tb), OT[ob][:], CO[:, C_GNW:C_GNW + 1], OG.ap(h, tb), ALU.mult, ALU.mult,
                          [TL("ot0"), TCO, OG.T[h][tb]], [OG.T[h][tb]])

              head_begin(0)
              for g in p_qfg_groups(0):
                  g()
              p_v(0)
              gating(0)
              for h in range(8):
                  prelude(h)
                  side = []
                  if h + 1 < 8:
                      head_begin(h + 1)
                      side = p_qfg_groups(h + 1)
                  si = 0
                  u_batch(h, 0)
                  if h == 0:
                      chk(301)
                  for c in range(32):
                      if c % 4 == 0 and 4 * (c // 4 + 1) < 31:
                          u_batch(h, c // 4 + 1)
                          if h == 0 and c == 0:
                              chk(302)
                      r_step(h, c)
                      if h == 0:
                          chk(310 + c)
                      want = (len(side) * (c + 1)) // 32
                      while si < want:
                          side[si]()
                          si += 1
                  while si < len(side):
                      side[si]()
                      si += 1
                  if h + 1 < 8:
                      p_v(h + 1)
                      gating(h + 1)

              S.barrier()
              chk(5)
              MRG = BUFB
              for s_ in range(4):
                  W, TW = wnext()
                  W3 = W[:].rearrange("p (k e) -> p k e", e=512)
                  for jj in range(2):
                      j = 2 * s_ + jj
                      co = jj * 256
                      for tb in range(NTB):
                          psA, TpsA = psnext()
                          mm(psA[:], [(W3[:, k, co:co + 128], OG.ap(k, tb)) for k in range(8)], [TW] + OG.col(tb), [TpsA])
                          psB, TpsB = psnext()
                          mm(psB[:], [(W3[:, k, co + 128:co + 256], XB.ap(k, tb)) for k in range(8)], [TW] + XB.col(tb), [TpsB])
                          b = tb % 2
                          act(TMP[b][:], psB[:], AF.Sigmoid, [TpsB, TCO], [TL(f"tmp{b}")], bias=bcol(6, j))
                          tt(MRG.ap(j, tb), psA[:], TMP[b][:], ALU.mult, [TpsA, TL(f"tmp{b}")], [MRG.T[j][tb]])

              S.barrier()
              chk(6)
              YC = BUFA
              S.op(S.dve, lambda e: e.memset(UP[0][:, 0:CW - 1], 0.0), writes=[TL("up0")])
              for s_ in range(4):
                  W, TW = wnext()
                  W3 = W[:].rearrange("p (k e) -> p k e", e=512)
                  for jj in range(2):
                      j = 2 * s_ + jj
                      co = jj * 256
                      ub = 0
                      for (ta, tb_, nm) in ((0, 16, "dga"), (16, CW, "dgb")):
                          nt = tb_ - ta
                          wcol = CO[:, C_WDW + j * CW + ta:C_WDW + j * CW + tb_]
                          tt(DG[ub][:, ta:tb_, :], IDF.unsqueeze(1).to_broadcast([128, nt, 128]),
                             wcol.unsqueeze(2).to_broadcast([128, nt, 128]), ALU.mult, [TL("consts"), TCO], [TL(nm)])
                      for tb in range(NTB):
                          psA, TpsA = psnext()
                          mm(psA[:], [(W3[:, k, co:co + 128], XB.ap(k, tb)) for k in range(8)], [TW] + XB.col(tb), [TpsA])
                          psB, TpsB = psnext()
                          mm(psB[:], [(W3[:, k, co + 128:co + 256], XB.ap(k, tb)) for k in range(8)], [TW] + XB.col(tb), [TpsB])
                          b = tb % 2
                          act(TMP[b][:], psB[:], AF.Sigmoid, [TpsB, TCO], [TL(f"tmp{b}")], bias=bcol(5, j))
                          stt(UP[ub][:, CW - 1 + tb * TB:CW - 1 + (tb + 1) * TB], psA[:], bcol(4, j), TMP[b][:], ALU.add, ALU.mult,
                              [TpsA, TCO, TL(f"tmp{b}")], [TL(f"up{ub}")])
                      for tb in range(NTB):
                          CB, TCB = ((PS4, TPS4), (PS5, TPS5))[tb % 2]
                          mm(CB[:], [(DG[ub][:, tap, :], UP[ub][:, tb * TB + tap:tb * TB + tap + TB]) for tap in range(CW)],
                             [TL("dga"), TL("dgb"), TL(f"up{ub}")], [TCB])
                          act(YC.ap(j, tb), CB[:], AF.Identity, [TCB, TCO], [YC.T[j][tb]], bias=CO[:, C_BDW + j:C_BDW + j + 1])
              ln_fm(YC, False, colsp, C_CLG, C_CLB, AF.Silu, YC)

              chk(7)
              for s_ in range(4):
                  W, TW = wnext()
                  W3 = W[:].rearrange("p (k e) -> p k e", e=512)
                  for jj in range(2):
                      j = 2 * s_ + jj
                      co = jj * 256
                      for tb in range(NTB):
                          psA, TpsA = psnext()
                          mm(psA[:], [(W3[:, k, co:co + 128], YC.ap(k, tb)) for k in range(8)], [TW] + YC.col(tb), [TpsA])
                          psB, TpsB = psnext()
                          mm(psB[:], [(W3[:, k, co + 128:co + 256], XB.ap(k, tb)) for k in range(8)], [TW] + XB.col(tb), [TpsB])
                          b = tb % 2
                          act(TMP[b][:], psB[:], AF.Sigmoid, [TpsB, TCO], [TL(f"tmp{b}")], bias=bcol(7, j))
                          stt(T1[b][:], psA[:], CO[:, C_BB + j:C_BB + j + 1], TMP[b][:], ALU.add, ALU.mult,
                              [TpsA, TCO, TL(f"tmp{b}")], [TL(f"t1{b}")])
                          tt(MRG.ap(j, tb), MRG.ap(j, tb), T1[b][:], ALU.add, [MRG.T[j][tb], TL(f"t1{b}")], [MRG.T[j][tb]])
              chk(8)
              for s_ in range(2):
                  W, TW = wnext()
                  W3 = W[:].rearrange("p (k e) -> p k e", e=512)
                  for jj in range(4):
                      j = 4 * s_ + jj
                      for tb in range(NTB):
                          ps, Tps = psnext()
                          mm(ps[:], [(W3[:, k, jj * 128:(jj + 1) * 128], MRG.ap(k, tb)) for k in range(8)], [TW] + MRG.col(tb), [Tps])
                          stt(X.ap(j, tb), X.ap(j, tb), ALPHA, ps[:], ALU.mult, ALU.add, [X.T[j][tb], Tps], [X.T[j][tb]])
                          act(XB.ap(j, tb), X.ap(j, tb), AF.Copy, [X.T[j][tb]], [XB.T[j][tb]])
              ln_fm(X, True, colsp, C_L1G, C_L1B, AF.Identity, X, XB)

              S.barrier()
              chk(9)
              THID = [[Tile() for _ in range(2)] for _ in range(NFC)]
              for half in range(2):
                  for s_ in range(11):
                      W, TW = wnext()
                      W3 = W[:].rearrange("p (k e) -> p k e", e=512)
                      for jj in range(2):
                          j = 2 * s_ + jj
                          co = jj * 256
                          for t2 in range(2):
                              tb = half * 2 + t2
                              psA, TpsA = psnext()
                              mm(psA[:], [(W3[:, k, co:co + 128], XB.ap(k, tb)) for k in range(8)], [TW] + XB.col(tb), [TpsA])
                              psB, TpsB = psnext()
                              mm(psB[:], [(W3[:, k, co + 128:co + 256], XB.ap(k, tb)) for k in range(8)], [TW] + XB.col(tb), [TpsB])
                              b = t2
                              act(TMP[b][:], psA[:], AF.Silu, [TpsA], [TL(f"tmp{b}")])
                              tt(HID_t[:, j, t2 * TB:(t2 + 1) * TB], psB[:], TMP[b][:], ALU.mult, [TpsB, TL(f"tmp{b}")], [THID[j][t2]])
                  for s_ in range(8):
                      W, TW = wnext()
                      W3 = W[:, 0:NFC * 128].rearrange("p (k e) -> p k e", e=128)
                      for t2 in range(2):
                          tb = half * 2 + t2
                          ps, Tps = psnext()
                          mm(ps[:], [(W3[:, k, :], HID_t[:, k, t2 * TB:(t2 + 1) * TB]) for k in range(NFC)],
                             [TW] + [THID[k][t2] for k in range(NFC)], [Tps])
                          stt(X.ap(s_, tb), X.ap(s_, tb), ALPHA, ps[:], ALU.mult, ALU.add, [X.T[s_][tb], Tps], [X.T[s_][tb]])
                          act(XB.ap(s_, tb), X.ap(s_, tb), AF.Copy, [X.T[s_][tb]], [XB.T[s_][tb]])
              ln_fm(X, True, colsp, C_L2G, C_L2B, AF.Identity, X, XB)


          except _Stop:
              break
        S.barrier()
        last = None
        for tt_i in range(16):
            b = tt_i % 2
            t0 = tt_i * 128
            tb = tt_i // 4
            for half in range(2):
                ps, Tps = psnext()
                for jj in range(4):
                    j = half * 4 + jj
                    tr(ps[:, jj * 128:(jj + 1) * 128], X_t[:, j, t0:t0 + 128], IDF, [X.T[j][tb], TL("consts")], [Tps])
                act(XN[b][:, half * 512:(half + 1) * 512], ps[:], AF.Copy, [Tps], [TXN[b]])
            last = S.op(S.sp, lambda e, b=b, t0=t0: e.dma_start(out=out_d[t0:t0 + 128, :], in_=XN[b][:]), reads=[TXN[b]], dma=st_o)
        S.emit(final_waits=[last])
    return nc


def _strip(W, colsets):
    K = W.shape[0]
    out = []
    for cols in colsets:
        Ws = W[:, cols]
        n = Ws.shape[1]
        out.append(Ws.reshape(K // 128, 128, n).transpose(1, 0, 2).reshape(128, (K // 128) * n))
    return np.ascontiguousarray(np.stack(out, 0))


def _fm(v):
    return np.ascontiguousarray(v.reshape(-1, 128).T)


def prep(inputs, depth=DEPTH):
    f = lambda a: np.asarray(a, dtype=np.float32)
    w_in, b_in = f(inputs["w_in"]), f(inputs["b_in"])
    w_a, w_b, w_o = f(inputs["w_a"]), f(inputs["w_b"]), f(inputs["w_o"])
    w_up, w_dn, w_dw = f(inputs["w_up"]), f(inputs["w_down"]), f(inputs["w_dw"])
    ar = np.arange(128)
    consts = np.zeros((128, NCONST), np.float32)
    consts[:, K_ID:K_ID + 128] = np.eye(128, dtype=np.float32)
    p = np.arange(128)[:, None] % 64
    t = np.arange(64)[None, :]
    consts[:, K_MASK:K_MASK + 64] = (p <= t).astype(np.float32)
    ln0 = np.concatenate([_fm(f(inputs["ln0_g"])), _fm(f(inputs["ln0_b"]))], axis=1)
    lbl = np.concatenate([_fm(f(inputs["lb_logits"])[l]) for l in range(DEPTH)], axis=1)
    cols = np.zeros((DEPTH, 128, NCOL), np.float32)
    bibc = np.zeros((DEPTH, 128, D), np.float32)
    wih, wic, wma, wmb, wo, wup, wdn = [], [], [], [], [], [], []
    for l in range(DEPTH):
        for sec in range(8):
            cols[l, :, C_BIN + sec * 8:C_BIN + sec * 8 + 8] = _fm(b_in[l, sec * D:(sec + 1) * D])
        cols[l, :, C_GNW] = f(inputs["g_norm_w"])[l]
        cols[l, :, C_WDW:C_WDW + 8 * CW] = w_dw[l].T.reshape(8, 128, CW).transpose(1, 0, 2).reshape(128, 8 * CW)
        for nm, c0 in (("b_dw", C_BDW), ("conv_ln_g", C_CLG), ("conv_ln_b", C_CLB), ("b_b", C_BB),
                       ("ln1_g", C_L1G), ("ln1_b", C_L1B), ("ln2_g", C_L2G), ("ln2_b", C_L2B)):
            cols[l, :, c0:c0 + 8] = _fm(f(inputs[nm])[l])
        bibc[l] = np.broadcast_to(b_in[l, 2 * D:3 * D][None, :], (128, D))
        if l >= depth:
            continue
        wih.append(_strip(w_in[l], [np.concatenate([sec * D + h * 128 + ar for sec in range(4)]) for h in range(8)]))
        wic.append(_strip(w_in[l], [np.concatenate([4 * D + j * 128 + ar, 5 * D + j * 128 + ar,
                                                    4 * D + (j + 1) * 128 + ar, 5 * D + (j + 1) * 128 + ar])
                                    for j in range(0, 8, 2)]))
        cat_a = np.concatenate([w_a[l], w_in[l][:, 6 * D:7 * D]], axis=1)
        cat_b = np.concatenate([w_b[l], w_in[l][:, 7 * D:8 * D]], axis=1)
        sets = [np.concatenate([j * 128 + ar, D + j * 128 + ar, (j + 1) * 128 + ar, D + (j + 1) * 128 + ar])
                for j in range(0, 8, 2)]
        wma.append(_strip(cat_a, sets))
        wmb.append(_strip(cat_b, sets))
        wo.append(_strip(w_o[l], [np.arange(s * 512, (s + 1) * 512) for s in range(2)]))
        wup.append(_strip(w_up[l], [np.concatenate([j * 128 + ar, FH + j * 128 + ar, (j + 1) * 128 + ar, FH + (j + 1) * 128 + ar])
                                    for j in range(0, NFC, 2)]))
        wdn.append(_strip(w_dn[l], [np.arange(s * 128, (s + 1) * 128) for s in range(8)]))

    def stk(lst, shape):
        a = np.zeros((DEPTH,) + shape, np.float32)
        for i, v in enumerate(lst):
            a[i] = v
        return a
    return {
        "consts": consts, "ln0": np.ascontiguousarray(ln0), "lbl": np.ascontiguousarray(lbl), "cols": cols, "bibc": bibc,
        "wih": stk(wih, (8, 128, 4096)), "wic": stk(wic, (4, 128, 4096)), "wma": stk(wma, (4, 128, 4096)),
        "wmb": stk(wmb, (4, 128, 4096)), "wo": stk(wo, (2, 128, 4096)), "wup": stk(wup, (11, 128, 4096)),
        "wdn": stk(wdn, (8, 128, NFC * 128)),
    }


_NC_CACHE = {}


def kernel(**inputs):
    x = np.asarray(inputs["x"], dtype=np.float32)
    shared = prep(inputs)
    if "nc" not in _NC_CACHE:
        _NC_CACHE["nc"] = build(DEPTH)
    nc = _NC_CACHE["nc"]
    in_maps = []
    for b in range(8):
        m = dict(shared)
        m["x"] = np.ascontiguousarray(x[b])
        in_maps.append(m)
    res = run_bass_kernel_spmd(nc, in_maps, core_ids=list(range(8)))
    return np.stack([np.asarray(r["out"], dtype=np.float32) for r in res.results], axis=0)
```
